# Optimizing a Trainium2 kernel written in Bass

```python
import math
import jax, jax.numpy as jnp
from jax import lax
import numpy as np

D_MODEL = 2048
BATCH = 8
SEQ = 2048
DEPTH = 2
DEC_BATCH = 16
DEC_SEQ = 32
PAST_LEN = 4096

CHUNK = 64
PLE_DIM = 256
CONV_W = 4
N_BRANCH = 3
BRANCH_W = D_MODEL // 2
LRU_WIDTH = BRANCH_W
LRU_BLOCKS = 16
LRU_BLOCK = LRU_WIDTH // LRU_BLOCKS
LRU_C = 8.0
DN_HEADS = 8
DN_DK = BRANCH_W // DN_HEADS
DN_DV = BRANCH_W // DN_HEADS
DN_QKV = DN_HEADS * (2 * DN_DK + DN_DV)
DN_CHUNK = 64
AT_HEADS = 8
AT_DH = BRANCH_W // AT_HEADS
AT_LEFT_CHUNKS = 8
AT_WINDOW = AT_LEFT_CHUNKS * CHUNK
AT_BAND = AT_WINDOW + CHUNK
REL_CLIP = 128
D_FF = -(-8 * D_MODEL // (3 * 256)) * 256
SPLITS = [LRU_WIDTH, LRU_WIDTH, DN_HEADS * DN_DK, DN_HEADS * DN_DK, DN_HEADS * DN_DV, DN_HEADS * DN_DV, DN_HEADS, DN_HEADS, AT_HEADS * AT_DH, AT_HEADS * AT_DH, AT_HEADS * AT_DH]
N_IN = sum(SPLITS)
SPLIT_POINTS = [int(v) for v in np.cumsum(SPLITS)[:-1]]

kernel_name = 'hybrid_stream_encoder_step'


def rmsnorm(x, g, eps=1e-6):
    xf = x.astype(jnp.float32)
    y = xf * lax.rsqrt(jnp.mean(xf * xf, axis=-1, keepdims=True) + eps)
    return (y * g.astype(jnp.float32)).astype(x.dtype)


def l2norm(x, eps=1e-6):
    xf = x.astype(jnp.float32)
    return xf * lax.rsqrt(jnp.sum(xf * xf, axis=-1, keepdims=True) + eps)


def causal_dwconv(x, buf, w):
    L = x.shape[1]
    xe = jnp.concatenate([buf.astype(x.dtype), x], axis=1)
    y = xe[:, 0:L] * w[0]
    for j in range(1, CONV_W):
        y = y + xe[:, j:j + L] * w[j]
    return y, xe[:, L:]


def _lin_combine(e1, e2):
    a1, b1 = e1
    a2, b2 = e2
    return a1 * a2, a2 * b1 + b2


def rg_lru(x, h0, w_r, b_r, w_i, b_i, lam):
    bsz, L, W = x.shape
    f32 = jnp.float32
    xf = x.astype(f32)
    xb = xf.reshape(bsz, L, LRU_BLOCKS, LRU_BLOCK)
    r = jax.nn.sigmoid(jnp.einsum('blnc,ncd->blnd', xb, w_r.astype(f32)).reshape(bsz, L, W) + b_r.astype(f32))
    gi = jax.nn.sigmoid(jnp.einsum('blnc,ncd->blnd', xb, w_i.astype(f32)).reshape(bsz, L, W) + b_i.astype(f32))
    log_a = -LRU_C * r * jax.nn.softplus(-lam.astype(f32))
    a = jnp.exp(log_a)
    b = jnp.sqrt(-jnp.expm1(2.0 * log_a)) * (gi * xf)
    a_cum, b_cum = lax.associative_scan(_lin_combine, (a, b), axis=1)
    h = a_cum * h0.astype(f32)[:, None, :] + b_cum
    return h.astype(x.dtype), h[:, -1]


def _to_blocks(t, n, pad):
    t = jnp.pad(t, [(0, 0), (0, pad)] + [(0, 0)] * (t.ndim - 2))
    t = t.reshape((t.shape[0], n, DN_CHUNK) + t.shape[2:])
    return jnp.transpose(t, (1, 0, 3, 2) + tuple(range(4, t.ndim)))


def gated_delta_chunked(q, k, v, g, beta, s0):
    f32 = jnp.float32
    bsz, L, H, Dk = q.shape
    Dv = v.shape[-1]
    n = -(-L // DN_CHUNK)
    pad = n * DN_CHUNK - L
    qc = _to_blocks(q * (Dk ** -0.5), n, pad)
    kc = _to_blocks(k, n, pad)
    vc = _to_blocks(v, n, pad)
    bc = _to_blocks(beta, n, pad)
    gcum = jnp.cumsum(_to_blocks(g, n, pad), axis=-1)
    idx = jnp.arange(DN_CHUNK)
    incl = idx[:, None] >= idx[None, :]
    strict = idx[:, None] > idx[None, :]
    diff = gcum[..., :, None] - gcum[..., None, :]
    decay = jnp.where(incl, jnp.exp(jnp.where(incl, diff, 0.0)), 0.0)
    kb = kc * bc[..., None]
    lmat = jnp.where(strict, jnp.einsum('nbhik,nbhjk->nbhij', kb, kc) * decay, 0.0)
    tmat = lmat + jnp.eye(DN_CHUNK, dtype=f32)
    u = lax.linalg.triangular_solve(tmat, vc * bc[..., None], left_side=True, lower=True, unit_diagonal=True)
    w = lax.linalg.triangular_solve(tmat, kb * jnp.exp(gcum)[..., None], left_side=True, lower=True, unit_diagonal=True)
    qk = jnp.einsum('nbhik,nbhjk->nbhij', qc, kc) * decay

    def step(S, inp):
        qi, ki, ui, wi, qki, gi = inp
        v_new = ui - jnp.einsum('bhck,bhkv->bhcv', wi, S)
        o = jnp.einsum('bhck,bhkv->bhcv', qi * jnp.exp(gi)[..., None], S) + jnp.einsum('bhij,bhjv->bhiv', qki, v_new)
        glast = gi[..., -1]
        k_dec = ki * jnp.exp(glast[..., None] - gi)[..., None]
        S = S * jnp.exp(glast)[..., None, None] + jnp.einsum('bhck,bhcv->bhkv', k_dec, v_new)
        return S, o

    s1, o = lax.scan(step, s0, (qc, kc, u, w, qk, gcum))
    o = jnp.transpose(o, (1, 0, 3, 2, 4)).reshape(bsz, n * DN_CHUNK, H, Dv)[:, :L]
    return o, s1


def band_attention(q, k, v, qpos, kpos, kvalid, rel_bias):
    s = jnp.einsum('bqhd,bkhd->bhqk', q.astype(jnp.float32), k.astype(jnp.float32)) * (AT_DH ** -0.5)
    rel = jnp.clip(qpos[:, None] - kpos[None, :], -REL_CLIP, REL_CLIP) + REL_CLIP
    s = s + rel_bias.astype(jnp.float32)[:, rel]
    if kvalid is not None:
        s = jnp.where(kvalid[None, None, None, :], s, -1e30)
    p = jax.nn.softmax(s, axis=-1)
    return jnp.einsum('bhqk,bkhd->bqhd', p, v.astype(jnp.float32)).astype(q.dtype)


def prompt_band_attention(q, k, v, rel_bias):
    bsz, L, H, Dh = q.shape
    n = L // CHUNK
    kp = jnp.pad(k, ((0, 0), (AT_WINDOW, 0), (0, 0), (0, 0)))
    vp = jnp.pad(v, ((0, 0), (AT_WINDOW, 0), (0, 0), (0, 0)))

    def one_chunk(c):
        start = c * CHUNK
        q_c = lax.dynamic_slice_in_dim(q, start, CHUNK, axis=1)
        k_b = lax.dynamic_slice_in_dim(kp, start, AT_BAND, axis=1)
        v_b = lax.dynamic_slice_in_dim(vp, start, AT_BAND, axis=1)
        qpos = start + jnp.arange(CHUNK)
        kpos = start - AT_WINDOW + jnp.arange(AT_BAND)
        return band_attention(q_c, k_b, v_b, qpos, kpos, kpos >= 0, rel_bias)

    o = lax.map(one_chunk, jnp.arange(n))
    return jnp.transpose(o, (1, 0, 2, 3, 4)).reshape(bsz, L, H, Dh)


def trunk_layer(x, pe, lp, state):
    f32 = jnp.float32
    bsz, seq_len, _ = x.shape
    if state is None:
        conv_a0 = jnp.zeros((bsz, CONV_W - 1, LRU_WIDTH), x.dtype)
        h0 = jnp.zeros((bsz, LRU_WIDTH), f32)
        conv_b0 = jnp.zeros((bsz, CONV_W - 1, DN_QKV), x.dtype)
        s0 = jnp.zeros((bsz, DN_HEADS, DN_DK, DN_DV), f32)
    else:
        conv_a0, h0, conv_b0, s0, k_cache, v_cache = state
    u = rmsnorm(x, lp['norm_mix'])
    proj = jnp.einsum('bld,dn->bln', u, lp['w_in'])
    rx, rgate, dq, dkk, dvv, dz, dalpha, dbeta, cq, ck, cv = jnp.split(proj, SPLIT_POINTS, axis=-1)

    xa, conv_a1 = causal_dwconv(rx, conv_a0, lp['w_conv_a'])
    xa = xa + lp['b_conv_a']
    ha, h1 = rg_lru(xa, h0, lp['w_lru_r'], lp['b_lru_r'], lp['w_lru_i'], lp['b_lru_i'], lp['lru_lambda'])
    out_a = ha * jax.nn.gelu(rgate)

    qkv, conv_b1 = causal_dwconv(jnp.concatenate([dq, dkk, dvv], axis=-1), conv_b0, lp['w_conv_b'])
    qkv = jax.nn.silu(qkv)
    q_b, k_b, v_b = jnp.split(qkv, [DN_HEADS * DN_DK, 2 * DN_HEADS * DN_DK], axis=-1)
    q_b = l2norm(q_b.reshape(bsz, seq_len, DN_HEADS, DN_DK))
    k_b = l2norm(k_b.reshape(bsz, seq_len, DN_HEADS, DN_DK))
    v_b = v_b.reshape(bsz, seq_len, DN_HEADS, DN_DV).astype(f32)
    beta = jax.nn.sigmoid(dbeta.astype(f32))
    log_decay = -jnp.exp(lp['dn_a_log'].astype(f32)) * jax.nn.softplus(dalpha.astype(f32) + lp['dn_dt_bias'].astype(f32))
    o_b, s1 = gated_delta_chunked(q_b, k_b, v_b, log_decay, beta, s0.astype(f32))
    o_b = rmsnorm(o_b, lp['dn_norm']).astype(x.dtype) * jax.nn.silu(dz.reshape(bsz, seq_len, DN_HEADS, DN_DV))
    out_b = o_b.reshape(bsz, seq_len, DN_HEADS * DN_DV)

    q_c = rmsnorm(cq.reshape(bsz, seq_len, AT_HEADS, AT_DH), lp['attn_q_norm'])
    k_c = rmsnorm(ck.reshape(bsz, seq_len, AT_HEADS, AT_DH), lp['attn_k_norm'])
    v_c = cv.reshape(bsz, seq_len, AT_HEADS, AT_DH)
    if state is None:
        out_c = prompt_band_attention(q_c, k_c, v_c, lp['attn_rel_bias'])
        keep = min(AT_WINDOW, seq_len)
        k_new = k_c[:, seq_len - keep:]
        v_new = v_c[:, seq_len - keep:]
    else:
        n_cache = k_cache.shape[1]
        k_all = jnp.concatenate([k_cache.astype(x.dtype), k_c], axis=1)
        v_all = jnp.concatenate([v_cache.astype(x.dtype), v_c], axis=1)
        qpos = PAST_LEN + jnp.arange(seq_len)
        kpos = PAST_LEN - n_cache + jnp.arange(n_cache + seq_len)
        out_c = band_attention(q_c, k_all, v_all, qpos, kpos, None, lp['attn_rel_bias'])
        k_new = k_c
        v_new = v_c
    out_c = out_c.reshape(bsz, seq_len, AT_HEADS * AT_DH)

    gates = jax.nn.sigmoid(jnp.einsum('bld,dn->bln', u, lp['w_gate']) + lp['b_gate']).reshape(bsz, seq_len, N_BRANCH, D_MODEL)
    branches = jnp.stack([out_a, out_b, out_c], axis=2)
    branch_d = jnp.einsum('blnc,ncd->blnd', branches, lp['w_branch_out'])
    merged = jnp.sum(gates * branch_d, axis=2)
    x = x + jnp.einsum('bld,de->ble', merged, lp['w_out'])

    hn = rmsnorm(x, lp['norm_ffn'])
    hid = jax.nn.silu(jnp.einsum('bld,df->blf', hn, lp['w_ffn_gate'])) * jnp.einsum('bld,df->blf', hn, lp['w_ffn_up'])
    x = x + jnp.einsum('blf,fd->bld', hid, lp['w_ffn_down'])

    pg = jax.nn.sigmoid(jnp.einsum('bld,de->ble', rmsnorm(x, lp['norm_ple']), lp['w_ple_gate']))
    x = x + pg * jnp.einsum('blp,pd->bld', pe.astype(x.dtype), lp['w_ple_proj'])
    return x, (conv_a1, h1, conv_b1, s1, k_new, v_new)


def setup_inputs(seed: int = 0) -> dict:
    key = jax.random.key(seed)
    keys = list(jax.random.split(key, 48))
    f32 = jnp.float32

    def nxt():
        return keys.pop()

    def normal(shape, scale):
        return jax.random.normal(nxt(), shape, f32) * scale

    def gain(shape):
        return 1.0 + normal(shape, 0.02)

    cache_len = min(AT_WINDOW, PAST_LEN)
    lam_u = jax.random.uniform(nxt(), (DEPTH, LRU_WIDTH), f32, 0.9, 0.999)
    lam_s = lam_u ** (1.0 / LRU_C)
    lru_lambda = jnp.log(lam_s) - jnp.log1p(-lam_s)
    dn_a_log = jnp.log(jax.random.uniform(nxt(), (DEPTH, DN_HEADS), f32, 1.0, 16.0))
    dt = jnp.exp(jax.random.uniform(nxt(), (DEPTH, DN_HEADS), f32, math.log(1e-3), math.log(1e-1)))
    dn_dt_bias = dt + jnp.log(-jnp.expm1(-dt))
    return {
        'x_prompt': normal((BATCH, SEQ, D_MODEL), 1.0),
        'x_sample': normal((DEC_BATCH, DEC_SEQ, D_MODEL), 1.0),
        'p_prompt': normal((DEPTH, BATCH, SEQ, PLE_DIM), 1.0),
        'p_sample': normal((DEPTH, DEC_BATCH, DEC_SEQ, PLE_DIM), 1.0),
        'cache_attn_k': normal((DEPTH, DEC_BATCH, cache_len, AT_HEADS, AT_DH), 1.0),
        'cache_attn_v': normal((DEPTH, DEC_BATCH, cache_len, AT_HEADS, AT_DH), 1.0),
        'state_conv_a': normal((DEPTH, DEC_BATCH, CONV_W - 1, LRU_WIDTH), 1.0),
        'state_lru_h': normal((DEPTH, DEC_BATCH, LRU_WIDTH), 0.5),
        'state_conv_b': normal((DEPTH, DEC_BATCH, CONV_W - 1, DN_QKV), 1.0),
        'state_delta_S': normal((DEPTH, DEC_BATCH, DN_HEADS, DN_DK, DN_DV), 0.1),
        'norm_mix': gain((DEPTH, D_MODEL)),
        'w_in': normal((DEPTH, D_MODEL, N_IN), D_MODEL ** -0.5),
        'w_gate': normal((DEPTH, D_MODEL, N_BRANCH * D_MODEL), D_MODEL ** -0.5),
        'b_gate': normal((DEPTH, N_BRANCH * D_MODEL), 0.02),
        'w_conv_a': normal((DEPTH, CONV_W, LRU_WIDTH), 0.5),
        'b_conv_a': normal((DEPTH, LRU_WIDTH), 0.02),
        'w_lru_r': normal((DEPTH, LRU_BLOCKS, LRU_BLOCK, LRU_BLOCK), LRU_BLOCK ** -0.5),
        'b_lru_r': normal((DEPTH, LRU_WIDTH), 0.02),
        'w_lru_i': normal((DEPTH, LRU_BLOCKS, LRU_BLOCK, LRU_BLOCK), LRU_BLOCK ** -0.5),
        'b_lru_i': normal((DEPTH, LRU_WIDTH), 0.02),
        'lru_lambda': lru_lambda,
        'w_conv_b': normal((DEPTH, CONV_W, DN_QKV), 0.5),
        'dn_a_log': dn_a_log,
        'dn_dt_bias': dn_dt_bias,
        'dn_norm': gain((DEPTH, DN_DV)),
        'attn_q_norm': gain((DEPTH, AT_DH)),
        'attn_k_norm': gain((DEPTH, AT_DH)),
        'attn_rel_bias': normal((DEPTH, AT_HEADS, 2 * REL_CLIP + 1), 0.1),
        'w_branch_out': normal((DEPTH, N_BRANCH, BRANCH_W, D_MODEL), BRANCH_W ** -0.5),
        'w_out': normal((DEPTH, D_MODEL, D_MODEL), D_MODEL ** -0.5),
        'norm_ffn': gain((DEPTH, D_MODEL)),
        'w_ffn_gate': normal((DEPTH, D_MODEL, D_FF), D_MODEL ** -0.5),
        'w_ffn_up': normal((DEPTH, D_MODEL, D_FF), D_MODEL ** -0.5),
        'w_ffn_down': normal((DEPTH, D_FF, D_MODEL), D_FF ** -0.5),
        'norm_ple': gain((DEPTH, D_MODEL)),
        'w_ple_gate': normal((DEPTH, D_MODEL, D_MODEL), D_MODEL ** -0.5),
        'w_ple_proj': normal((DEPTH, PLE_DIM, D_MODEL), PLE_DIM ** -0.5),
    }


def reference(x_prompt, x_sample, p_prompt, p_sample, cache_attn_k, cache_attn_v, state_conv_a, state_lru_h, state_conv_b, state_delta_S, norm_mix, w_in, w_gate, b_gate, w_conv_a, b_conv_a, w_lru_r, b_lru_r, w_lru_i, b_lru_i, lru_lambda, w_conv_b, dn_a_log, dn_dt_bias, dn_norm, attn_q_norm, attn_k_norm, attn_rel_bias, w_branch_out, w_out, norm_ffn, w_ffn_gate, w_ffn_up, w_ffn_down, norm_ple, w_ple_gate, w_ple_proj):
    xp = x_prompt
    xs = x_sample
    sts_p = []
    sts_s = []
    for i in range(DEPTH):
        lp = {
            'norm_mix': norm_mix[i], 'w_in': w_in[i], 'w_gate': w_gate[i], 'b_gate': b_gate[i],
            'w_conv_a': w_conv_a[i], 'b_conv_a': b_conv_a[i], 'w_lru_r': w_lru_r[i], 'b_lru_r': b_lru_r[i],
            'w_lru_i': w_lru_i[i], 'b_lru_i': b_lru_i[i], 'lru_lambda': lru_lambda[i], 'w_conv_b': w_conv_b[i],
            'dn_a_log': dn_a_log[i], 'dn_dt_bias': dn_dt_bias[i], 'dn_norm': dn_norm[i],
            'attn_q_norm': attn_q_norm[i], 'attn_k_norm': attn_k_norm[i], 'attn_rel_bias': attn_rel_bias[i],
            'w_branch_out': w_branch_out[i], 'w_out': w_out[i], 'norm_ffn': norm_ffn[i],
            'w_ffn_gate': w_ffn_gate[i], 'w_ffn_up': w_ffn_up[i], 'w_ffn_down': w_ffn_down[i],
            'norm_ple': norm_ple[i], 'w_ple_gate': w_ple_gate[i], 'w_ple_proj': w_ple_proj[i],
        }
        xp, st_p = trunk_layer(xp, p_prompt[i], lp, None)
        xs, st_s = trunk_layer(xs, p_sample[i], lp, (state_conv_a[i], state_lru_h[i], state_conv_b[i], state_delta_S[i], cache_attn_k[i], cache_attn_v[i]))
        sts_p.append(st_p)
        sts_s.append(st_s)
    p_conv_a = jnp.stack([s[0] for s in sts_p])
    p_lru_h = jnp.stack([s[1] for s in sts_p])
    p_conv_b = jnp.stack([s[2] for s in sts_p])
    p_delta_S = jnp.stack([s[3] for s in sts_p])
    p_attn_k = jnp.stack([s[4] for s in sts_p])
    p_attn_v = jnp.stack([s[5] for s in sts_p])
    s_conv_a = jnp.stack([s[0] for s in sts_s])
    s_lru_h = jnp.stack([s[1] for s in sts_s])
    s_conv_b = jnp.stack([s[2] for s in sts_s])
    s_delta_S = jnp.stack([s[3] for s in sts_s])
    s_attn_k = jnp.stack([s[4] for s in sts_s])
    s_attn_v = jnp.stack([s[5] for s in sts_s])
    return (xp, xs, p_conv_a, p_lru_h, p_conv_b, p_delta_S, p_attn_k, p_attn_v, s_conv_a, s_lru_h, s_conv_b, s_delta_S, s_attn_k, s_attn_v)
```

```python
import numpy as np
from contextlib import ExitStack
import concourse.bass as bass
import concourse.mybir as mybir
from concourse.bass_utils import run_bass_kernel_spmd

F32 = mybir.dt.float32
BF16 = mybir.dt.bfloat16
AF = mybir.ActivationFunctionType
ALU = mybir.AluOpType
AX = mybir.AxisListType

D = 2048
NCH = 16
TP = 2048
TS = 32
T = TP + 2 * TS
NIN = 9232
DFF = 5632
NFF = 44
PLE = 256
L = 2
TT = [(0, 512), (512, 512), (1024, 512), (1536, 512), (2048, 64)]
GELU_K = 2.0 * 0.7978845608028654
SEQS = [(0, 2048, 0), (2048, 32, 2051), (2080, 32, 2086)]
CB = 2121

V_NMIX, V_NFFN, V_NPLE = 0, 16, 32
V_BGATE = 48
V_WCA = 96
V_BCA = 128
V_BLR = 136
V_BLI = 144
V_LAM = 152
V_WCB = 160
V_QN = 256
V_KN = 257
V_ALOG = 258
V_DTB = 266
V_DNN = 274
NV = 402
C_ID, C_ONE, C_UPI, C_LOS, C_LOI, C_UPS = 0, 128, 256, 320, 384, 448
NCC = 512


ALL_RES = []


class Res:
    __slots__ = ("name", "w", "rd")

    def __init__(self, name):
        self.name = name
        self.w = None
        self.rd = []
        ALL_RES.append(self)


class Op:
    __slots__ = ("eng", "fn", "deps", "dma", "key", "sig", "val", "sem")


class Prog:
    ENGS = ["sync", "act", "pool", "dve", "pe"]

    def __init__(self, nc, es):
        self.nc = nc
        self.es = es
        self.ops = {e: [] for e in self.ENGS}
        self.keycnt = {}
        self.engcnt = {e: 0 for e in self.ENGS}
        self.engsem = {e: es.enter_context(nc.semaphore("s_" + e)) for e in self.ENGS}
        self.keysem = {}
        self.n = 0

    def op(self, eng, fn, r=(), w=(), dma=False, key=None, extra=()):
        o = Op()
        o.eng = eng
        o.fn = fn
        o.dma = dma
        o.key = key
        o.sig = dma
        o.val = 0
        o.sem = None
        deps = {}
        for x in r:
            if x.w is not None:
                deps[id(x.w)] = (x.w, True)
        for x in w:
            if x.w is not None and id(x.w) not in deps:
                deps[id(x.w)] = (x.w, False)
            for q in x.rd:
                if id(q) not in deps:
                    deps[id(q)] = (q, False)
        for q in extra:
            deps[id(q)] = (q, True)
        fin = []
        for d, raw in deps.values():
            if d is o:
                continue
            if d.dma:
                fin.append((d, self.keycnt[d.key] * 16))
                continue
            if (not dma) and d.eng == eng:
                if eng == "pe" or not raw:
                    continue
            d.sig = True
            fin.append((d, None))
        o.deps = fin
        if dma:
            assert key is not None
            self.keycnt[key] = self.keycnt.get(key, 0) + 1
        for x in r:
            x.rd.append(o)
        for x in w:
            x.w = o
            x.rd = []
        self.ops[eng].append(o)
        self.n += 1
        return o

    def flush(self):
        nc = self.nc
        for k in self.keycnt:
            if k not in self.keysem:
                self.keysem[k] = self.es.enter_context(nc.semaphore("k_%d" % len(self.keysem)))
        for e in self.ENGS:
            if self.ops[e]:
                last = [o for o in self.ops[e] if not o.dma]
                if last:
                    last[-1].sig = True
            for o in self.ops[e]:
                if o.dma:
                    o.sem = self.keysem[o.key]
                elif o.sig:
                    self.engcnt[e] += 1
                    o.val = self.engcnt[e]
                    o.sem = self.engsem[e]
        ops = self.ops
        finals = [(self.engsem[e], self.engcnt[e]) for e in self.ENGS if self.engcnt[e] > 0]
        finals += [(self.keysem[k], c * 16) for k, c in self.keycnt.items()]

        def make(e):
            def body(eng):
                waited = {}
                for o in ops[e]:
                    for d, v in o.deps:
                        s = d.sem
                        if v is None:
                            v = d.val
                        if waited.get(id(s), 0) < v:
                            eng.wait_ge(s, v)
                            waited[id(s)] = v
                    ins = o.fn(eng)
                    if o.sig:
                        ins.then_inc(o.sem, 16 if o.dma else 1)
                for s, v in finals:
                    if v > 0:
                        eng.wait_ge(s, v)
            return body

        with nc.Block() as block:
            block.sync(make("sync"))
            block.scalar(make("act"))
            block.gpsimd(make("pool"))
            block.vector(make("dve"))
            block.tensor(make("pe"))
        self.ops = {e: [] for e in self.ENGS}
        for x in ALL_RES:
            x.w = None
            x.rd = []


class Tn:
    __slots__ = ("h", "res", "name")

    def __init__(self, h, name):
        self.h = h
        self.res = Res(name)
        self.name = name

    def __getitem__(self, k):
        return self.h[k]


class Pool:
    def __init__(self, items):
        self.items = items
        self.i = 0

    def next(self):
        t = self.items[self.i % len(self.items)]
        self.i += 1
        return t


def build(dbg=(), nlayers=L, stop=None):
    nc = bass.Bass("TRN2", target_bir_lowering=False)
    es = ExitStack()
    del ALL_RES[:]
    P = Prog(nc, es)
    resmap = {}
    outstores = []

    def R(*key):
        if key not in resmap:
            resmap[key] = Res(str(key))
        return resmap[key]

    def din(name, shape, dt=F32):
        return nc.dram_tensor(name, list(shape), dt, kind="ExternalInput").ap()

    def dout(name, shape, dt=F32):
        return nc.dram_tensor(name, list(shape), dt, kind="ExternalOutput").ap()

    def dscr(name, shape, dt=F32):
        kind = "ExternalOutput" if name in dbg else "Internal"
        return nc.dram_tensor(name, list(shape), dt, kind=kind).ap()

    def sb(name, shape, dt=F32):
        return Tn(es.enter_context(nc.sbuf_tensor(name, list(shape), dt)), name)

    def sbpool(name, shape, dt, n):
        return Pool([sb("%s%d" % (name, i), shape, dt) for i in range(n)])

    def psum(name, shape, dt=F32):
        return Tn(es.enter_context(nc.psum_tensor(name, list(shape), dt)), name)

    def load(dst_ap, src_ap, w, r=(), eng="sync", key=None):
        wr = [x.res if isinstance(x, Tn) else x for x in w]
        rr = [x.res if isinstance(x, Tn) else x for x in r]
        k = key if key is not None else ("ld", wr[0].name)
        return P.op(eng, lambda e: e.dma_start(out=dst_ap, in_=src_ap), r=rr, w=wr, dma=True, key=k)

    def store(dst_ap, src_ap, r, w=(), eng="sync", key=None, final=False):
        wr = [x.res if isinstance(x, Tn) else x for x in w]
        rr = [x.res if isinstance(x, Tn) else x for x in r]
        k = key if key is not None else ("st", rr[0].name)
        o = P.op(eng, lambda e: e.dma_start(out=dst_ap, in_=src_ap), r=rr, w=wr, dma=True, key=k)
        if final:
            outstores.append(o)
        return o

    def rs(xs):
        return [x.res if isinstance(x, Tn) else x for x in xs]

    def act(out, in_, func, r, w, bias=None, scale=None, accum=None):
        kw = {}
        if bias is not None:
            kw["bias"] = bias
        if scale is not None:
            kw["scale"] = scale
        if accum is not None:
            kw["accum_out"] = accum
        return P.op("act", lambda e: e.activation(out=out, in_=in_, func=func, **kw), r=rs(r), w=rs(w))

    def dve(fn, r, w):
        return P.op("dve", fn, r=rs(r), w=rs(w))

    def pe(fn, r, w):
        return P.op("pe", fn, r=rs(r), w=rs(w))

    x_tok = din("x_tok", [T, D])
    pe_tok = din("pe_tok", [L, T, PLE])
    cache_k = din("cache_k", [L, 2, 512, 1024])
    cache_v = din("cache_v", [L, 2, 512, 1024])
    st_conv_a = din("st_conv_a", [L, 2, 3, 1024])
    st_lru_h = din("st_lru_h", [L, 2, 1024])
    st_conv_b = din("st_conv_b", [L, 2, 3, 3072])
    st_S = din("st_S", [L, 2, 8, 128, 128])
    w_in = din("w_in", [L, D, NIN])
    w_gate = din("w_gate", [L, D, 3 * D])
    w_bo = din("w_bo", [L, 3 * 1024, D])
    w_out = din("w_out", [L, D, D])
    w_fg = din("w_fg", [L, D, DFF])
    w_fu = din("w_fu", [L, D, DFF])
    w_fd = din("w_fd", [L, DFF, D])
    w_pg = din("w_pg", [L, D, D])
    w_pp = din("w_pp", [L, PLE, D])
    lru_bd = din("lru_bd", [L, 2, 8, 128, 128])
    vecs_d = din("vecs", [L, 128, NV])
    bias_p_d = din("attn_bias_p", [L, 128, 8, 5, 128])
    bias_s_d = din("attn_bias_s", [L, 128, 8, 5, 32])
    consts_d = din("consts", [128, NCC])

    y_p = dout("y_p", [TP, D])
    y_s = dout("y_s", [2 * TS, D])
    o_conv_a = dout("o_conv_a", [L, 3, 3, 1024])
    o_lru_h = dout("o_lru_h", [L, 3, 1024])
    o_conv_b = dout("o_conv_b", [L, 3, 3, 3072])
    o_S = dout("o_S", [L, 3, 8, 128, 128])
    o_pk = dout("o_pk", [L, 512, 1024])
    o_pv = dout("o_pv", [L, 512, 1024])
    o_sk = dout("o_sk", [L, 2, 32, 1024])
    o_sv = dout("o_sv", [L, 2, 32, 1024])

    XT = [dscr("xt0", [D, T]), dscr("xt1", [D, T])]
    OUTS = dscr("outs_t", [3072, T], BF16)
    HID = dscr("hid_t", [DFF, T], BF16)

    uniq = [0]

    def sbp(st, name, shape, dt=F32):
        uniq[0] += 1
        return Tn(st.enter_context(nc.sbuf_tensor("%s_u%d" % (name, uniq[0]), list(shape), dt)), name)

    ACT_T = sb("act_t", [128, NCH, T], BF16)
    ACTR = [Res("actr%d" % i) for i in range(len(TT))]
    CST = sb("cst", [128, NCC])
    CSTB = sb("cstb", [128, 256], BF16)
    VEC = sb("vec", [128, NV])
    WSH = [None]

    def mkws(st, n, nelem):
        WSH[0] = Pool([sbp(st, "ws%d" % i, [128, nelem], BF16) for i in range(n)])
    PS = Pool([psum("ps%d" % i, [128, 512]) for i in range(6)])
    PSB = Pool([psum("psb%d" % i, [128, 1024], BF16) for i in range(2)])
    W512 = sbpool("w512", [128, 512], F32, 7)
    B512 = sbpool("b512", [128, 512], BF16, 4)

    ident = CST[:, C_ID:C_ID + 128]
    ones_b = CSTB[:, 128:256]
    ident_b = CSTB[:, 0:128]

    load(CST[:, :], consts_d[:, :], [CST])
    dve(lambda e: e.tensor_copy(out=CSTB[:, :], in_=CST[:, 0:256]), [CST], [CSTB])

    def XR(b, j, ti):
        return R("xt", b, j, ti)

    def mm_acc(ps_ap, pairs, r, w):
        def fn(e):
            ins = None
            n = len(pairs)
            for i, (a, b) in enumerate(pairs):
                ins = e.matmul(ps_ap, lhsT=a, rhs=b, start=(i == 0), stop=(i == n - 1))
            return ins
        return pe(fn, r, w)

    def wload(src2d, nk, ncols):
        s = WSH[0].next()
        v = s[:, 0:nk * ncols].rearrange("p (k n) -> p k n", k=nk)
        load(v, src2d.rearrange("(k p) n -> p k n", p=128), [s], eng="pool")
        return s, v

    def slow_load(dst, src, w, r=()):
        wr = rs(w)
        return P.op("sync", lambda e: e.dma_start(out=dst, in_=src, allow_slow_non_contiguous=True),
                    r=rs(r), w=wr, dma=True, key=("ld", wr[0].name))

    def slow_store(dst, src, r, final=True):
        rr = rs(r)
        o = P.op("sync", lambda e: e.dma_start(out=dst, in_=src, allow_slow_non_contiguous=True),
                 r=rr, w=[], dma=True, key=("st", rr[0].name))
        if final:
            outstores.append(o)
        return o

    def phase0():
        st = ExitStack()
        TOKT = Pool([sbp(st, "tokt%d" % i, [128, D]) for i in range(2)])
        FTT = Pool([sbp(st, "ftt%d" % i, [128, NCH, 128]) for i in range(2)])
        for tt in range(17):
            t0 = tt * 128
            n = min(128, T - t0)
            tk = TOKT.next()
            load(tk[0:n, :], x_tok[t0:t0 + n, :], [tk])
            ft = FTT.next()
            for g in range(4):
                pt = PS.next()
                for q in range(4):
                    c = g * 4 + q
                    pe(lambda e, pt=pt, q=q, tk=tk, c=c, n=n: e.transpose(
                        out=pt[:, q * 128:q * 128 + n], in_=tk[0:n, c * 128:(c + 1) * 128], identity=ident[0:n, 0:n]),
                        [tk, CST], [pt])
                dve(lambda e, pt=pt, ft=ft, g=g, n=n: e.tensor_copy(
                    out=ft[:, g * 4:(g + 1) * 4, 0:n], in_=pt[:, :].rearrange("p (q t) -> p q t", q=4)[:, :, 0:n]),
                    [pt], [ft])
            store(XT[0][:, t0:t0 + n].rearrange("(c p) t -> p c t", p=128), ft[:, :, 0:n], [ft],
                  [XR(0, j, min(tt // 4, 4)) for j in range(NCH)])
        P.flush()
        st.close()

    def norm_phase(b, l, vcol):
        st = ExitStack()
        XTL = sbp(st, "xtl", [128, NCH, 512])
        for ti, (t0, n) in enumerate(TT):
            load(XTL[:, :, 0:n], XT[b][:, t0:t0 + n].rearrange("(c p) t -> p c t", p=128), [XTL],
                 r=[XR(b, j, ti) for j in range(NCH)])
            pt = PS.next()
            for c in range(NCH):
                sq = B512.next()
                act(sq[:, 0:n], XTL[:, c, 0:n], AF.Square, [XTL], [sq])
                pe(lambda e, pt=pt, sq=sq, c=c, n=n: e.matmul(pt[:, 0:n], lhsT=ones_b, rhs=sq[:, 0:n],
                                                              start=(c == 0), stop=(c == NCH - 1)), [sq, CSTB], [pt])
            sd = W512.next()
            act(sd[:, 0:n], pt[:, 0:n], AF.Sqrt, [pt], [sd], bias=EPSC[:, 0:1], scale=1.0 / D)
            rstd = W512.next()
            dve(lambda e, rstd=rstd, sd=sd, n=n: e.reciprocal(out=rstd[:, 0:n], in_=sd[:, 0:n]), [sd], [rstd])
            for c in range(NCH):
                dve(lambda e, c=c, t0=t0, n=n, rstd=rstd: e.scalar_tensor_tensor(
                    out=ACT_T[:, c, t0:t0 + n], in0=XTL[:, c, 0:n], scalar=VEC[:, vcol + c:vcol + c + 1],
                    in1=rstd[:, 0:n], op0=ALU.mult, op1=ALU.mult), [XTL, rstd, VEC], [ACTR[ti]])
        P.flush()
        st.close()

    EPSC = sb("epsc", [128, 2])
    dve(lambda e: e.memset(EPSC[:, 0:1], 1e-6), [], [EPSC])
    dve(lambda e: e.memset(EPSC[:, 1:2], 1.0), [EPSC], [EPSC])

    def proj_chunk(wsrc2d, consume, nk=NCH):
        s, v = wload(wsrc2d, nk, 128)
        for ti, (t0, n) in enumerate(TT):
            pt = PS.next()
            mm_acc(pt[:, 0:n], [(v[:, k, :], ACT_T[:, k, t0:t0 + n]) for k in range(nk)], [s, ACTR[ti]], [pt])
            consume(ti, t0, n, pt)

    def evac_to_convbuf(RXB):
        def consume(ti, t0, n, pt):
            if ti < 4:
                act(RXB[:, 3 + t0:3 + t0 + n], pt[:, 0:n], AF.Copy, [pt], [RXB])
            else:
                act(RXB[:, 2051:2121].rearrange("p (s c) -> p s c", c=35)[:, :, 3:35],
                    pt[:, 0:64].rearrange("p (s c) -> p s c", c=32), AF.Copy, [pt], [RXB])
        return consume

    def conv_tile(RXB, wcol0, wstride, ti, t0, n, bias_ap=None, pool=None):
        o = (pool or W512).next()
        if ti < 4:
            src = lambda j: RXB[:, t0 + j:t0 + j + n]
            dst = o[:, 0:n]
        else:
            src = lambda j: RXB[:, 2051:2121].rearrange("p (s c) -> p s c", c=35)[:, :, j:j + 32]
            dst = o[:, 0:64].rearrange("p (s c) -> p s c", c=32)
        w = lambda j: VEC[:, wcol0 + j * wstride:wcol0 + j * wstride + 1]
        if bias_ap is not None:
            dve(lambda e: e.tensor_scalar(out=dst, in0=src(0), scalar1=w(0), scalar2=bias_ap, op0=ALU.mult, op1=ALU.add),
                [RXB, VEC], [o])
        else:
            dve(lambda e: e.tensor_scalar(out=dst, in0=src(0), scalar1=w(0), scalar2=None, op0=ALU.mult), [RXB, VEC], [o])
        for j in (1, 2, 3):
            dve(lambda e, j=j: e.scalar_tensor_tensor(out=dst, in0=src(j), scalar=w(j), in1=dst, op0=ALU.mult, op1=ALU.add),
                [RXB, VEC, o], [o])
        return o

    def conv_state_io(RXB, l, st_in, o_out, c0):
        dve(lambda e: e.memset(RXB[:, 0:3], 0.0), [], [RXB])
        for s in range(2):
            cb = SEQS[s + 1][2]
            slow_load(RXB[:, cb:cb + 3], st_in[l, s, :, c0:c0 + 128].rearrange("k p -> p k"), [RXB])

    def conv_state_out(RXB, l, o_out, c0):
        for s, (ts, ln, cb) in enumerate(SEQS):
            slow_store(o_out[l, s, :, c0:c0 + 128].rearrange("k p -> p k"), RXB[:, cb + ln:cb + ln + 3], [RXB])

    def mixer_a(l):
        st = ExitStack()
        mkws(st, 3, 2048)
        RXB2 = [sbp(st, "rxb%d" % i_, [128, CB]) for i_ in range(2)]
        HST = sbp(st, "hst", [128, 2, 8])
        CC = sbp(st, "cc", [128, 8])
        BDP = Pool([sbp(st, "bd%d" % i, [128, 2, 128], BF16) for i in range(3)])
        OAP = Pool([sbp(st, "oa%d" % i, [128, T], BF16) for i in range(2)])
        GLB2 = [sbp(st, "glb%d" % i_, [128, T]) for i_ in range(2)]
        HL = sbp(st, "hl", [128, 8, 3])
        AAP = Pool([sbp(st, "aap%d" % i_, [128, 512]) for i_ in range(3)])
        BBP = Pool([sbp(st, "bbp%d" % i_, [128, 512]) for i_ in range(3)])
        HHP = Pool([sbp(st, "hhp%d" % i_, [128, 512]) for i_ in range(3)])
        for s in range(2):
            slow_load(HST[:, s, :], st_lru_h[l, s, :].rearrange("(c p) -> p c", p=128), [HST])
        act(CC[:, :], VEC[:, V_LAM:V_LAM + 8], AF.Exp, [VEC], [CC], scale=-1.0)
        act(CC[:, :], CC[:, :], AF.Ln, [CC], [CC], bias=EPSC[:, 1:2], scale=1.0)
        dve(lambda e: e.tensor_scalar(out=CC[:, :], in0=CC[:, :], scalar1=-8.0, scalar2=None, op0=ALU.mult), [CC], [CC])
        bds = {}

        def a_proj(j):
            RXB = RXB2[j % 2]
            GLB = GLB2[j % 2]
            bd = BDP.next()
            bds[j] = bd
            load(bd[:, :, :], lru_bd[l, :, j, :, :].rearrange("g p n -> p g n"), [bd], eng="pool")
            conv_state_io(RXB, l, st_conv_a, o_conv_a, j * 128)
            proj_chunk(w_in[l, :, j * 128:(j + 1) * 128], evac_to_convbuf(RXB))
            conv_state_out(RXB, l, o_conv_a, j * 128)

            def gelu_consume(ti, t0, n, pt):
                rg = W512.next()
                act(rg[:, 0:n], pt[:, 0:n], AF.Copy, [pt], [rg])
                t1 = W512.next()
                dve(lambda e: e.tensor_tensor(out=t1[:, 0:n], in0=rg[:, 0:n], in1=rg[:, 0:n], op=ALU.mult), [rg], [t1])
                dve(lambda e: e.tensor_scalar(out=t1[:, 0:n], in0=t1[:, 0:n], scalar1=0.044715, scalar2=1.0,
                                              op0=ALU.mult, op1=ALU.add), [t1], [t1])
                dve(lambda e: e.tensor_tensor(out=t1[:, 0:n], in0=t1[:, 0:n], in1=rg[:, 0:n], op=ALU.mult), [t1, rg], [t1])
                act(t1[:, 0:n], t1[:, 0:n], AF.Sigmoid, [t1], [t1], scale=GELU_K)
                dve(lambda e: e.tensor_tensor(out=GLB[:, t0:t0 + n], in0=t1[:, 0:n], in1=rg[:, 0:n], op=ALU.mult),
                    [t1, rg], [GLB])
            proj_chunk(w_in[l, :, 1024 + j * 128:1024 + (j + 1) * 128], gelu_consume)

        def a_tiles(j):
            RXB = RXB2[j % 2]
            GLB = GLB2[j % 2]
            bd = bds[j]
            oa = OAP.next()
            hprev = None
            pend = {}

            def a_pre(ti, t0, n, j=j, bd=bd):
                xa = conv_tile(RXB, V_WCA + j, 8, ti, t0, n, bias_ap=VEC[:, V_BCA + j:V_BCA + j + 1])
                xab = B512.next()
                act(xab[:, 0:n], xa[:, 0:n], AF.Copy, [xa], [xab])
                pr = PS.next()
                pe(lambda e: e.matmul(pr[:, 0:n], lhsT=bd[:, 0, :], rhs=xab[:, 0:n], start=True, stop=True), [bd, xab], [pr])
                pi = PS.next()
                pe(lambda e: e.matmul(pi[:, 0:n], lhsT=bd[:, 1, :], rhs=xab[:, 0:n], start=True, stop=True), [bd, xab], [pi])
                rr = W512.next()
                act(rr[:, 0:n], pr[:, 0:n], AF.Sigmoid, [pr, VEC], [rr], bias=VEC[:, V_BLR + j:V_BLR + j + 1])
                gi = W512.next()
                act(gi[:, 0:n], pi[:, 0:n], AF.Sigmoid, [pi, VEC], [gi], bias=VEC[:, V_BLI + j:V_BLI + j + 1])
                aa = AAP.next()
                act(aa[:, 0:n], rr[:, 0:n], AF.Exp, [rr, CC], [aa], scale=CC[:, j:j + 1])
                bb = BBP.next()
                dve(lambda e: e.tensor_tensor(out=bb[:, 0:n], in0=aa[:, 0:n], in1=aa[:, 0:n], op=ALU.mult), [aa], [bb])
                dve(lambda e: e.tensor_scalar(out=bb[:, 0:n], in0=bb[:, 0:n], scalar1=-1.0, scalar2=1.0, op0=ALU.mult, op1=ALU.add), [bb], [bb])
                act(bb[:, 0:n], bb[:, 0:n], AF.Sqrt, [bb], [bb])
                dve(lambda e: e.tensor_tensor(out=bb[:, 0:n], in0=bb[:, 0:n], in1=gi[:, 0:n], op=ALU.mult), [bb, gi], [bb])
                dve(lambda e: e.tensor_tensor(out=bb[:, 0:n], in0=bb[:, 0:n], in1=xa[:, 0:n], op=ALU.mult), [bb, xa], [bb])
                pend[ti] = (aa, bb)

            def a_scan(ti, t0, n, hprev, j=j, oa=oa):
                aa, bb = pend.pop(ti)
                hh = HHP.next()
                if ti < 4:
                    init = 0.0 if ti == 0 else hprev[:, 511:512]
                    rd = [aa, bb] + ([hprev] if ti > 0 else [])
                    dve(lambda e: e.tensor_tensor_scan(out=hh[:, 0:n], data0=aa[:, 0:n], data1=bb[:, 0:n], initial=init,
                                                       op0=ALU.mult, op1=ALU.add), rd, [hh])
                    if ti == 3:
                        dve(lambda e: e.tensor_copy(out=HL[:, j, 0:1], in_=hh[:, 511:512]), [hh], [HL])
                else:
                    for s in range(2):
                        dve(lambda e, s=s: e.tensor_tensor_scan(
                            out=hh[:, s * 32:(s + 1) * 32], data0=aa[:, s * 32:(s + 1) * 32], data1=bb[:, s * 32:(s + 1) * 32],
                            initial=HST[:, s, j:j + 1], op0=ALU.mult, op1=ALU.add), [aa, bb, HST], [hh])
                        dve(lambda e, s=s: e.tensor_copy(out=HL[:, j, 1 + s:2 + s], in_=hh[:, s * 32 + 31:s * 32 + 32]), [hh], [HL])
                dve(lambda e: e.tensor_tensor(out=oa[:, t0:t0 + n], in0=hh[:, 0:n], in1=GLB[:, t0:t0 + n], op=ALU.mult), [hh, GLB], [oa])
                return hh
            a_pre(0, TT[0][0], TT[0][1])
            for ti, (t0, n) in enumerate(TT):
                if ti + 1 < len(TT):
                    a_pre(ti + 1, TT[ti + 1][0], TT[ti + 1][1])
                hprev = a_scan(ti, t0, n, hprev)
            store(OUTS[j * 128:(j + 1) * 128, :], oa[:, :], [oa], [R("outs", j)])
        a_proj(0)
        for j in range(8):
            if j + 1 < 8:
                a_proj(j + 1)
            a_tiles(j)
        for s in range(3):
            slow_store(o_lru_h[l, s, :].rearrange("(c p) -> p c", p=128), HL[:, :, s], [HL])
        P.flush()
        st.close()

    MRGD = dscr("mrg_t", [D, T], BF16)

    def load_act(src):
        for ti, (t0, n) in enumerate(TT):
            load(ACT_T[:, :, t0:t0 + n], src[:, t0:t0 + n].rearrange("(c p) t -> p c t", p=128), [ACTR[ti]],
                 r=[R("mrg", j, ti) for j in range(NCH)], key=("ld", "act_t"))

    def resid_consume(src_b, dst_b, j, extra_fn=None):
        def consume(ti, t0, n, pt):
            xt = W512.next()
            load(xt[:, 0:n], XT[src_b][j * 128:(j + 1) * 128, t0:t0 + n], [xt], r=[XR(src_b, j, ti)])
            o = W512.next()
            dve(lambda e: e.tensor_tensor(out=o[:, 0:n], in0=pt[:, 0:n], in1=xt[:, 0:n], op=ALU.add), [pt, xt], [o])
            store(XT[dst_b][j * 128:(j + 1) * 128, t0:t0 + n], o[:, 0:n], [o], [XR(dst_b, j, ti)])
        return consume

    def merge_phase(l, cur, nxt):
        st = ExitStack()
        mkws(st, 4, 6144)
        OTP = Pool([sbp(st, "ot%d" % i, [128, 24, 512], BF16) for i in range(2)])
        MJP = Pool([sbp(st, "mj%d" % i, [128, T], BF16) for i in range(2)])
        wg3 = w_gate[l].rearrange("(k p) (n c) -> p k n c", p=128, n=3)
        wb3 = w_bo[l].rearrange("(n k p) c -> p n k c", n=3, p=128)
        for j in range(NCH):
            sg = WSH[0].next()
            vg = sg[:, 0:6144].rearrange("p (k n c) -> p k n c", k=16, n=3)
            for nb in range(3):
                load(vg[:, :, nb, :], wg3[:, :, nb, j * 128:(j + 1) * 128], [sg], eng="pool")
            sbo = WSH[0].next()
            vb = sbo[:, 0:3072].rearrange("p (n k c) -> p n k c", n=3, k=8)
            load(vb, wb3[:, :, :, j * 128:(j + 1) * 128], [sbo], eng="pool")
            mj = MJP.next()
            for ti, (t0, n) in enumerate(TT):
                ot = OTP.next()
                load(ot[:, :, 0:n], OUTS[:, t0:t0 + n].rearrange("(c p) t -> p c t", p=128), [ot],
                     r=[R("outs", c) for c in range(24)])
                mrg = W512.next()
                for nb in range(3):
                    pg = PS.next()
                    mm_acc(pg[:, 0:n], [(vg[:, k, nb, :], ACT_T[:, k, t0:t0 + n]) for k in range(16)], [sg, ACTR[ti]], [pg])
                    gs = W512.next()
                    act(gs[:, 0:n], pg[:, 0:n], AF.Sigmoid, [pg, VEC], [gs],
                        bias=VEC[:, V_BGATE + nb * 16 + j:V_BGATE + nb * 16 + j + 1])
                    pb = PS.next()
                    mm_acc(pb[:, 0:n], [(vb[:, nb, k, :], ot[:, nb * 8 + k, 0:n]) for k in range(8)], [sbo, ot], [pb])
                    if nb == 0:
                        dve(lambda e, mrg=mrg, gs=gs, pb=pb, n=n: e.tensor_tensor(out=mrg[:, 0:n], in0=pb[:, 0:n], in1=gs[:, 0:n], op=ALU.mult),
                            [pb, gs], [mrg])
                    else:
                        dve(lambda e, gs=gs, pb=pb, n=n: e.tensor_tensor(out=gs[:, 0:n], in0=pb[:, 0:n], in1=gs[:, 0:n], op=ALU.mult),
                            [pb, gs], [gs])
                        dve(lambda e, mrg=mrg, gs=gs, n=n: e.tensor_tensor(out=mrg[:, 0:n], in0=mrg[:, 0:n], in1=gs[:, 0:n], op=ALU.add),
                            [mrg, gs], [mrg])
                act(mj[:, t0:t0 + n], mrg[:, 0:n], AF.Copy, [mrg], [mj])
            store(MRGD[j * 128:(j + 1) * 128, :], mj[:, :], [mj], [R("mrg", j, ti) for ti in range(5)])
        P.flush()
        st.close()

    def ffn_phase(l, cur, nxt):
        st = ExitStack()
        mkws(st, 6, 2048)
        HJP = Pool([sbp(st, "hj%d" % i, [128, T], BF16) for i in range(2)])
        for j in range(NFF):
            s1, v1 = wload(w_fg[l, :, j * 128:(j + 1) * 128], 16, 128)
            s2, v2 = wload(w_fu[l, :, j * 128:(j + 1) * 128], 16, 128)
            hj = HJP.next()
            for ti, (t0, n) in enumerate(TT):
                pg = PS.next()
                mm_acc(pg[:, 0:n], [(v1[:, k, :], ACT_T[:, k, t0:t0 + n]) for k in range(16)], [s1, ACTR[ti]], [pg])
                pu = PS.next()
                mm_acc(pu[:, 0:n], [(v2[:, k, :], ACT_T[:, k, t0:t0 + n]) for k in range(16)], [s2, ACTR[ti]], [pu])
                sgl = W512.next()
                act(sgl[:, 0:n], pg[:, 0:n], AF.Silu, [pg], [sgl])
                dve(lambda e, hj=hj, sgl=sgl, pu=pu, t0=t0, n=n: e.tensor_tensor(out=hj[:, t0:t0 + n], in0=pu[:, 0:n], in1=sgl[:, 0:n], op=ALU.mult),
                    [pu, sgl], [hj])
            store(HID[j * 128:(j + 1) * 128, :], hj[:, :], [hj], [R("hid", j)])
        P.flush()
        st.close()

    def ple_phase(l, cur, nxt):
        st = ExitStack()
        mkws(st, 4, 2048)
        PET = sbp(st, "pet", [128, 2, T], BF16)
        PTK = Pool([sbp(st, "ptk%d" % i, [128, PLE]) for i in range(2)])
        for tt in range(17):
            t0 = tt * 128
            n = min(128, T - t0)
            tk = PTK.next()
            load(tk[0:n, :], pe_tok[l, t0:t0 + n, :], [tk])
            pt = PS.next()
            for q in range(2):
                pe(lambda e, pt=pt, q=q, tk=tk, n=n: e.transpose(out=pt[:, q * 128:q * 128 + n], in_=tk[0:n, q * 128:(q + 1) * 128],
                                                                 identity=ident[0:n, 0:n]), [tk, CST], [pt])
            dve(lambda e, pt=pt, t0=t0, n=n: e.tensor_copy(out=PET[:, :, t0:t0 + n],
                                                           in_=pt[:, 0:256].rearrange("p (q t) -> p q t", q=2)[:, :, 0:n]), [pt], [PET])
        for j in range(NCH):
            s1, v1 = wload(w_pg[l, :, j * 128:(j + 1) * 128], 16, 128)
            s2, v2 = wload(w_pp[l, :, j * 128:(j + 1) * 128], 2, 128)
            for ti, (t0, n) in enumerate(TT):
                pg = PS.next()
                mm_acc(pg[:, 0:n], [(v1[:, k, :], ACT_T[:, k, t0:t0 + n]) for k in range(16)], [s1, ACTR[ti]], [pg])
                pp = PS.next()
                mm_acc(pp[:, 0:n], [(v2[:, k, :], PET[:, k, t0:t0 + n]) for k in range(2)], [s2, PET], [pp])
                sg = W512.next()
                act(sg[:, 0:n], pg[:, 0:n], AF.Sigmoid, [pg], [sg])
                xt = W512.next()
                load(xt[:, 0:n], XT[cur][j * 128:(j + 1) * 128, t0:t0 + n], [xt], r=[XR(cur, j, ti)])
                dve(lambda e, sg=sg, pp=pp, n=n: e.tensor_tensor(out=sg[:, 0:n], in0=pp[:, 0:n], in1=sg[:, 0:n], op=ALU.mult), [pp, sg], [sg])
                o = W512.next()
                dve(lambda e, o=o, sg=sg, xt=xt, n=n: e.tensor_tensor(out=o[:, 0:n], in0=sg[:, 0:n], in1=xt[:, 0:n], op=ALU.add), [sg, xt], [o])
                store(XT[nxt][j * 128:(j + 1) * 128, t0:t0 + n], o[:, 0:n], [o], [XR(nxt, j, ti)])
        P.flush()
        st.close()

    VECN = sb("vecn", [128, 16])

    def norm_tile(xt, xtr, ti, t0, n, gain):
        pt = PS.next()
        for c in range(NCH):
            sq = B512.next()
            act(sq[:, 0:n], xt[:, c, 0:n], AF.Square, [xtr[c]], [sq])
            pe(lambda e, pt=pt, sq=sq, c=c: e.matmul(pt[:, 0:n], lhsT=ones_b, rhs=sq[:, 0:n], start=(c == 0), stop=(c == NCH - 1)),
               [sq, CSTB], [pt])
        sd = W512.next()
        act(sd[:, 0:n], pt[:, 0:n], AF.Sqrt, [pt], [sd], bias=EPSC[:, 0:1], scale=1.0 / D)
        dve(lambda e: e.reciprocal(out=sd[:, 0:n], in_=sd[:, 0:n]), [sd], [sd])
        for c in range(NCH):
            dve(lambda e, c=c: e.scalar_tensor_tensor(out=ACT_T[:, c, t0:t0 + n], in0=xt[:, c, 0:n], scalar=gain(c), in1=sd[:, 0:n],
                                                      op0=ALU.mult, op1=ALU.mult), [xtr[c], sd, VEC, VECN], [ACTR[ti]])

    def fused_tiles(st, cur, nxt, gain, tile_fn, nb=2):
        XTP = [sbp(st, "xtp%d" % i_, [128, NCH, 512]) for i_ in range(nb)]
        XTr = [[Res("xtp%d_%d" % (i_, c)) for c in range(NCH)] for i_ in range(nb)]
        for ti, (t0, n) in enumerate(TT):
            xt = XTP[ti % nb]
            xtr = XTr[ti % nb]
            for j in range(NCH):
                src = tile_fn(ti, t0, n, j)
                xin = W512.next()
                load(xin[:, 0:n], XT[cur][j * 128:(j + 1) * 128, t0:t0 + n], [xin], r=[XR(cur, j, ti)])
                dve(lambda e, src=src, xin=xin, xt=xt, j=j, n=n: e.tensor_tensor(out=xt[:, j, 0:n], in0=src[0], in1=xin[:, 0:n], op=ALU.add),
                    list(src[1]) + [xin], [xtr[j]])
                store(XT[nxt][j * 128:(j + 1) * 128, t0:t0 + n], xt[:, j, 0:n], [xtr[j]], [XR(nxt, j, ti)], key=("st", "xtp%d" % (ti % nb)))
            if gain is not None:
                norm_tile(xt, xtr, ti, t0, n, gain)

    def wout_norm_phase(l, cur, nxt):
        st = ExitStack()
        mkws(st, 4, 2048)
        load_act(MRGD)

        def tile_fn(ti, t0, n, j):
            s, v = wload(w_out[l, :, j * 128:(j + 1) * 128], NCH, 128)
            pt = PS.next()
            mm_acc(pt[:, 0:n], [(v[:, k, :], ACT_T[:, k, t0:t0 + n]) for k in range(NCH)], [s, ACTR[ti]], [pt])
            return (pt[:, 0:n], [pt])
        fused_tiles(st, cur, nxt, lambda c: VEC[:, V_NFFN + c:V_NFFN + c + 1], tile_fn)
        P.flush()
        st.close()

    def ffn_down_norm_phase(l, cur, nxt):
        st = ExitStack()
        mkws(st, 3, 5632)
        HT = sbp(st, "ht", [128, NFF, 512], BF16)

        def tile_fn(ti, t0, n, j):
            if j == 0:
                load(HT[:, :, 0:n], HID[:, t0:t0 + n].rearrange("(c p) t -> p c t", p=128), [HT], r=[R("hid", jj) for jj in range(NFF)])
            s, v = wload(w_fd[l, :, j * 128:(j + 1) * 128], NFF, 128)
            pt = PS.next()
            mm_acc(pt[:, 0:n], [(v[:, k, :], HT[:, k, 0:n]) for k in range(NFF)], [s, HT], [pt])
            return (pt[:, 0:n], [pt])
        fused_tiles(st, cur, nxt, lambda c: VEC[:, V_NPLE + c:V_NPLE + c + 1], tile_fn, nb=1)
        P.flush()
        st.close()

    def ple_norm_phase(l, cur, nxt, next_norm):
        st = ExitStack()
        mkws(st, 4, 2048)
        WPP = sbp(st, "wpp", [128, 2, D], BF16)
        load(WPP[:, :, :], w_pp[l].rearrange("(k p) n -> p k n", p=128), [WPP], eng="pool")
        PET = sbp(st, "pet", [128, 2, T], BF16)
        PTK = Pool([sbp(st, "ptk%d" % i, [128, PLE]) for i in range(2)])
        if next_norm:
            load(VECN[:, :], vecs_d[l + 1, :, V_NMIX:V_NMIX + 16], [VECN])
        for tt in range(17):
            t0 = tt * 128
            n = min(128, T - t0)
            tk = PTK.next()
            load(tk[0:n, :], pe_tok[l, t0:t0 + n, :], [tk])
            pt = PS.next()
            for q in range(2):
                pe(lambda e, pt=pt, q=q, tk=tk, n=n: e.transpose(out=pt[:, q * 128:q * 128 + n], in_=tk[0:n, q * 128:(q + 1) * 128],
                                                                 identity=ident[0:n, 0:n]), [tk, CST], [pt])
            dve(lambda e, pt=pt, t0=t0, n=n: e.tensor_copy(out=PET[:, :, t0:t0 + n],
                                                           in_=pt[:, 0:256].rearrange("p (q t) -> p q t", q=2)[:, :, 0:n]), [pt], [PET])

        def tile_fn(ti, t0, n, j):
            s1, v1 = wload(w_pg[l, :, j * 128:(j + 1) * 128], 16, 128)
            pg = PS.next()
            mm_acc(pg[:, 0:n], [(v1[:, k, :], ACT_T[:, k, t0:t0 + n]) for k in range(16)], [s1, ACTR[ti]], [pg])
            pp = PS.next()
            mm_acc(pp[:, 0:n], [(WPP[:, k, j * 128:(j + 1) * 128], PET[:, k, t0:t0 + n]) for k in range(2)], [WPP, PET], [pp])
            sg = W512.next()
            act(sg[:, 0:n], pg[:, 0:n], AF.Sigmoid, [pg], [sg])
            dve(lambda e: e.tensor_tensor(out=sg[:, 0:n], in0=pp[:, 0:n], in1=sg[:, 0:n], op=ALU.mult), [pp, sg], [sg])
            return (sg[:, 0:n], [sg])
        fused_tiles(st, cur, nxt, (lambda c: VECN[:, c:c + 1]) if next_norm else None, tile_fn)
        P.flush()
        st.close()

    def final_phase(b):
        st = ExitStack()
        FIN = Pool([sbp(st, "fin%d" % i, [128, NCH, 128]) for i in range(2)])
        TOUT = Pool([sbp(st, "tout%d" % i, [128, D]) for i in range(2)])
        for tt in range(17):
            t0 = tt * 128
            n = min(128, T - t0)
            fi = FIN.next()
            load(fi[:, :, 0:n], XT[b][:, t0:t0 + n].rearrange("(c p) t -> p c t", p=128), [fi],
                 r=[XR(b, j, min(tt // 4, 4)) for j in range(NCH)])
            to = TOUT.next()
            for g in range(4):
                pt = PS.next()
                for q in range(4):
                    c = g * 4 + q
                    pe(lambda e, pt=pt, q=q, fi=fi, c=c, n=n: e.transpose(out=pt[0:n, q * 128:(q + 1) * 128], in_=fi[:, c, 0:n],
                                                                          identity=ident), [fi, CST], [pt])
                dve(lambda e, pt=pt, to=to, g=g, n=n: e.tensor_copy(out=to[0:n, g * 512:(g + 1) * 512], in_=pt[0:n, :]), [pt], [to])
            if tt < 16:
                store(y_p[t0:t0 + n, :], to[0:n, :], [to], final=True)
            else:
                store(y_s[:, :], to[0:n, :], [to], final=True)
        P.flush()
        st.close()

    SCL = 128.0 ** -0.5

    def headnorm_consume(dst_bf, gcol, raw_keep=None):
        def consume(ti, t0, n, pt):
            raw = W512.next()
            act(raw[:, 0:n], pt[:, 0:n], AF.Copy, [pt], [raw])
            sq = B512.next()
            act(sq[:, 0:n], pt[:, 0:n], AF.Square, [pt], [sq])
            p2 = PS.next()
            pe(lambda e: e.matmul(p2[:, 0:n], lhsT=ones_b, rhs=sq[:, 0:n], start=True, stop=True), [sq, CSTB], [p2])
            sd = W512.next()
            act(sd[:, 0:n], p2[:, 0:n], AF.Sqrt, [p2], [sd], bias=EPSC[:, 0:1], scale=1.0 / 128)
            dve(lambda e: e.reciprocal(out=sd[:, 0:n], in_=sd[:, 0:n]), [sd], [sd])
            if raw_keep is not None:
                dve(lambda e: e.scalar_tensor_tensor(out=raw[:, 0:n], in0=raw[:, 0:n], scalar=VEC[:, gcol:gcol + 1], in1=sd[:, 0:n],
                                                     op0=ALU.mult, op1=ALU.mult), [raw, sd, VEC], [raw])
                dve(lambda e: e.tensor_copy(out=dst_bf[:, t0:t0 + n], in_=raw[:, 0:n]), [raw], [dst_bf])
                raw_keep(ti, t0, n, raw)
            else:
                dve(lambda e: e.scalar_tensor_tensor(out=dst_bf[:, t0:t0 + n], in0=raw[:, 0:n], scalar=VEC[:, gcol:gcol + 1], in1=sd[:, 0:n],
                                                     op0=ALU.mult, op1=ALU.mult), [raw, sd, VEC], [dst_bf])
        return consume

    def kv_out(l, h, o_p, o_s, KO):
        def keep(ti, t0, n, raw):
            if ti < 3:
                return
            pt = PS.next()
            nq = 4 if ti == 3 else 1
            w = 128 if ti == 3 else 64
            for q in range(nq):
                pe(lambda e, q=q: e.transpose(out=pt[0:w, q * 128:(q + 1) * 128], in_=raw[:, q * 128:q * 128 + w], identity=ident),
                   [raw, CST], [pt])
            ko = KO.next()
            dve(lambda e: e.tensor_copy(out=ko[0:w, 0:nq * 128], in_=pt[0:w, 0:nq * 128]), [pt], [ko])
            if ti == 3:
                store(o_p[l].rearrange("(a p) c -> p a c", p=128)[:, :, h * 128:(h + 1) * 128],
                      ko[:, :].rearrange("p (a c) -> p a c", a=4), [ko], final=True)
            else:
                store(o_s[l].rearrange("s t c -> (s t) c")[:, h * 128:(h + 1) * 128], ko[0:64, 0:128], [ko], final=True)
        return keep

    def mixer_c(l):
        st = ExitStack()
        mkws(st, 4, 2048)
        QNT = sbp(st, "qnt", [128, T], BF16)
        KNT = sbp(st, "knt", [128, T], BF16)
        VFT = sbp(st, "vft", [128, T], BF16)
        VTM = sbp(st, "vtm", [128, 17, 128], BF16)
        BP = sbp(st, "bp", [128, 5, 128])
        BSM = sbp(st, "bsm", [128, 5, 32])
        CKT = sbp(st, "ckt", [128, 2, 512], BF16)
        CKS = sbp(st, "cks", [128, 2, 4, 128], BF16)
        CVS = sbp(st, "cvs", [128, 2, 4, 128], BF16)
        OCP = Pool([sbp(st, "oc%d" % i, [128, T], BF16) for i in range(2)])
        KO = Pool([sbp(st, "ko%d" % i, [128, 512]) for i in range(3)])
        SCP = Pool([sbp(st, "sc%d" % i, [128, 640]) for i in range(2)])
        PTP = Pool([sbp(st, "ptp%d" % i, [128, 640], BF16) for i in range(2)])
        for h in range(8):
            load(BP[:, :, :], bias_p_d[l, :, h, :, :], [BP])
            load(BSM[:, :, :], bias_s_d[l, :, h, :, :], [BSM])
            for s in range(2):
                load(CKS[:, s, :, :], cache_k[l, s].rearrange("(a p) c -> p a c", p=128)[:, :, h * 128:(h + 1) * 128], [CKS], eng="pool")
                load(CVS[:, s, :, :], cache_v[l, s].rearrange("(a p) c -> p a c", p=128)[:, :, h * 128:(h + 1) * 128], [CVS], eng="pool")
            proj_chunk(w_in[l, :, 6160 + h * 128:6160 + (h + 1) * 128], headnorm_consume(QNT, V_QN))
            proj_chunk(w_in[l, :, 7184 + h * 128:7184 + (h + 1) * 128], headnorm_consume(KNT, V_KN, kv_out(l, h, o_pk, o_sk, KO)))
            vkeep = kv_out(l, h, o_pv, o_sv, KO)

            def vconsume(ti, t0, n, pt):
                raw = W512.next()
                act(raw[:, 0:n], pt[:, 0:n], AF.Copy, [pt], [raw])
                dve(lambda e: e.tensor_copy(out=VFT[:, t0:t0 + n], in_=raw[:, 0:n]), [raw], [VFT])
                vkeep(ti, t0, n, raw)
                pb = PSB.next()
                nq = 4 if ti < 4 else 1
                w = 128 if ti < 4 else 64
                for q in range(nq):
                    pe(lambda e, q=q: e.transpose(out=pb[0:w, q * 128:(q + 1) * 128], in_=VFT[:, t0 + q * 128:t0 + q * 128 + w], identity=ident_b),
                       [VFT, CSTB], [pb])
                dve(lambda e: e.tensor_copy(out=VTM[0:w, 4 * ti:4 * ti + nq, :], in_=pb[0:w, 0:nq * 128].rearrange("p (a c) -> p a c", a=nq)),
                    [pb], [VTM])
            proj_chunk(w_in[l, :, 8208 + h * 128:8208 + (h + 1) * 128], vconsume)
            for s in range(2):
                pb = PSB.next()
                for a in range(4):
                    pe(lambda e, pb=pb, s=s, a=a: e.transpose(out=pb[:, a * 128:(a + 1) * 128], in_=CKS[:, s, a, :], identity=ident_b),
                       [CKS, CSTB], [pb])
                dve(lambda e, pb=pb, s=s: e.tensor_copy(out=CKT[:, s, :], in_=pb[:, 0:512]), [pb], [CKT])
            oc = OCP.next()
            ptts = {}

            def att_A(m):
                i0 = max(0, 4 - m)
                pa = PS.next()
                pbk = PS.next()
                for i in range(i0, 5):
                    kt = m - 4 + i
                    dstp = pa[:, i * 128:(i + 1) * 128] if i < 4 else pbk[:, 0:128]
                    pe(lambda e, dstp=dstp, kt=kt, m=m: e.matmul(dstp, lhsT=KNT[:, kt * 128:(kt + 1) * 128], rhs=QNT[:, m * 128:(m + 1) * 128],
                                                              start=True, stop=True), [KNT, QNT], [pa if i < 4 else pbk])
                sc = SCP.next()
                if i0 < 4:
                    dve(lambda e, sc=sc, pa=pa, i0=i0: e.scalar_tensor_tensor(
                        out=sc[:, i0 * 128:512], in0=pa[:, i0 * 128:512], scalar=SCL,
                        in1=BP[:, i0:4, :].rearrange("p a q -> p (a q)"), op0=ALU.mult, op1=ALU.add), [pa, BP], [sc])
                dve(lambda e, sc=sc, pbk=pbk: e.scalar_tensor_tensor(out=sc[:, 512:640], in0=pbk[:, 0:128], scalar=SCL, in1=BP[:, 4, :],
                                                                     op0=ALU.mult, op1=ALU.add), [pbk, BP], [sc])
                ptt = PTP.next()
                act(ptt[:, i0 * 128:640], sc[:, i0 * 128:640], AF.Exp, [sc], [ptt])
                ptts[m] = (ptt, i0)

            def att_B(m):
                ptt, i0 = ptts.pop(m)
                po = PS.next()
                mm_acc(po[:, 0:128], [(VTM[:, m - 4 + i, :], ptt[:, i * 128:(i + 1) * 128]) for i in range(i0, 5)], [VTM, ptt], [po])
                psm = PS.next()
                mm_acc(psm[:, 0:128], [(ones_b, ptt[:, i * 128:(i + 1) * 128]) for i in range(i0, 5)], [CSTB, ptt], [psm])
                rec = W512.next()
                dve(lambda e, rec=rec, psm=psm: e.reciprocal(out=rec[:, 0:128], in_=psm[:, 0:128]), [psm], [rec])
                dve(lambda e, oc=oc, po=po, rec=rec, m=m: e.tensor_tensor(out=oc[:, m * 128:(m + 1) * 128], in0=po[:, 0:128], in1=rec[:, 0:128],
                                                                          op=ALU.mult), [po, rec], [oc])
            att_A(0)
            for m in range(16):
                if m + 1 < 16:
                    att_A(m + 1)
                att_B(m)
            for s in range(2):
                tq = 2048 + 32 * s
                pa = PS.next()
                for a in range(4):
                    pe(lambda e, pa=pa, a=a, s=s, tq=tq: e.matmul(pa[:, a * 32:(a + 1) * 32], lhsT=CKT[:, s, a * 128:(a + 1) * 128],
                                                                  rhs=QNT[:, tq:tq + 32], start=True, stop=True), [CKT, QNT], [pa])
                pe(lambda e, pa=pa, s=s, tq=tq: e.matmul(pa[32 * s:32 * s + 32, 128:160], lhsT=KNT[:, tq:tq + 32], rhs=QNT[:, tq:tq + 32],
                                                         start=True, stop=True), [KNT, QNT], [pa])
                sc = SCP.next()
                dve(lambda e, sc=sc, pa=pa: e.scalar_tensor_tensor(out=sc[:, 0:128], in0=pa[:, 0:128], scalar=SCL,
                                                                   in1=BSM[:, 0:4, :].rearrange("p a q -> p (a q)"), op0=ALU.mult, op1=ALU.add),
                    [pa, BSM], [sc])
                dve(lambda e, sc=sc, pa=pa, s=s: e.scalar_tensor_tensor(out=sc[32 * s:32 * s + 32, 128:160], in0=pa[32 * s:32 * s + 32, 128:160],
                                                                        scalar=SCL, in1=BSM[32 * s:32 * s + 32, 4, :], op0=ALU.mult, op1=ALU.add),
                    [pa, BSM, sc], [sc])
                ptt = PTP.next()
                act(ptt[:, 0:128], sc[:, 0:128], AF.Exp, [sc], [ptt])
                act(ptt[32 * s:32 * s + 32, 128:160], sc[32 * s:32 * s + 32, 128:160], AF.Exp, [sc, ptt], [ptt])
                po = PS.next()
                prs = [(CVS[:, s, a, :], ptt[:, a * 32:(a + 1) * 32]) for a in range(4)]
                prs.append((VTM[32 * s:32 * s + 32, 16, :], ptt[32 * s:32 * s + 32, 128:160]))
                mm_acc(po[:, 0:32], prs, [CVS, VTM, ptt], [po])
                psm = PS.next()
                prs2 = [(ones_b, ptt[:, a * 32:(a + 1) * 32]) for a in range(4)]
                prs2.append((CSTB[32 * s:32 * s + 32, 128:256], ptt[32 * s:32 * s + 32, 128:160]))
                mm_acc(psm[:, 0:32], prs2, [CSTB, ptt], [psm])
                rec = W512.next()
                dve(lambda e, rec=rec, psm=psm: e.reciprocal(out=rec[:, 0:32], in_=psm[:, 0:32]), [psm], [rec])
                dve(lambda e, oc=oc, po=po, rec=rec, tq=tq: e.tensor_tensor(out=oc[:, tq:tq + 32], in0=po[:, 0:32], in1=rec[:, 0:32], op=ALU.mult),
                    [po, rec], [oc])
            store(OUTS[(16 + h) * 128:(17 + h) * 128, :], oc[:, :], [oc], [R("outs", 16 + h)])
        P.flush()
        st.close()

    DZS = dscr("dzs", [T, 1024])
    CHK = [(n * 64, 64, 0) for n in range(32)] + [(2048, 32, 1), (2080, 32, 2)]
    NCK = len(CHK)

    def mixer_b(l):
        st = ExitStack()
        B = PS.items
        BB = PSB.items
        RXB = sbp(st, "rxb", [128, CB])
        QNT = sbp(st, "qnt", [128, T], BF16)
        KNT = sbp(st, "knt", [128, T], BF16)
        VCB = sbp(st, "vcb", [128, T], BF16)
        OBT = sbp(st, "obt", [128, T], BF16)
        GBT = sbp(st, "gbt", [64, NCK, 24])
        BEG = sbp(st, "beg", [64, NCK, 8])
        EKD = sbp(st, "ekd", [64, NCK, 8])
        EGL = sbp(st, "egl", [128, NCK, 8])
        NEGA = sbp(st, "nega", [128, 8])
        SF = sbp(st, "sf", [128, 3, 128])
        SFr = [Res("sfr%d" % i) for i in range(3)]
        GSU = sbp(st, "gsu", [64, NCK * 16])
        GS = Tn(GSU.h[:, :].rearrange("p (c x) -> p c x", x=16), "gsu")
        GS.res = GSU.res
        GUW = Tn(GSU.h[:, 0:512].rearrange("p (w c) -> p w c", w=8), "gsu")
        GUW.res = GSU.res
        GBr = Res("gbr")

        st0 = ExitStack()
        mkws(st0, 3, 8192)
        act(NEGA[:, :], VEC[:, V_ALOG:V_ALOG + 8], AF.Exp, [VEC], [NEGA])
        dve(lambda e: e.tensor_scalar(out=NEGA[:, :], in0=NEGA[:, :], scalar1=-1.0, scalar2=None, op0=ALU.mult), [NEGA], [NEGA])
        sab, vab = wload(w_in[l, :, 6144:6160], 16, 16)
        GRP = [(0, 32, 64), (32, 2, 32)]
        for gi, (c0, ncx, C) in enumerate(GRP):
            bk = B[gi]
            for i in range(ncx):
                t0 = CHK[c0 + i][0]
                mm_acc(bk[0:C, i * 16:(i + 1) * 16], [(ACT_T[:, k, t0:t0 + C], vab[:, k, :]) for k in range(16)], [sab] + ACTR, [bk])
            pv3 = bk[0:C, 0:ncx * 16].rearrange("p (c x) -> p c x", x=16)
            tm = GS[0:C, c0:c0 + ncx, 0:8]
            dve(lambda e, tm=tm, pv3=pv3, C=C, ncx=ncx: e.tensor_tensor(
                out=tm, in0=pv3[:, :, 0:8], in1=VEC[0:C, V_DTB:V_DTB + 8].unsqueeze(1).broadcast_to([C, ncx, 8]), op=ALU.add),
                [bk, VEC], [GS])
            act(tm, tm, AF.Exp, [GS], [GS])
            act(tm, tm, AF.Ln, [GS], [GS], bias=EPSC[0:C, 1:2], scale=1.0)
            dve(lambda e, tm=tm, C=C, ncx=ncx, c0=c0: e.tensor_tensor(
                out=GBT[0:C, c0:c0 + ncx, 0:8], in0=tm, in1=NEGA[0:C, :].unsqueeze(1).broadcast_to([C, ncx, 8]), op=ALU.mult),
                [GS, NEGA], [GBr])
            act(GBT[0:C, c0:c0 + ncx, 8:16], pv3[:, :, 8:16], AF.Sigmoid, [bk], [GBr])
            dve(lambda e, C=C, ncx=ncx, c0=c0: e.tensor_scalar(out=GBT[0:C, c0:c0 + ncx, 16:24], in0=GBT[0:C, c0:c0 + ncx, 8:16],
                                                              scalar1=-1.0, scalar2=None, op0=ALU.mult), [GBr], [GBr])
        for gi, (c0, ncx, C) in enumerate(GRP):
            bk = B[2 + gi]
            bk2 = B[4 + gi]
            for i in range(ncx):
                ci = c0 + i
                pe(lambda e, bk=bk, C=C, ci=ci, i=i: e.matmul(bk[0:C, i * 16:i * 16 + 8], lhsT=CST[0:C, C_UPI:C_UPI + C], rhs=GBT[0:C, ci, 0:8],
                                                             start=True, stop=True), [CST, GBr], [bk])
                pe(lambda e, bk=bk, C=C, ci=ci, i=i: e.matmul(bk[0:C, i * 16 + 8:i * 16 + 16], lhsT=CST[0:C, C_ONE:C_ONE + C], rhs=GBT[0:C, ci, 0:8],
                                                             start=True, stop=True), [CST, GBr], [bk])
                pe(lambda e, bk2=bk2, C=C, ci=ci, i=i: e.matmul(bk2[:, i * 8:i * 8 + 8], lhsT=CST[0:C, C_ONE:C_ONE + 128], rhs=GBT[0:C, ci, 0:8],
                                                               start=True, stop=True), [CST, GBr], [bk2])
            gsv = GS[0:C, c0:c0 + ncx, :]
            act(gsv, bk[0:C, 0:ncx * 16].rearrange("p (c x) -> p c x", x=16), AF.Copy, [bk], [GS])
            act(EGL[:, c0:c0 + ncx, :], bk2[:, 0:ncx * 8].rearrange("p (c x) -> p c x", x=8), AF.Exp, [bk2], [GBr])
            dve(lambda e, gsv=gsv: e.tensor_tensor(out=gsv[:, :, 8:16], in0=gsv[:, :, 8:16], in1=gsv[:, :, 0:8], op=ALU.subtract), [GS], [GS])
            act(EKD[0:C, c0:c0 + ncx, :], gsv[:, :, 8:16], AF.Exp, [GS], [GBr])
            act(gsv[:, :, 0:8], gsv[:, :, 0:8], AF.Exp, [GS], [GS])
            dve(lambda e, gsv=gsv, C=C, ncx=ncx, c0=c0: e.tensor_tensor(out=BEG[0:C, c0:c0 + ncx, :], in0=gsv[:, :, 0:8],
                                                                        in1=GBT[0:C, c0:c0 + ncx, 8:16], op=ALU.mult), [GS, GBr], [GBr])
        wz = [wload(w_in[l, :, 5120 + hf * 512:5120 + (hf + 1) * 512], 16, 512) for hf in range(2)]
        for ci, (t0, C, sq) in enumerate(CHK):
            for hf in range(2):
                pt = PS.next()
                mm_acc(pt[0:C, :], [(ACT_T[:, k, t0:t0 + C], wz[hf][1][:, k, :]) for k in range(16)], [wz[hf][0]] + ACTR, [pt])
                zt = W512.next()
                act(zt[0:C, :], pt[0:C, :], AF.Silu, [pt], [zt])
                store(DZS[t0:t0 + C, hf * 512:(hf + 1) * 512], zt[0:C, :], [zt], [R("dzs", ci)])

        P.flush()
        st0.close()
        mkws(st, 4, 2048)
        CVP = Pool([sbp(st, "cvp%d" % i_, [128, 512]) for i_ in range(10)])
        EGBW = sbp(st, "egbw", [128, 8, 64], BF16)
        P0T = sbp(st, "p0t", [64, 8, 64])

        def dbl(name, shape, dt, nb=2, nr=4):
            t = [sbp(st, "%s%d" % (name, i), shape, dt) for i in range(nb)]
            r = [[Res("%s_%d_%d" % (name, i, q)) for q in range(nr)] for i in range(nb)]
            if nb == 1:
                t = t * 2
                r = r * 2
            return t, r
        EDW, EDr = dbl("edw", [64, 8, 128], F32, 1, 2)
        QGW, QGr = dbl("qgw", [128, 8, 64], BF16, 2, 1)
        QKTW, QKr = dbl("qktw", [64, 8, 64], BF16, 2, 2)
        KVW, KVr = dbl("kvw", [64, 8, 256], BF16, 1, 2)
        KDEW, KDr = dbl("kdew", [64, 8, 128], BF16, 1, 2)
        NWTW, NWr = dbl("nwtw", [128, 8, 64], BF16, 2, 1)
        NYW, NYr = dbl("nyw", [64, 8, 128], BF16, 1, 4)
        PBW, PBr = dbl("pbw", [64, 8, 64], BF16, 1, 4)
        WUW, WUr = dbl("wuw", [64, 8, 256], BF16, 2, 4)
        MTW, MTr = dbl("mtw", [128, 8, 128], BF16, 2, 4)
        BCW, BCr = dbl("bcw", [128, 8, 128], BF16, 2, 4)
        SBT = sbp(st, "sbt", [128, 2, 128], BF16)
        SBTr = [Res("sbtr0"), Res("sbtr1")]
        SM = Pool([sbp(st, "sm%d" % i, [64, 128]) for i in range(6)])
        SMB = Pool([sbp(st, "smb%d" % i, [64, 128], BF16) for i in range(6)])
        SC1 = Pool([sbp(st, "sc1%d" % i, [64, 2]) for i in range(4)])
        WAVES = [(w * 8, 8, 64) for w in range(4)] + [(32, 2, 32)]

        def do_head(h):
            def q_proj(which):
                c0 = which * 1024 + h * 128
                conv_state_io(RXB, l, st_conv_b, o_conv_b, c0)
                proj_chunk(w_in[l, :, 2048 + c0:2048 + c0 + 128], evac_to_convbuf(RXB))
                conv_state_out(RXB, l, o_conv_b, c0)

            def q_conv(which, dst):
                cvs = []
                for ti, (t0, n) in enumerate(TT):
                    cv = conv_tile(RXB, V_WCB + which * 8 + h, 24, ti, t0, n, pool=CVP)
                    cvs.append(cv)
                return cvs

            def q_tail(which, dst, cvs):
                for ti, (t0, n) in enumerate(TT):
                    cv = cvs[ti]
                    if which == 2:
                        act(dst[:, t0:t0 + n], cv[:, 0:n], AF.Silu, [cv], [dst])
                        continue
                    act(cv[:, 0:n], cv[:, 0:n], AF.Silu, [cv], [cv])
                    sq_ = B512.next()
                    act(sq_[:, 0:n], cv[:, 0:n], AF.Square, [cv], [sq_])
                    p2 = PS.next()
                    pe(lambda e, p2=p2, sq_=sq_, n=n: e.matmul(p2[:, 0:n], lhsT=ones_b, rhs=sq_[:, 0:n], start=True, stop=True), [sq_, CSTB], [p2])
                    sd = W512.next()
                    act(sd[:, 0:n], p2[:, 0:n], AF.Sqrt, [p2], [sd], bias=EPSC[:, 0:1], scale=1.0)
                    dve(lambda e, sd=sd, n=n: e.reciprocal(out=sd[:, 0:n], in_=sd[:, 0:n]), [sd], [sd])
                    dve(lambda e, sd=sd, cv=cv, t0=t0, n=n: e.scalar_tensor_tensor(
                        out=dst[:, t0:t0 + n], in0=cv[:, 0:n], scalar=(SCL if which == 0 else 1.0), in1=sd[:, 0:n],
                        op0=ALU.mult, op1=ALU.mult), [cv, sd], [dst])
            q_proj(0)
            cq = q_conv(0, QNT)
            q_proj(1)
            q_tail(0, QNT, cq)
            ck = q_conv(1, KNT)
            q_proj(2)
            q_tail(1, KNT, ck)
            cvv = q_conv(2, VCB)
            q_tail(2, VCB, cvv)
            for s in range(2):
                load(SF[:, 1 + s, :], st_S[l, s, h, :, :], [SFr[1 + s]], key=("ld", "sf%d" % s))
            dve(lambda e: e.memset(SF[:, 0, :], 0.0), [], [SFr[0]])
            dve(lambda e: e.memset(SBT[:, 0, :], 0.0), [], [SBTr[0]])

            def stage1(wi):
                c0, W, C = WAVES[wi]
                par = wi % 2
                tq0 = CHK[c0][0]
                steps = []
                bc = lambda ap, n_: ap.unsqueeze(1).broadcast_to([C, n_, C])
                NH = (W + 3) // 4
                hv = [(q * 4, min(4, W - q * 4)) for q in range(NH)]
                NP_ = (W + 1) // 2
                pr = [(p * 2, min(2, W - p * 2)) for p in range(NP_)]
                allr = lambda rl: list(rl)

                def v4(ap, n_):
                    return ap.rearrange("p (w a b) -> p w a b", w=n_, a=2)[:, :, :, 0:C]

                def sA():
                    dve(lambda e: e.tensor_tensor(out=GUW[0:C, 0:W, 0:C], in0=bc(CST[0:C, C_UPI:C_UPI + C], W),
                                                  in1=GBT[0:C, c0:c0 + W, h:h + 1].broadcast_to([C, W, C]), op=ALU.mult), [CST, GBr], [GUW])
                    for sl in range(W):
                        bk = B[sl // 4]
                        o = (sl % 4) * 128
                        pe(lambda e, bk=bk, o=o, sl=sl: e.matmul(bk[0:C, o:o + C], lhsT=CST[0:C, C_LOS:C_LOS + C], rhs=GUW[0:C, sl, 0:C],
                                                                  start=True, stop=True), [CST, GUW], [bk])
                        pe(lambda e, bk=bk, o=o, sl=sl: e.matmul(bk[0:C, o + 64:o + 64 + C], lhsT=GUW[0:C, sl, 0:C], rhs=CST[0:C, C_LOS:C_LOS + C],
                                                                  start=True, stop=True), [CST, GUW], [bk])
                        pe(lambda e, sl=sl: e.matmul(B[2][:, sl * 64:sl * 64 + C], lhsT=CST[0:C, C_ONE:C_ONE + 128], rhs=GUW[0:C, sl, 0:C],
                                                      start=True, stop=True), [CST, GUW], [B[2]])
                steps.append(sA)

                def sB():
                    for q, (s0, n_) in enumerate(hv):
                        edv = EDW[par][0:C, s0:s0 + n_, :].rearrange("p w (a b) -> p w a b", a=2)[:, :, :, 0:C]
                        act(edv, v4(B[q][0:C, 0:n_ * 128], n_), AF.Exp, [B[q]], [EDr[par][q]])
                    edv = EDW[par][0:C, 0:W, :].rearrange("p w (a b) -> p w a b", a=2)[:, :, :, 0:C]
                    dve(lambda e: e.tensor_tensor(
                        out=edv, in0=edv, in1=CST[0:C, C_UPI:C_UPI + 128].rearrange("p (a b) -> p a b", a=2)[:, :, 0:C]
                        .unsqueeze(1).broadcast_to([C, W, 2, C]), op=ALU.mult), allr(EDr[par]) + [CST], allr(EDr[par]))
                    egv = EGBW[:, 0:W, 0:C]
                    act(egv, B[2][:, 0:W * 64].rearrange("p (w c) -> p w c", w=W)[:, :, 0:C], AF.Exp, [B[2]], [EGBW])
                    dve(lambda e: e.tensor_tensor(out=QGW[par][:, 0:W, 0:C], in0=QNT[:, tq0:tq0 + W * C].rearrange("p (w c) -> p w c", w=W),
                                                  in1=egv, op=ALU.mult), [QNT, EGBW], [QGr[par][0]])
                    for sl in range(W):
                        t0 = tq0 + sl * C
                        bk = B[sl // 4]
                        o = (sl % 4) * 128
                        pe(lambda e, bk=bk, o=o, t0=t0: e.matmul(bk[0:C, o:o + C], lhsT=KNT[:, t0:t0 + C], rhs=QNT[:, t0:t0 + C], start=True, stop=True),
                           [KNT, QNT], [bk])
                        pe(lambda e, bk=bk, o=o, t0=t0: e.matmul(bk[0:C, o + 64:o + 64 + C], lhsT=KNT[:, t0:t0 + C], rhs=KNT[:, t0:t0 + C], start=True, stop=True),
                           [KNT], [bk])
                steps.append(sB)

                def sC():
                    for q, (s0, n_) in enumerate(hv):
                        kv = v4(B[q][0:C, 0:n_ * 128], n_)
                        edv = EDW[par][0:C, s0:s0 + n_, :].rearrange("p w (a b) -> p w a b", a=2)[:, :, :, 0:C]
                        dve(lambda e, kv=kv, edv=edv, s0=s0, n_=n_: e.tensor_tensor(out=QKTW[par][0:C, s0:s0 + n_, 0:C], in0=kv[:, :, 0, :], in1=edv[:, :, 0, :],
                                                                                  op=ALU.mult), [B[q], EDr[par][q]], [QKr[par][q]])
                        dve(lambda e, kv=kv, edv=edv, s0=s0, n_=n_: e.tensor_tensor(out=P0T[0:C, s0:s0 + n_, 0:C], in0=kv[:, :, 1, :], in1=edv[:, :, 1, :],
                                                                                  op=ALU.mult), [B[q], EDr[par][q]], [P0T])
                    dve(lambda e: e.tensor_tensor(out=PBW[par][0:C, 0:W, 0:C], in0=P0T[0:C, 0:W, 0:C],
                                                  in1=GBT[0:C, c0:c0 + W, 16 + h:17 + h].broadcast_to([C, W, C]), op=ALU.mult),
                        [P0T, GBr], allr(PBr[par]))
                    for sl in range(W):
                        pe(lambda e, sl=sl: e.transpose(out=BB[0][0:C, sl * 64:sl * 64 + C], in_=PBW[par][0:C, sl, 0:C], identity=ident_b[0:C, 0:C]),
                           [PBr[par][sl // 2], CSTB], [BB[0]])
                    for sl in range(min(4, W)):
                        t0 = tq0 + sl * C
                        o = sl * 256
                        pe(lambda e, o=o, t0=t0: e.transpose(out=BB[1][0:C, o:o + 128], in_=KNT[:, t0:t0 + C], identity=ident_b), [KNT, CSTB], [BB[1]])
                        pe(lambda e, o=o, t0=t0: e.transpose(out=BB[1][0:C, o + 128:o + 256], in_=VCB[:, t0:t0 + C], identity=ident_b), [VCB, CSTB], [BB[1]])
                steps.append(sC)

                def kvb(bk, s0, n_, q):
                    kvv = bk[0:C, 0:n_ * 256].rearrange("p (w a d) -> p w a d", w=n_, a=2)
                    dve(lambda e: e.tensor_tensor(out=KVW[par][0:C, s0:s0 + n_, 0:128], in0=kvv[:, :, 0, :],
                                                  in1=BEG[0:C, c0 + s0:c0 + s0 + n_, h:h + 1].broadcast_to([C, n_, 128]), op=ALU.mult), [bk, GBr], [KVr[par][q]])
                    dve(lambda e: e.tensor_tensor(out=KDEW[par][0:C, s0:s0 + n_, :], in0=kvv[:, :, 0, :],
                                                  in1=EKD[0:C, c0 + s0:c0 + s0 + n_, h:h + 1].broadcast_to([C, n_, 128]), op=ALU.mult), [bk, GBr], [KDr[par][q]])
                    dve(lambda e: e.tensor_tensor(out=KVW[par][0:C, s0:s0 + n_, 128:256], in0=kvv[:, :, 1, :],
                                                  in1=GBT[0:C, c0 + s0:c0 + s0 + n_, 8 + h:9 + h].broadcast_to([C, n_, 128]), op=ALU.mult), [bk, GBr], [KVr[par][q]])

                def sD():
                    nv = BB[0][0:C, 0:W * 64].rearrange("p (w c) -> p w c", w=W)[:, :, 0:C]
                    act(NYW[par][0:C, 0:W, 0:C], nv, AF.Copy, [BB[0]], allr(NYr[par]))
                    dve(lambda e: e.tensor_tensor(out=NYW[par][0:C, 0:W, 64:64 + C], in0=nv, in1=bc(ident_b[0:C, 0:C], W), op=ALU.add),
                        [BB[0], CSTB], allr(NYr[par]))
                    kvb(BB[1], 0, min(4, W), 0)
                    if W > 4:
                        for sl in range(4, W):
                            t0 = tq0 + sl * C
                            o = (sl - 4) * 256
                            pe(lambda e, o=o, t0=t0: e.transpose(out=BB[0][0:C, o:o + 128], in_=KNT[:, t0:t0 + C], identity=ident_b), [KNT, CSTB], [BB[0]])
                            pe(lambda e, o=o, t0=t0: e.transpose(out=BB[0][0:C, o + 128:o + 256], in_=VCB[:, t0:t0 + C], identity=ident_b), [VCB, CSTB], [BB[0]])
                        kvb(BB[0], 4, W - 4, 1)
                steps.append(sD)

                def level(lev):
                    def f():
                        for p_, (s0, n2) in enumerate(pr):
                            bk = B[p_ % 3]
                            for j2 in range(n2):
                                sl = s0 + j2
                                o = j2 * 192
                                pe(lambda e, bk=bk, o=o, sl=sl: e.matmul(bk[0:C, o:o + 128], lhsT=PBW[par][0:C, sl, 0:C], rhs=NYW[par][0:C, sl, :],
                                                                          start=True, stop=True), [PBr[par][p_], NYr[par][p_]], [bk])
                                pe(lambda e, bk=bk, o=o, sl=sl: e.matmul(bk[0:C, o + 128:o + 128 + C], lhsT=NYW[par][0:C, sl, 0:C], rhs=PBW[par][0:C, sl, 0:C],
                                                                          start=True, stop=True), [PBr[par][p_], NYr[par][p_]], [bk])
                            pvw = bk[0:C, 0:n2 * 192].rearrange("p (w x) -> p w x", w=n2)
                            if lev > 0:
                                dve(lambda e, pvw=pvw, s0=s0, n2=n2: e.tensor_tensor(out=NYW[par][0:C, s0:s0 + n2, 64:64 + C], in0=pvw[:, :, 64:64 + C],
                                                                                    in1=NYW[par][0:C, s0:s0 + n2, 64:64 + C], op=ALU.add),
                                    [bk, NYr[par][p_]], [NYr[par][p_]])
                            act(NYW[par][0:C, s0:s0 + n2, 0:C], pvw[:, :, 0:C], AF.Copy, [bk], [NYr[par][p_]])
                            act(PBW[par][0:C, s0:s0 + n2, 0:C], pvw[:, :, 128:128 + C], AF.Copy, [bk], [PBr[par][p_]])
                    return f
                for lev in range(6):
                    steps.append(level(lev))

                def sE():
                    for p_, (s0, n2) in enumerate(pr):
                        bk = B[p_ % 3]
                        for j2 in range(n2):
                            sl = s0 + j2
                            pe(lambda e, bk=bk, j2=j2, sl=sl: e.matmul(bk[0:C, j2 * 256:(j2 + 1) * 256], lhsT=NYW[par][0:C, sl, 64:64 + C], rhs=KVW[par][0:C, sl, :],
                                                                        start=True, stop=True), [NYr[par][p_], KVr[par][sl // 4]], [bk])
                        act(WUW[par][0:C, s0:s0 + n2, :], bk[0:C, 0:n2 * 256].rearrange("p (w x) -> p w x", w=n2), AF.Copy, [bk], [WUr[par][p_]])
                    bkw = B[NP_ % 3]
                    for sl in range(W):
                        pe(lambda e, sl=sl: e.matmul(bkw[:, sl * 64:sl * 64 + C], lhsT=KVW[par][0:C, sl, 0:128], rhs=NYW[par][0:C, sl, 64:64 + C],
                                                      start=True, stop=True), [KVr[par][sl // 4], NYr[par][sl // 2]], [bkw])
                    act(NWTW[par][:, 0:W, 0:C], bkw[:, 0:W * 64].rearrange("p (w c) -> p w c", w=W)[:, :, 0:C], AF.Copy, [bkw], [NWr[par][0]], scale=-1.0)
                steps.append(sE)

                def sF():
                    for p_, (s0, n2) in enumerate(pr):
                        bk = B[(p_ + NP_ + 1) % 3]
                        for j2 in range(n2):
                            sl = s0 + j2
                            pe(lambda e, bk=bk, j2=j2, sl=sl: e.matmul(bk[:, j2 * 256:j2 * 256 + 128], lhsT=WUW[par][0:C, sl, 0:128], rhs=KDEW[par][0:C, sl, :],
                                                                        start=True, stop=True), [WUr[par][p_], KDr[par][sl // 4]], [bk])
                            pe(lambda e, bk=bk, j2=j2, sl=sl: e.matmul(bk[:, j2 * 256 + 128:j2 * 256 + 256], lhsT=KDEW[par][0:C, sl, :], rhs=WUW[par][0:C, sl, 128:256],
                                                                        start=True, stop=True), [WUr[par][p_], KDr[par][sl // 4]], [bk])
                        bv = bk[:, 0:n2 * 256].rearrange("p (w a d) -> p w a d", w=n2, a=2)
                        o1 = act(MTW[par][:, s0:s0 + n2, :], bv[:, :, 0, :], AF.Copy, [bk], [MTr[par][p_]], scale=-1.0)
                        P.op("dve", lambda e, bv=bv, s0=s0, n2=n2: e.tensor_copy(out=BCW[par][:, s0:s0 + n2, :], in_=bv[:, :, 1, :]),
                             r=rs([bk]), w=rs([BCr[par][p_]]), extra=[o1])
                steps.append(sF)
                return steps

            kcount = [0]

            def stage2(wi, sl):
                c0, W, C = WAVES[wi]
                par = wi % 2
                ci = c0 + sl
                t0, C, sq = CHK[ci]
                if sq == 0:
                    ko = kcount[0] % 2
                    kcount[0] += 1
                else:
                    ko = 0
                    act(SBT[:, 0, :], SF[:, sq, :], AF.Copy, [SFr[sq]], [SBTr[0]])
                kn = 1 - ko
                by = B[4 + ci % 2]
                bx = B[3]
                pe(lambda e: e.matmul(by[0:C, 0:128], lhsT=NWTW[par][:, sl, 0:C], rhs=SBT[:, ko, :], start=True, stop=False),
                   [NWr[par][0], SBTr[ko]], [by])
                pe(lambda e: e.matmul(by[0:C, 0:128], lhsT=ident_b[0:C, 0:C], rhs=WUW[par][0:C, sl, 128:256], start=False, stop=True),
                   [WUr[par][sl // 2], CSTB], [by])
                pe(lambda e: e.matmul(bx[:, 0:128], lhsT=MTW[par][:, sl, :], rhs=SBT[:, ko, :], start=True, stop=False),
                   [MTr[par][sl // 2], SBTr[ko]], [bx])
                pe(lambda e: e.matmul(bx[:, 0:128], lhsT=ident_b, rhs=BCW[par][:, sl, :], start=False, stop=True),
                   [BCr[par][sl // 2], CSTB], [bx])
                dve(lambda e: e.scalar_tensor_tensor(out=SBT[:, kn, :], in0=SF[:, sq, :], scalar=EGL[:, ci, h:h + 1], in1=bx[:, 0:128],
                                                     op0=ALU.mult, op1=ALU.add), [bx, GBr, SFr[sq]], [SBTr[kn]])
                dve(lambda e: e.scalar_tensor_tensor(out=SF[:, sq, :], in0=SF[:, sq, :], scalar=EGL[:, ci, h:h + 1], in1=bx[:, 0:128],
                                                     op0=ALU.mult, op1=ALU.add), [bx, GBr, SFr[sq]], [SFr[sq]])
                vn = SMB.next()
                act(vn[0:C, :], by[0:C, 0:128], AF.Copy, [by], [vn])
                pe(lambda e: e.matmul(by[0:C, 128:256], lhsT=QGW[par][:, sl, 0:C], rhs=SBT[:, ko, :], start=True, stop=False),
                   [QGr[par][0], SBTr[ko]], [by])
                pe(lambda e: e.matmul(by[0:C, 128:256], lhsT=QKTW[par][0:C, sl, 0:C], rhs=vn[0:C, :], start=False, stop=True),
                   [QKr[par][sl // 4], vn], [by])
                junk = SMB.next()
                s1 = SC1.next()
                act(junk[0:C, :], by[0:C, 128:256], AF.Square, [by], [junk, s1], accum=s1[0:C, 0:1])
                act(s1[0:C, 1:2], s1[0:C, 0:1], AF.Sqrt, [s1], [s1], bias=EPSC[0:C, 0:1], scale=1.0 / 128)
                dve(lambda e: e.reciprocal(out=s1[0:C, 1:2], in_=s1[0:C, 1:2]), [s1], [s1])
                on = SM.next()
                dve(lambda e: e.scalar_tensor_tensor(out=on[0:C, :], in0=by[0:C, 128:256], scalar=s1[0:C, 1:2],
                                                     in1=VEC[0:C, V_DNN:V_DNN + 128], op0=ALU.mult, op1=ALU.mult), [by, s1, VEC], [on])
                dz = SM.next()
                load(dz[0:C, :], DZS[t0:t0 + C, h * 128:(h + 1) * 128], [dz], r=[R("dzs", ci)])
                dve(lambda e: e.tensor_tensor(out=on[0:C, :], in0=on[0:C, :], in1=dz[0:C, :], op=ALU.mult), [on, dz], [on])
                pe(lambda e: e.transpose(out=by[:, 256:256 + C], in_=on[0:C, :], identity=ident[0:C, 0:C]), [on, CST], [by])
                act(OBT[:, t0:t0 + C], by[:, 256:256 + C], AF.Copy, [by], [OBT])

            nw = len(WAVES)
            for s in stage1(0):
                s()
            for wi in range(nw):
                nxt = stage1(wi + 1) if wi + 1 < nw else []
                W = WAVES[wi][1]
                ns = len(nxt)
                per = (ns + W - 1) // W if W else ns
                k = 0
                for i in range(W):
                    stage2(wi, i)
                    for _ in range(per):
                        if k < ns:
                            nxt[k]()
                            k += 1
                while k < ns:
                    nxt[k]()
                    k += 1
            for sq in range(3):
                store(o_S[l, sq, h, :, :], SF[:, sq, :], [SFr[sq]], final=True, key=("st", "sf%d" % sq))
            store(OUTS[(8 + h) * 128:(9 + h) * 128, :], OBT[:, :], [OBT], [R("outs", 8 + h)])
        for h_ in range(8):
            do_head(h_)
        P.flush()
        st.close()


    dve(lambda e: e.memset(VECN[:, :], 1.0), [], [VECN])
    phase0()
    cur = 0
    for l in range(nlayers):
        load(VEC[:, :], vecs_d[l], [VEC])
        if l == 0:
            norm_phase(cur, l, V_NMIX)
        mixer_a(l)
        if "nob" not in dbg:
            mixer_b(l)
        if "noc" not in dbg:
            mixer_c(l)
        merge_phase(l, cur, 1 - cur)
        wout_norm_phase(l, cur, 1 - cur)
        cur = 1 - cur
        ffn_phase(l, cur, 1 - cur)
        ffn_down_norm_phase(l, cur, 1 - cur)
        cur = 1 - cur
        ple_norm_phase(l, cur, 1 - cur, l + 1 < nlayers)
        cur = 1 - cur
    final_phase(cur)
    return finish(nc, P, es, outstores)


def finish(nc, P, es, outstores):
    P.flush()
    es.close()
    return nc


def _vecs(inp):
    v = np.zeros((L, 128, NV), np.float32)
    def pc(a, n):
        return np.ascontiguousarray(a.reshape(n, 128).T)
    for l in range(L):
        v[l, :, V_NMIX:V_NMIX + 16] = pc(inp["norm_mix"][l], 16)
        v[l, :, V_NFFN:V_NFFN + 16] = pc(inp["norm_ffn"][l], 16)
        v[l, :, V_NPLE:V_NPLE + 16] = pc(inp["norm_ple"][l], 16)
        v[l, :, V_BGATE:V_BGATE + 48] = pc(inp["b_gate"][l], 48)
        for j in range(4):
            v[l, :, V_WCA + j * 8:V_WCA + j * 8 + 8] = pc(inp["w_conv_a"][l, j], 8)
            v[l, :, V_WCB + j * 24:V_WCB + j * 24 + 24] = pc(inp["w_conv_b"][l, j], 24)
        v[l, :, V_BCA:V_BCA + 8] = pc(inp["b_conv_a"][l], 8)
        v[l, :, V_BLR:V_BLR + 8] = pc(inp["b_lru_r"][l], 8)
        v[l, :, V_BLI:V_BLI + 8] = pc(inp["b_lru_i"][l], 8)
        v[l, :, V_LAM:V_LAM + 8] = pc(inp["lru_lambda"][l], 8)
        v[l, :, V_QN] = inp["attn_q_norm"][l]
        v[l, :, V_KN] = inp["attn_k_norm"][l]
        v[l, :, V_ALOG:V_ALOG + 8] = inp["dn_a_log"][l][None, :]
        v[l, :, V_DTB:V_DTB + 8] = inp["dn_dt_bias"][l][None, :]
        v[l, :, V_DNN:V_DNN + 128] = inp["dn_norm"][l][None, :]
    return v


def _consts():
    c = np.zeros((128, NCC), np.float32)
    c[:, C_ID:C_ID + 128] = np.eye(128, dtype=np.float32)
    c[:, C_ONE:C_ONE + 128] = 1.0
    i = np.arange(64)
    for half in (0, 64):
        c[half:half + 64, C_UPI:C_UPI + 64] = (i[:, None] <= i[None, :])
        c[half:half + 64, C_LOS:C_LOS + 64] = (i[:, None] > i[None, :])
        c[half:half + 64, C_LOI:C_LOI + 64] = (i[:, None] >= i[None, :])
        c[half:half + 64, C_UPS:C_UPS + 64] = (i[:, None] < i[None, :])
    return c


def _bias_tables(rel_bias):
    NEG = np.float32(-30000.0)
    ext = np.concatenate([rel_bias, np.full((L, 8, 1), NEG, np.float32)], axis=2)
    kk = np.arange(128)[:, None, None]
    ti = np.arange(5)[None, :, None]
    qq = np.arange(128)[None, None, :]
    kpos = (ti - 4) * 128 + kk
    rel = qq - kpos
    idx = np.clip(rel, -128, 128) + 128
    qc = qq // 64
    kc = (kpos + 512) // 64 - 8
    vis = (kc <= qc) & (kc >= qc - 8)
    idx = np.where(vis, idx, 257)
    bp = ext[:, :, idx]
    bp = np.ascontiguousarray(np.transpose(bp, (0, 2, 1, 3, 4)))
    row = np.arange(128)[:, None, None]
    til = np.arange(5)[None, :, None]
    j = np.where(til < 4, til * 128 + row, 512 + row % 32)
    valid = (til < 4) | (row < 64)
    q = np.arange(32)[None, None, :]
    rel = 512 + q - j
    idx = np.clip(rel, -128, 128) + 128
    idx = np.where(valid & (q >= 0), idx, 257)
    bs = ext[:, :, idx]
    bs = np.ascontiguousarray(np.transpose(bs, (0, 2, 1, 3, 4)))
    return bp, bs


def _lru_bd(inp):
    bd = np.zeros((L, 2, 8, 128, 128), np.float32)
    for l in range(L):
        for g, nm in enumerate(("w_lru_r", "w_lru_i")):
            w = inp[nm][l]
            for c in range(8):
                bd[l, g, c, 0:64, 0:64] = w[2 * c]
                bd[l, g, c, 64:128, 64:128] = w[2 * c + 1]
    return bd


def make_in_maps(inp, cores=range(8)):
    f = lambda a: np.ascontiguousarray(a, dtype=np.float32)
    shared = {
        "w_in": f(inp["w_in"]), "w_gate": f(inp["w_gate"]),
        "w_bo": f(inp["w_branch_out"]).reshape(L, 3072, D), "w_out": f(inp["w_out"]),
        "w_fg": f(inp["w_ffn_gate"]), "w_fu": f(inp["w_ffn_up"]), "w_fd": f(inp["w_ffn_down"]),
        "w_pg": f(inp["w_ple_gate"]), "w_pp": f(inp["w_ple_proj"]),
        "lru_bd": _lru_bd(inp), "vecs": _vecs(inp), "consts": _consts(),
    }
    bp, bs = _bias_tables(f(inp["attn_rel_bias"]))
    shared["attn_bias_p"] = bp
    shared["attn_bias_s"] = bs
    maps = []
    for c in cores:
        s = slice(2 * c, 2 * c + 2)
        m = dict(shared)
        m["x_tok"] = np.concatenate([inp["x_prompt"][c], inp["x_sample"][s].reshape(64, D)], 0).astype(np.float32)
        m["pe_tok"] = np.concatenate([inp["p_prompt"][:, c], inp["p_sample"][:, s].reshape(L, 64, PLE)], 1).astype(np.float32)
        m["cache_k"] = f(inp["cache_attn_k"][:, s]).reshape(L, 2, 512, 1024)
        m["cache_v"] = f(inp["cache_attn_v"][:, s]).reshape(L, 2, 512, 1024)
        m["st_conv_a"] = f(inp["state_conv_a"][:, s])
        m["st_lru_h"] = f(inp["state_lru_h"][:, s])
        m["st_conv_b"] = f(inp["state_conv_b"][:, s])
        m["st_S"] = f(inp["state_delta_S"][:, s])
        maps.append(m)
    return maps


def kernel(**inputs):
    inp = {k: np.asarray(v) for k, v in inputs.items()}
    nc = build()
    maps = make_in_maps(inp, cores=range(8))
    res = run_bass_kernel_spmd(nc, maps, core_ids=list(range(8)))
    rs_ = [{k: np.asarray(v) for k, v in r.items()} for r in res.results]
    f = np.float32
    y_p = np.stack([r["y_p"] for r in rs_]).astype(f)
    y_s = np.concatenate([r["y_s"].reshape(2, TS, D) for r in rs_], 0).astype(f)

    def pst(name, shp_tail):
        p = np.stack([r[name][:, 0] for r in rs_], 1).astype(f)
        s = np.concatenate([r[name][:, 1:3] for r in rs_], 1).astype(f)
        return p.reshape((L, 8) + shp_tail), s.reshape((L, 16) + shp_tail)
    p_ca, s_ca = pst("o_conv_a", (3, 1024))
    p_h, s_h = pst("o_lru_h", (1024,))
    p_cb, s_cb = pst("o_conv_b", (3, 3072))
    p_S, s_S = pst("o_S", (8, 128, 128))
    p_k = np.stack([r["o_pk"] for r in rs_], 1).reshape(L, 8, 512, 8, 128).astype(f)
    p_v = np.stack([r["o_pv"] for r in rs_], 1).reshape(L, 8, 512, 8, 128).astype(f)
    s_k = np.concatenate([r["o_sk"] for r in rs_], 1).reshape(L, 16, TS, 8, 128).astype(f)
    s_v = np.concatenate([r["o_sv"] for r in rs_], 1).reshape(L, 16, TS, 8, 128).astype(f)
    return (y_p, y_s, p_ca, p_h, p_cb, p_S, p_k, p_v, s_ca, s_h, s_cb, s_S, s_k, s_v)
```

```python
import numpy as np
from contextlib import ExitStack
import concourse.bass as bass
import concourse.mybir as mybir
from concourse.bass_utils import run_bass_kernel_spmd

F32 = mybir.dt.float32
BF16 = mybir.dt.bfloat16
AF = mybir.ActivationFunctionType
ALU = mybir.AluOpType
AX = mybir.AxisListType

D = 2048
NCH = 16
TP = 2048
TS = 32
T = TP + 2 * TS
NIN = 9232
DFF = 5632
NFF = 44
PLE = 256
L = 2
TT = [(0, 512), (512, 512), (1024, 512), (1536, 512), (2048, 64)]
GELU_K = 2.0 * 0.7978845608028654
SEQS = [(0, 2048, 0), (2048, 32, 2051), (2080, 32, 2086)]
CB = 2121

V_NMIX, V_NFFN, V_NPLE = 0, 16, 32
V_BGATE = 48
V_WCA = 96
V_BCA = 128
V_BLR = 136
V_BLI = 144
V_LAM = 152
V_WCB = 160
V_QN = 256
V_KN = 257
V_ALOG = 258
V_DTB = 266
V_DNN = 274
NV = 402
C_ID, C_ONE, C_UPI, C_LOS, C_LOI, C_UPS = 0, 128, 256, 320, 384, 448
NCC = 512


ALL_RES = []


class Res:
    __slots__ = ("name", "w", "rd")

    def __init__(self, name):
        self.name = name
        self.w = None
        self.rd = []
        ALL_RES.append(self)


class Op:
    __slots__ = ("eng", "fn", "deps", "dma", "key", "sig", "val", "sem")


class Prog:
    ENGS = ["sync", "act", "pool", "dve", "pe"]

    def __init__(self, nc, es):
        self.nc = nc
        self.es = es
        self.ops = {e: [] for e in self.ENGS}
        self.keycnt = {}
        self.engcnt = {e: 0 for e in self.ENGS}
        self.engsem = {e: es.enter_context(nc.semaphore("s_" + e)) for e in self.ENGS}
        self.keysem = {}
        self.n = 0

    def op(self, eng, fn, r=(), w=(), dma=False, key=None, extra=()):
        o = Op()
        o.eng = eng
        o.fn = fn
        o.dma = dma
        o.key = key
        o.sig = dma
        o.val = 0
        o.sem = None
        deps = {}
        for x in r:
            if x.w is not None:
                deps[id(x.w)] = (x.w, True)
        for x in w:
            if x.w is not None and id(x.w) not in deps:
                deps[id(x.w)] = (x.w, False)
            for q in x.rd:
                if id(q) not in deps:
                    deps[id(q)] = (q, False)
        for q in extra:
            deps[id(q)] = (q, True)
        fin = []
        for d, raw in deps.values():
            if d is o:
                continue
            if d.dma:
                fin.append((d, self.keycnt[d.key] * 16))
                continue
            if (not dma) and d.eng == eng:
                if eng == "pe" or not raw:
                    continue
            d.sig = True
            fin.append((d, None))
        o.deps = fin
        if dma:
            assert key is not None
            self.keycnt[key] = self.keycnt.get(key, 0) + 1
        for x in r:
            x.rd.append(o)
        for x in w:
            x.w = o
            x.rd = []
        self.ops[eng].append(o)
        self.n += 1
        return o

    def flush(self):
        nc = self.nc
        for k in self.keycnt:
            if k not in self.keysem:
                self.keysem[k] = self.es.enter_context(nc.semaphore("k_%d" % len(self.keysem)))
        for e in self.ENGS:
            if self.ops[e]:
                last = [o for o in self.ops[e] if not o.dma]
                if last:
                    last[-1].sig = True
            for o in self.ops[e]:
                if o.dma:
                    o.sem = self.keysem[o.key]
                elif o.sig:
                    self.engcnt[e] += 1
                    o.val = self.engcnt[e]
                    o.sem = self.engsem[e]
        ops = self.ops
        finals = [(self.engsem[e], self.engcnt[e]) for e in self.ENGS if self.engcnt[e] > 0]
        finals += [(self.keysem[k], c * 16) for k, c in self.keycnt.items()]

        def make(e):
            def body(eng):
                waited = {}
                for o in ops[e]:
                    for d, v in o.deps:
                        s = d.sem
                        if v is None:
                            v = d.val
                        if waited.get(id(s), 0) < v:
                            eng.wait_ge(s, v)
                            waited[id(s)] = v
                    ins = o.fn(eng)
                    if o.sig:
                        ins.then_inc(o.sem, 16 if o.dma else 1)
                for s, v in finals:
                    if v > 0:
                        eng.wait_ge(s, v)
            return body

        with nc.Block() as block:
            block.sync(make("sync"))
            block.scalar(make("act"))
            block.gpsimd(make("pool"))
            block.vector(make("dve"))
            block.tensor(make("pe"))
        self.ops = {e: [] for e in self.ENGS}
        for x in ALL_RES:
            x.w = None
            x.rd = []


class Tn:
    __slots__ = ("h", "res", "name")

    def __init__(self, h, name):
        self.h = h
        self.res = Res(name)
        self.name = name

    def __getitem__(self, k):
        return self.h[k]


class Pool:
    def __init__(self, items):
        self.items = items
        self.i = 0

    def next(self):
        t = self.items[self.i % len(self.items)]
        self.i += 1
        return t


def build(dbg=(), nlayers=L, stop=None):
    nc = bass.Bass("TRN2", target_bir_lowering=False)
    es = ExitStack()
    del ALL_RES[:]
    P = Prog(nc, es)
    resmap = {}
    outstores = []

    def R(*key):
        if key not in resmap:
            resmap[key] = Res(str(key))
        return resmap[key]

    def din(name, shape, dt=F32):
        return nc.dram_tensor(name, list(shape), dt, kind="ExternalInput").ap()

    def dout(name, shape, dt=F32):
        return nc.dram_tensor(name, list(shape), dt, kind="ExternalOutput").ap()

    def dscr(name, shape, dt=F32):
        kind = "ExternalOutput" if name in dbg else "Internal"
        return nc.dram_tensor(name, list(shape), dt, kind=kind).ap()

    def sb(name, shape, dt=F32):
        return Tn(es.enter_context(nc.sbuf_tensor(name, list(shape), dt)), name)

    def sbpool(name, shape, dt, n):
        return Pool([sb("%s%d" % (name, i), shape, dt) for i in range(n)])

    def psum(name, shape, dt=F32):
        return Tn(es.enter_context(nc.psum_tensor(name, list(shape), dt)), name)

    def load(dst_ap, src_ap, w, r=(), eng="sync", key=None):
        wr = [x.res if isinstance(x, Tn) else x for x in w]
        rr = [x.res if isinstance(x, Tn) else x for x in r]
        k = key if key is not None else ("ld", wr[0].name)
        return P.op(eng, lambda e: e.dma_start(out=dst_ap, in_=src_ap), r=rr, w=wr, dma=True, key=k)

    def store(dst_ap, src_ap, r, w=(), eng="sync", key=None, final=False):
        wr = [x.res if isinstance(x, Tn) else x for x in w]
        rr = [x.res if isinstance(x, Tn) else x for x in r]
        k = key if key is not None else ("st", rr[0].name)
        o = P.op(eng, lambda e: e.dma_start(out=dst_ap, in_=src_ap), r=rr, w=wr, dma=True, key=k)
        if final:
            outstores.append(o)
        return o

    def rs(xs):
        return [x.res if isinstance(x, Tn) else x for x in xs]

    def act(out, in_, func, r, w, bias=None, scale=None, accum=None):
        kw = {}
        if bias is not None:
            kw["bias"] = bias
        if scale is not None:
            kw["scale"] = scale
        if accum is not None:
            kw["accum_out"] = accum
        return P.op("act", lambda e: e.activation(out=out, in_=in_, func=func, **kw), r=rs(r), w=rs(w))

    def dve(fn, r, w):
        return P.op("dve", fn, r=rs(r), w=rs(w))

    def pe(fn, r, w):
        return P.op("pe", fn, r=rs(r), w=rs(w))

    x_tok = din("x_tok", [T, D])
    pe_tok = din("pe_tok", [L, T, PLE])
    cache_k = din("cache_k", [L, 2, 512, 1024])
    cache_v = din("cache_v", [L, 2, 512, 1024])
    st_conv_a = din("st_conv_a", [L, 2, 3, 1024])
    st_lru_h = din("st_lru_h", [L, 2, 1024])
    st_conv_b = din("st_conv_b", [L, 2, 3, 3072])
    st_S = din("st_S", [L, 2, 8, 128, 128])
    w_in = din("w_in", [L, D, NIN])
    w_gate = din("w_gate", [L, D, 3 * D])
    w_bo = din("w_bo", [L, 3 * 1024, D])
    w_out = din("w_out", [L, D, D])
    w_fg = din("w_fg", [L, D, DFF])
    w_fu = din("w_fu", [L, D, DFF])
    w_fd = din("w_fd", [L, DFF, D])
    w_pg = din("w_pg", [L, D, D])
    w_pp = din("w_pp", [L, PLE, D])
    lru_bd = din("lru_bd", [L, 2, 8, 128, 128])
    vecs_d = din("vecs", [L, 128, NV])
    bias_p_d = din("attn_bias_p", [L, 128, 8, 5, 128])
    bias_s_d = din("attn_bias_s", [L, 128, 8, 5, 32])
    consts_d = din("consts", [128, NCC])

    y_p = dout("y_p", [TP, D])
    y_s = dout("y_s", [2 * TS, D])
    o_conv_a = dout("o_conv_a", [L, 3, 3, 1024])
    o_lru_h = dout("o_lru_h", [L, 3, 1024])
    o_conv_b = dout("o_conv_b", [L, 3, 3, 3072])
    o_S = dout("o_S", [L, 3, 8, 128, 128])
    o_pk = dout("o_pk", [L, 512, 1024])
    o_pv = dout("o_pv", [L, 512, 1024])
    o_sk = dout("o_sk", [L, 2, 32, 1024])
    o_sv = dout("o_sv", [L, 2, 32, 1024])

    XT = [dscr("xt0", [D, T]), dscr("xt1", [D, T])]
    OUTS = dscr("outs_t", [3072, T], BF16)
    HID = dscr("hid_t", [DFF, T], BF16)

    uniq = [0]

    def sbp(st, name, shape, dt=F32):
        uniq[0] += 1
        return Tn(st.enter_context(nc.sbuf_tensor("%s_u%d" % (name, uniq[0]), list(shape), dt)), name)

    ACT_T = sb("act_t", [128, NCH, T], BF16)
    ACTR = [Res("actr%d" % i) for i in range(len(TT))]
    CST = sb("cst", [128, NCC])
    CSTB = sb("cstb", [128, 256], BF16)
    VEC = sb("vec", [128, NV])
    WSH = [None]

    def mkws(st, n, nelem):
        WSH[0] = Pool([sbp(st, "ws%d" % i, [128, nelem], BF16) for i in range(n)])
    PS = Pool([psum("ps%d" % i, [128, 512]) for i in range(6)])
    PSB = Pool([psum("psb%d" % i, [128, 1024], BF16) for i in range(2)])
    W512 = sbpool("w512", [128, 512], F32, 7)
    B512 = sbpool("b512", [128, 512], BF16, 4)

    ident = CST[:, C_ID:C_ID + 128]
    ones_b = CSTB[:, 128:256]
    ident_b = CSTB[:, 0:128]

    load(CST[:, :], consts_d[:, :], [CST])
    dve(lambda e: e.tensor_copy(out=CSTB[:, :], in_=CST[:, 0:256]), [CST], [CSTB])

    def XR(b, j, ti):
        return R("xt", b, j, ti)

    def mm_acc(ps_ap, pairs, r, w):
        def fn(e):
            ins = None
            n = len(pairs)
            for i, (a, b) in enumerate(pairs):
                ins = e.matmul(ps_ap, lhsT=a, rhs=b, start=(i == 0), stop=(i == n - 1))
            return ins
        return pe(fn, r, w)

    def wload(src2d, nk, ncols):
        s = WSH[0].next()
        v = s[:, 0:nk * ncols].rearrange("p (k n) -> p k n", k=nk)
        load(v, src2d.rearrange("(k p) n -> p k n", p=128), [s], eng="pool")
        return s, v

    def slow_load(dst, src, w, r=()):
        wr = rs(w)
        return P.op("sync", lambda e: e.dma_start(out=dst, in_=src, allow_slow_non_contiguous=True),
                    r=rs(r), w=wr, dma=True, key=("ld", wr[0].name))

    def slow_store(dst, src, r, final=True):
        rr = rs(r)
        o = P.op("sync", lambda e: e.dma_start(out=dst, in_=src, allow_slow_non_contiguous=True),
                 r=rr, w=[], dma=True, key=("st", rr[0].name))
        if final:
            outstores.append(o)
        return o

    def phase0():
        st = ExitStack()
        TOKT = Pool([sbp(st, "tokt%d" % i, [128, D]) for i in range(2)])
        FTT = Pool([sbp(st, "ftt%d" % i, [128, NCH, 128]) for i in range(2)])
        for tt in range(17):
            t0 = tt * 128
            n = min(128, T - t0)
            tk = TOKT.next()
            load(tk[0:n, :], x_tok[t0:t0 + n, :], [tk])
            ft = FTT.next()
            for g in range(4):
                pt = PS.next()
                for q in range(4):
                    c = g * 4 + q
                    pe(lambda e, pt=pt, q=q, tk=tk, c=c, n=n: e.transpose(
                        out=pt[:, q * 128:q * 128 + n], in_=tk[0:n, c * 128:(c + 1) * 128], identity=ident[0:n, 0:n]),
                        [tk, CST], [pt])
                dve(lambda e, pt=pt, ft=ft, g=g, n=n: e.tensor_copy(
                    out=ft[:, g * 4:(g + 1) * 4, 0:n], in_=pt[:, :].rearrange("p (q t) -> p q t", q=4)[:, :, 0:n]),
                    [pt], [ft])
            store(XT[0][:, t0:t0 + n].rearrange("(c p) t -> p c t", p=128), ft[:, :, 0:n], [ft],
                  [XR(0, j, min(tt // 4, 4)) for j in range(NCH)])
        P.flush()
        st.close()

    def norm_phase(b, l, vcol):
        st = ExitStack()
        XTL = sbp(st, "xtl", [128, NCH, 512])
        for ti, (t0, n) in enumerate(TT):
            load(XTL[:, :, 0:n], XT[b][:, t0:t0 + n].rearrange("(c p) t -> p c t", p=128), [XTL],
                 r=[XR(b, j, ti) for j in range(NCH)])
            pt = PS.next()
            for c in range(NCH):
                sq = B512.next()
                act(sq[:, 0:n], XTL[:, c, 0:n], AF.Square, [XTL], [sq])
                pe(lambda e, pt=pt, sq=sq, c=c, n=n: e.matmul(pt[:, 0:n], lhsT=ones_b, rhs=sq[:, 0:n],
                                                              start=(c == 0), stop=(c == NCH - 1)), [sq, CSTB], [pt])
            sd = W512.next()
            act(sd[:, 0:n], pt[:, 0:n], AF.Sqrt, [pt], [sd], bias=EPSC[:, 0:1], scale=1.0 / D)
            rstd = W512.next()
            dve(lambda e, rstd=rstd, sd=sd, n=n: e.reciprocal(out=rstd[:, 0:n], in_=sd[:, 0:n]), [sd], [rstd])
            for c in range(NCH):
                dve(lambda e, c=c, t0=t0, n=n, rstd=rstd: e.scalar_tensor_tensor(
                    out=ACT_T[:, c, t0:t0 + n], in0=XTL[:, c, 0:n], scalar=VEC[:, vcol + c:vcol + c + 1],
                    in1=rstd[:, 0:n], op0=ALU.mult, op1=ALU.mult), [XTL, rstd, VEC], [ACTR[ti]])
        P.flush()
        st.close()

    EPSC = sb("epsc", [128, 2])
    dve(lambda e: e.memset(EPSC[:, 0:1], 1e-6), [], [EPSC])
    dve(lambda e: e.memset(EPSC[:, 1:2], 1.0), [EPSC], [EPSC])

    def proj_chunk(wsrc2d, consume, nk=NCH):
        s, v = wload(wsrc2d, nk, 128)
        for ti, (t0, n) in enumerate(TT):
            pt = PS.next()
            mm_acc(pt[:, 0:n], [(v[:, k, :], ACT_T[:, k, t0:t0 + n]) for k in range(nk)], [s, ACTR[ti]], [pt])
            consume(ti, t0, n, pt)

    def evac_to_convbuf(RXB):
        def consume(ti, t0, n, pt):
            if ti < 4:
                act(RXB[:, 3 + t0:3 + t0 + n], pt[:, 0:n], AF.Copy, [pt], [RXB])
            else:
                act(RXB[:, 2051:2121].rearrange("p (s c) -> p s c", c=35)[:, :, 3:35],
                    pt[:, 0:64].rearrange("p (s c) -> p s c", c=32), AF.Copy, [pt], [RXB])
        return consume

    def conv_tile(RXB, wcol0, wstride, ti, t0, n, bias_ap=None, pool=None):
        o = (pool or W512).next()
        if ti < 4:
            src = lambda j: RXB[:, t0 + j:t0 + j + n]
            dst = o[:, 0:n]
        else:
            src = lambda j: RXB[:, 2051:2121].rearrange("p (s c) -> p s c", c=35)[:, :, j:j + 32]
            dst = o[:, 0:64].rearrange("p (s c) -> p s c", c=32)
        w = lambda j: VEC[:, wcol0 + j * wstride:wcol0 + j * wstride + 1]
        if bias_ap is not None:
            dve(lambda e: e.tensor_scalar(out=dst, in0=src(0), scalar1=w(0), scalar2=bias_ap, op0=ALU.mult, op1=ALU.add),
                [RXB, VEC], [o])
        else:
            dve(lambda e: e.tensor_scalar(out=dst, in0=src(0), scalar1=w(0), scalar2=None, op0=ALU.mult), [RXB, VEC], [o])
        for j in (1, 2, 3):
            dve(lambda e, j=j: e.scalar_tensor_tensor(out=dst, in0=src(j), scalar=w(j), in1=dst, op0=ALU.mult, op1=ALU.add),
                [RXB, VEC, o], [o])
        return o

    def conv_state_io(RXB, l, st_in, o_out, c0):
        dve(lambda e: e.memset(RXB[:, 0:3], 0.0), [], [RXB])
        for s in range(2):
            cb = SEQS[s + 1][2]
            slow_load(RXB[:, cb:cb + 3], st_in[l, s, :, c0:c0 + 128].rearrange("k p -> p k"), [RXB])

    def conv_state_out(RXB, l, o_out, c0):
        for s, (ts, ln, cb) in enumerate(SEQS):
            slow_store(o_out[l, s, :, c0:c0 + 128].rearrange("k p -> p k"), RXB[:, cb + ln:cb + ln + 3], [RXB])

    def mixer_a(l):
        st = ExitStack()
        mkws(st, 3, 2048)
        RXB2 = [sbp(st, "rxb%d" % i_, [128, CB]) for i_ in range(2)]
        HST = sbp(st, "hst", [128, 2, 8])
        CC = sbp(st, "cc", [128, 8])
        BDP = Pool([sbp(st, "bd%d" % i, [128, 2, 128], BF16) for i in range(3)])
        OAP = Pool([sbp(st, "oa%d" % i, [128, T], BF16) for i in range(2)])
        GLB2 = [sbp(st, "glb%d" % i_, [128, T]) for i_ in range(2)]
        HL = sbp(st, "hl", [128, 8, 3])
        XA5 = [sbp(st, "xa5%d" % i_, [128, 512]) for i_ in range(5)]
        XAB5 = [sbp(st, "xab5%d" % i_, [128, 512], BF16) for i_ in range(5)]
        RR5 = [sbp(st, "rr5%d" % i_, [128, 512]) for i_ in range(5)]
        GI5 = [sbp(st, "gi5%d" % i_, [128, 512]) for i_ in range(5)]
        AA5 = [sbp(st, "aa5%d" % i_, [128, 512]) for i_ in range(5)]
        BB5 = [sbp(st, "bb5%d" % i_, [128, 512]) for i_ in range(5)]
        HHP = Pool([sbp(st, "hhp%d" % i_, [128, 512]) for i_ in range(3)])
        for s in range(2):
            slow_load(HST[:, s, :], st_lru_h[l, s, :].rearrange("(c p) -> p c", p=128), [HST])
        act(CC[:, :], VEC[:, V_LAM:V_LAM + 8], AF.Exp, [VEC], [CC], scale=-1.0)
        act(CC[:, :], CC[:, :], AF.Ln, [CC], [CC], bias=EPSC[:, 1:2], scale=1.0)
        dve(lambda e: e.tensor_scalar(out=CC[:, :], in0=CC[:, :], scalar1=-8.0, scalar2=None, op0=ALU.mult), [CC], [CC])
        bds = {}

        def a_proj(j):
            RXB = RXB2[j % 2]
            GLB = GLB2[j % 2]
            bd = BDP.next()
            bds[j] = bd
            load(bd[:, :, :], lru_bd[l, :, j, :, :].rearrange("g p n -> p g n"), [bd], eng="pool")
            conv_state_io(RXB, l, st_conv_a, o_conv_a, j * 128)
            proj_chunk(w_in[l, :, j * 128:(j + 1) * 128], evac_to_convbuf(RXB))
            conv_state_out(RXB, l, o_conv_a, j * 128)

            def gelu_consume(ti, t0, n, pt):
                rg = W512.next()
                act(rg[:, 0:n], pt[:, 0:n], AF.Copy, [pt], [rg])
                t1 = W512.next()
                dve(lambda e: e.tensor_tensor(out=t1[:, 0:n], in0=rg[:, 0:n], in1=rg[:, 0:n], op=ALU.mult), [rg], [t1])
                dve(lambda e: e.tensor_scalar(out=t1[:, 0:n], in0=t1[:, 0:n], scalar1=0.044715, scalar2=1.0,
                                              op0=ALU.mult, op1=ALU.add), [t1], [t1])
                dve(lambda e: e.tensor_tensor(out=t1[:, 0:n], in0=t1[:, 0:n], in1=rg[:, 0:n], op=ALU.mult), [t1, rg], [t1])
                act(t1[:, 0:n], t1[:, 0:n], AF.Sigmoid, [t1], [t1], scale=GELU_K)
                dve(lambda e: e.tensor_tensor(out=GLB[:, t0:t0 + n], in0=t1[:, 0:n], in1=rg[:, 0:n], op=ALU.mult),
                    [t1, rg], [GLB])
            proj_chunk(w_in[l, :, 1024 + j * 128:1024 + (j + 1) * 128], gelu_consume)

        def a_tiles(j):
            RXB = RXB2[j % 2]
            GLB = GLB2[j % 2]
            bd = bds[j]
            oa = OAP.next()
            hprev = None
            xap = Pool(XA5)
            NT = len(TT)
            for ti, (t0, n) in enumerate(TT):
                xa = conv_tile(RXB, V_WCA + j, 8, ti, t0, n, bias_ap=VEC[:, V_BCA + j:V_BCA + j + 1], pool=xap)
                act(XAB5[ti][:, 0:n], xa[:, 0:n], AF.Copy, [xa], [XAB5[ti]])
            for ti, (t0, n) in enumerate(TT):
                pr = PS.next()
                pe(lambda e, pr=pr, ti=ti, n=n, bd=bd: e.matmul(pr[:, 0:n], lhsT=bd[:, 0, :], rhs=XAB5[ti][:, 0:n], start=True, stop=True), [bd, XAB5[ti]], [pr])
                pi = PS.next()
                pe(lambda e, pi=pi, ti=ti, n=n, bd=bd: e.matmul(pi[:, 0:n], lhsT=bd[:, 1, :], rhs=XAB5[ti][:, 0:n], start=True, stop=True), [bd, XAB5[ti]], [pi])
                act(RR5[ti][:, 0:n], pr[:, 0:n], AF.Sigmoid, [pr, VEC], [RR5[ti]], bias=VEC[:, V_BLR + j:V_BLR + j + 1])
                act(GI5[ti][:, 0:n], pi[:, 0:n], AF.Sigmoid, [pi, VEC], [GI5[ti]], bias=VEC[:, V_BLI + j:V_BLI + j + 1])
            for ti, (t0, n) in enumerate(TT):
                act(AA5[ti][:, 0:n], RR5[ti][:, 0:n], AF.Exp, [RR5[ti], CC], [AA5[ti]], scale=CC[:, j:j + 1])
            for ti, (t0, n) in enumerate(TT):
                dve(lambda e, ti=ti, n=n: e.tensor_tensor(out=BB5[ti][:, 0:n], in0=AA5[ti][:, 0:n], in1=AA5[ti][:, 0:n], op=ALU.mult), [AA5[ti]], [BB5[ti]])
                dve(lambda e, ti=ti, n=n: e.tensor_scalar(out=BB5[ti][:, 0:n], in0=BB5[ti][:, 0:n], scalar1=-1.0, scalar2=1.0, op0=ALU.mult, op1=ALU.add),
                    [BB5[ti]], [BB5[ti]])
            for ti, (t0, n) in enumerate(TT):
                act(BB5[ti][:, 0:n], BB5[ti][:, 0:n], AF.Sqrt, [BB5[ti]], [BB5[ti]])
            for ti, (t0, n) in enumerate(TT):
                dve(lambda e, ti=ti, n=n: e.tensor_tensor(out=BB5[ti][:, 0:n], in0=BB5[ti][:, 0:n], in1=GI5[ti][:, 0:n], op=ALU.mult), [BB5[ti], GI5[ti]], [BB5[ti]])
                dve(lambda e, ti=ti, n=n: e.tensor_tensor(out=BB5[ti][:, 0:n], in0=BB5[ti][:, 0:n], in1=XA5[ti][:, 0:n], op=ALU.mult), [BB5[ti], XA5[ti]], [BB5[ti]])
            pend = {ti: (AA5[ti], BB5[ti]) for ti in range(NT)}

            def a_scan(ti, t0, n, hprev, j=j, oa=oa):
                aa, bb = pend.pop(ti)
                hh = HHP.next()
                if ti < 4:
                    init = 0.0 if ti == 0 else hprev[:, 511:512]
                    rd = [aa, bb] + ([hprev] if ti > 0 else [])
                    dve(lambda e: e.tensor_tensor_scan(out=hh[:, 0:n], data0=aa[:, 0:n], data1=bb[:, 0:n], initial=init,
                                                       op0=ALU.mult, op1=ALU.add), rd, [hh])
                    if ti == 3:
                        dve(lambda e: e.tensor_copy(out=HL[:, j, 0:1], in_=hh[:, 511:512]), [hh], [HL])
                else:
                    for s in range(2):
                        dve(lambda e, s=s: e.tensor_tensor_scan(
                            out=hh[:, s * 32:(s + 1) * 32], data0=aa[:, s * 32:(s + 1) * 32], data1=bb[:, s * 32:(s + 1) * 32],
                            initial=HST[:, s, j:j + 1], op0=ALU.mult, op1=ALU.add), [aa, bb, HST], [hh])
                        dve(lambda e, s=s: e.tensor_copy(out=HL[:, j, 1 + s:2 + s], in_=hh[:, s * 32 + 31:s * 32 + 32]), [hh], [HL])
                dve(lambda e: e.tensor_tensor(out=oa[:, t0:t0 + n], in0=hh[:, 0:n], in1=GLB[:, t0:t0 + n], op=ALU.mult), [hh, GLB], [oa])
                return hh
            for ti, (t0, n) in enumerate(TT):
                hprev = a_scan(ti, t0, n, hprev)
            store(OUTS[j * 128:(j + 1) * 128, :], oa[:, :], [oa], [R("outs", j)])
        a_proj(0)
        for j in range(8):
            if j + 1 < 8:
                a_proj(j + 1)
            a_tiles(j)
        for s in range(3):
            slow_store(o_lru_h[l, s, :].rearrange("(c p) -> p c", p=128), HL[:, :, s], [HL])
        P.flush()
        st.close()

    MRGD = dscr("mrg_t", [D, T], BF16)

    def load_act(src):
        for ti, (t0, n) in enumerate(TT):
            load(ACT_T[:, :, t0:t0 + n], src[:, t0:t0 + n].rearrange("(c p) t -> p c t", p=128), [ACTR[ti]],
                 r=[R("mrg", j, ti) for j in range(NCH)], key=("ld", "act_t"))

    def resid_consume(src_b, dst_b, j, extra_fn=None):
        def consume(ti, t0, n, pt):
            xt = W512.next()
            load(xt[:, 0:n], XT[src_b][j * 128:(j + 1) * 128, t0:t0 + n], [xt], r=[XR(src_b, j, ti)])
            o = W512.next()
            dve(lambda e: e.tensor_tensor(out=o[:, 0:n], in0=pt[:, 0:n], in1=xt[:, 0:n], op=ALU.add), [pt, xt], [o])
            store(XT[dst_b][j * 128:(j + 1) * 128, t0:t0 + n], o[:, 0:n], [o], [XR(dst_b, j, ti)])
        return consume

    def merge_phase(l, cur, nxt):
        st = ExitStack()
        mkws(st, 4, 6144)
        OTP = Pool([sbp(st, "ot%d" % i, [128, 24, 512], BF16) for i in range(2)])
        MJP = Pool([sbp(st, "mj%d" % i, [128, T], BF16) for i in range(2)])
        wg3 = w_gate[l].rearrange("(k p) (n c) -> p k n c", p=128, n=3)
        wb3 = w_bo[l].rearrange("(n k p) c -> p n k c", n=3, p=128)
        for j in range(NCH):
            sg = WSH[0].next()
            vg = sg[:, 0:6144].rearrange("p (k n c) -> p k n c", k=16, n=3)
            for nb in range(3):
                load(vg[:, :, nb, :], wg3[:, :, nb, j * 128:(j + 1) * 128], [sg], eng="pool")
            sbo = WSH[0].next()
            vb = sbo[:, 0:3072].rearrange("p (n k c) -> p n k c", n=3, k=8)
            load(vb, wb3[:, :, :, j * 128:(j + 1) * 128], [sbo], eng="pool")
            mj = MJP.next()
            for ti, (t0, n) in enumerate(TT):
                ot = OTP.next()
                load(ot[:, :, 0:n], OUTS[:, t0:t0 + n].rearrange("(c p) t -> p c t", p=128), [ot],
                     r=[R("outs", c) for c in range(24)])
                mrg = W512.next()
                for nb in range(3):
                    pg = PS.next()
                    mm_acc(pg[:, 0:n], [(vg[:, k, nb, :], ACT_T[:, k, t0:t0 + n]) for k in range(16)], [sg, ACTR[ti]], [pg])
                    gs = W512.next()
                    act(gs[:, 0:n], pg[:, 0:n], AF.Sigmoid, [pg, VEC], [gs],
                        bias=VEC[:, V_BGATE + nb * 16 + j:V_BGATE + nb * 16 + j + 1])
                    pb = PS.next()
                    mm_acc(pb[:, 0:n], [(vb[:, nb, k, :], ot[:, nb * 8 + k, 0:n]) for k in range(8)], [sbo, ot], [pb])
                    if nb == 0:
                        dve(lambda e, mrg=mrg, gs=gs, pb=pb, n=n: e.tensor_tensor(out=mrg[:, 0:n], in0=pb[:, 0:n], in1=gs[:, 0:n], op=ALU.mult),
                            [pb, gs], [mrg])
                    else:
                        dve(lambda e, gs=gs, pb=pb, n=n: e.tensor_tensor(out=gs[:, 0:n], in0=pb[:, 0:n], in1=gs[:, 0:n], op=ALU.mult),
                            [pb, gs], [gs])
                        dve(lambda e, mrg=mrg, gs=gs, n=n: e.tensor_tensor(out=mrg[:, 0:n], in0=mrg[:, 0:n], in1=gs[:, 0:n], op=ALU.add),
                            [mrg, gs], [mrg])
                act(mj[:, t0:t0 + n], mrg[:, 0:n], AF.Copy, [mrg], [mj])
            store(MRGD[j * 128:(j + 1) * 128, :], mj[:, :], [mj], [R("mrg", j, ti) for ti in range(5)])
        P.flush()
        st.close()

    def ffn_phase(l, cur, nxt):
        st = ExitStack()
        mkws(st, 6, 2048)
        HJP = Pool([sbp(st, "hj%d" % i, [128, T], BF16) for i in range(2)])
        for j in range(NFF):
            s1, v1 = wload(w_fg[l, :, j * 128:(j + 1) * 128], 16, 128)
            s2, v2 = wload(w_fu[l, :, j * 128:(j + 1) * 128], 16, 128)
            hj = HJP.next()
            for ti, (t0, n) in enumerate(TT):
                pg = PS.next()
                mm_acc(pg[:, 0:n], [(v1[:, k, :], ACT_T[:, k, t0:t0 + n]) for k in range(16)], [s1, ACTR[ti]], [pg])
                pu = PS.next()
                mm_acc(pu[:, 0:n], [(v2[:, k, :], ACT_T[:, k, t0:t0 + n]) for k in range(16)], [s2, ACTR[ti]], [pu])
                sgl = W512.next()
                act(sgl[:, 0:n], pg[:, 0:n], AF.Silu, [pg], [sgl])
                dve(lambda e, hj=hj, sgl=sgl, pu=pu, t0=t0, n=n: e.tensor_tensor(out=hj[:, t0:t0 + n], in0=pu[:, 0:n], in1=sgl[:, 0:n], op=ALU.mult),
                    [pu, sgl], [hj])
            store(HID[j * 128:(j + 1) * 128, :], hj[:, :], [hj], [R("hid", j)])
        P.flush()
        st.close()

    def ple_phase(l, cur, nxt):
        st = ExitStack()
        mkws(st, 4, 2048)
        PET = sbp(st, "pet", [128, 2, T], BF16)
        PTK = Pool([sbp(st, "ptk%d" % i, [128, PLE]) for i in range(2)])
        for tt in range(17):
            t0 = tt * 128
            n = min(128, T - t0)
            tk = PTK.next()
            load(tk[0:n, :], pe_tok[l, t0:t0 + n, :], [tk])
            pt = PS.next()
            for q in range(2):
                pe(lambda e, pt=pt, q=q, tk=tk, n=n: e.transpose(out=pt[:, q * 128:q * 128 + n], in_=tk[0:n, q * 128:(q + 1) * 128],
                                                                 identity=ident[0:n, 0:n]), [tk, CST], [pt])
            dve(lambda e, pt=pt, t0=t0, n=n: e.tensor_copy(out=PET[:, :, t0:t0 + n],
                                                           in_=pt[:, 0:256].rearrange("p (q t) -> p q t", q=2)[:, :, 0:n]), [pt], [PET])
        for j in range(NCH):
            s1, v1 = wload(w_pg[l, :, j * 128:(j + 1) * 128], 16, 128)
            s2, v2 = wload(w_pp[l, :, j * 128:(j + 1) * 128], 2, 128)
            for ti, (t0, n) in enumerate(TT):
                pg = PS.next()
                mm_acc(pg[:, 0:n], [(v1[:, k, :], ACT_T[:, k, t0:t0 + n]) for k in range(16)], [s1, ACTR[ti]], [pg])
                pp = PS.next()
                mm_acc(pp[:, 0:n], [(v2[:, k, :], PET[:, k, t0:t0 + n]) for k in range(2)], [s2, PET], [pp])
                sg = W512.next()
                act(sg[:, 0:n], pg[:, 0:n], AF.Sigmoid, [pg], [sg])
                xt = W512.next()
                load(xt[:, 0:n], XT[cur][j * 128:(j + 1) * 128, t0:t0 + n], [xt], r=[XR(cur, j, ti)])
                dve(lambda e, sg=sg, pp=pp, n=n: e.tensor_tensor(out=sg[:, 0:n], in0=pp[:, 0:n], in1=sg[:, 0:n], op=ALU.mult), [pp, sg], [sg])
                o = W512.next()
                dve(lambda e, o=o, sg=sg, xt=xt, n=n: e.tensor_tensor(out=o[:, 0:n], in0=sg[:, 0:n], in1=xt[:, 0:n], op=ALU.add), [sg, xt], [o])
                store(XT[nxt][j * 128:(j + 1) * 128, t0:t0 + n], o[:, 0:n], [o], [XR(nxt, j, ti)])
        P.flush()
        st.close()

    VECN = sb("vecn", [128, 16])

    def norm_tile(xt, xtr, ti, t0, n, gain):
        pt = PS.next()
        for c in range(NCH):
            sq = B512.next()
            act(sq[:, 0:n], xt[:, c, 0:n], AF.Square, [xtr[c]], [sq])
            pe(lambda e, pt=pt, sq=sq, c=c: e.matmul(pt[:, 0:n], lhsT=ones_b, rhs=sq[:, 0:n], start=(c == 0), stop=(c == NCH - 1)),
               [sq, CSTB], [pt])
        sd = W512.next()
        act(sd[:, 0:n], pt[:, 0:n], AF.Sqrt, [pt], [sd], bias=EPSC[:, 0:1], scale=1.0 / D)
        dve(lambda e: e.reciprocal(out=sd[:, 0:n], in_=sd[:, 0:n]), [sd], [sd])
        for c in range(NCH):
            dve(lambda e, c=c: e.scalar_tensor_tensor(out=ACT_T[:, c, t0:t0 + n], in0=xt[:, c, 0:n], scalar=gain(c), in1=sd[:, 0:n],
                                                      op0=ALU.mult, op1=ALU.mult), [xtr[c], sd, VEC, VECN], [ACTR[ti]])

    def fused_tiles(st, cur, nxt, gain, tile_fn, nb=2):
        XTP = [sbp(st, "xtp%d" % i_, [128, NCH, 512]) for i_ in range(nb)]
        XTr = [[Res("xtp%d_%d" % (i_, c)) for c in range(NCH)] for i_ in range(nb)]
        for ti, (t0, n) in enumerate(TT):
            xt = XTP[ti % nb]
            xtr = XTr[ti % nb]
            for j in range(NCH):
                src = tile_fn(ti, t0, n, j)
                xin = W512.next()
                load(xin[:, 0:n], XT[cur][j * 128:(j + 1) * 128, t0:t0 + n], [xin], r=[XR(cur, j, ti)])
                dve(lambda e, src=src, xin=xin, xt=xt, j=j, n=n: e.tensor_tensor(out=xt[:, j, 0:n], in0=src[0], in1=xin[:, 0:n], op=ALU.add),
                    list(src[1]) + [xin], [xtr[j]])
                store(XT[nxt][j * 128:(j + 1) * 128, t0:t0 + n], xt[:, j, 0:n], [xtr[j]], [XR(nxt, j, ti)], key=("st", "xtp%d" % (ti % nb)))
            if gain is not None:
                norm_tile(xt, xtr, ti, t0, n, gain)

    def wout_norm_phase(l, cur, nxt):
        st = ExitStack()
        mkws(st, 4, 2048)
        load_act(MRGD)

        def tile_fn(ti, t0, n, j):
            s, v = wload(w_out[l, :, j * 128:(j + 1) * 128], NCH, 128)
            pt = PS.next()
            mm_acc(pt[:, 0:n], [(v[:, k, :], ACT_T[:, k, t0:t0 + n]) for k in range(NCH)], [s, ACTR[ti]], [pt])
            return (pt[:, 0:n], [pt])
        fused_tiles(st, cur, nxt, lambda c: VEC[:, V_NFFN + c:V_NFFN + c + 1], tile_fn)
        P.flush()
        st.close()

    def ffn_down_norm_phase(l, cur, nxt):
        st = ExitStack()
        mkws(st, 3, 5632)
        HT = sbp(st, "ht", [128, NFF, 512], BF16)

        def tile_fn(ti, t0, n, j):
            if j == 0:
                load(HT[:, :, 0:n], HID[:, t0:t0 + n].rearrange("(c p) t -> p c t", p=128), [HT], r=[R("hid", jj) for jj in range(NFF)])
            s, v = wload(w_fd[l, :, j * 128:(j + 1) * 128], NFF, 128)
            pt = PS.next()
            mm_acc(pt[:, 0:n], [(v[:, k, :], HT[:, k, 0:n]) for k in range(NFF)], [s, HT], [pt])
            return (pt[:, 0:n], [pt])
        fused_tiles(st, cur, nxt, lambda c: VEC[:, V_NPLE + c:V_NPLE + c + 1], tile_fn, nb=1)
        P.flush()
        st.close()

    def ple_norm_phase(l, cur, nxt, next_norm):
        st = ExitStack()
        mkws(st, 4, 2048)
        WPP = sbp(st, "wpp", [128, 2, D], BF16)
        load(WPP[:, :, :], w_pp[l].rearrange("(k p) n -> p k n", p=128), [WPP], eng="pool")
        PET = sbp(st, "pet", [128, 2, T], BF16)
        PTK = Pool([sbp(st, "ptk%d" % i, [128, PLE]) for i in range(2)])
        if next_norm:
            load(VECN[:, :], vecs_d[l + 1, :, V_NMIX:V_NMIX + 16], [VECN])
        for tt in range(17):
            t0 = tt * 128
            n = min(128, T - t0)
            tk = PTK.next()
            load(tk[0:n, :], pe_tok[l, t0:t0 + n, :], [tk])
            pt = PS.next()
            for q in range(2):
                pe(lambda e, pt=pt, q=q, tk=tk, n=n: e.transpose(out=pt[:, q * 128:q * 128 + n], in_=tk[0:n, q * 128:(q + 1) * 128],
                                                                 identity=ident[0:n, 0:n]), [tk, CST], [pt])
            dve(lambda e, pt=pt, t0=t0, n=n: e.tensor_copy(out=PET[:, :, t0:t0 + n],
                                                           in_=pt[:, 0:256].rearrange("p (q t) -> p q t", q=2)[:, :, 0:n]), [pt], [PET])

        def tile_fn(ti, t0, n, j):
            s1, v1 = wload(w_pg[l, :, j * 128:(j + 1) * 128], 16, 128)
            pg = PS.next()
            mm_acc(pg[:, 0:n], [(v1[:, k, :], ACT_T[:, k, t0:t0 + n]) for k in range(16)], [s1, ACTR[ti]], [pg])
            pp = PS.next()
            mm_acc(pp[:, 0:n], [(WPP[:, k, j * 128:(j + 1) * 128], PET[:, k, t0:t0 + n]) for k in range(2)], [WPP, PET], [pp])
            sg = W512.next()
            act(sg[:, 0:n], pg[:, 0:n], AF.Sigmoid, [pg], [sg])
            dve(lambda e: e.tensor_tensor(out=sg[:, 0:n], in0=pp[:, 0:n], in1=sg[:, 0:n], op=ALU.mult), [pp, sg], [sg])
            return (sg[:, 0:n], [sg])
        fused_tiles(st, cur, nxt, (lambda c: VECN[:, c:c + 1]) if next_norm else None, tile_fn)
        P.flush()
        st.close()

    def final_phase(b):
        st = ExitStack()
        FIN = Pool([sbp(st, "fin%d" % i, [128, NCH, 128]) for i in range(2)])
        TOUT = Pool([sbp(st, "tout%d" % i, [128, D]) for i in range(2)])
        for tt in range(17):
            t0 = tt * 128
            n = min(128, T - t0)
            fi = FIN.next()
            load(fi[:, :, 0:n], XT[b][:, t0:t0 + n].rearrange("(c p) t -> p c t", p=128), [fi],
                 r=[XR(b, j, min(tt // 4, 4)) for j in range(NCH)])
            to = TOUT.next()
            for g in range(4):
                pt = PS.next()
                for q in range(4):
                    c = g * 4 + q
                    pe(lambda e, pt=pt, q=q, fi=fi, c=c, n=n: e.transpose(out=pt[0:n, q * 128:(q + 1) * 128], in_=fi[:, c, 0:n],
                                                                          identity=ident), [fi, CST], [pt])
                dve(lambda e, pt=pt, to=to, g=g, n=n: e.tensor_copy(out=to[0:n, g * 512:(g + 1) * 512], in_=pt[0:n, :]), [pt], [to])
            if tt < 16:
                store(y_p[t0:t0 + n, :], to[0:n, :], [to], final=True)
            else:
                store(y_s[:, :], to[0:n, :], [to], final=True)
        P.flush()
        st.close()

    SCL = 128.0 ** -0.5

    def headnorm_consume(dst_bf, gcol, raw_keep=None):
        def consume(ti, t0, n, pt):
            raw = W512.next()
            act(raw[:, 0:n], pt[:, 0:n], AF.Copy, [pt], [raw])
            sq = B512.next()
            act(sq[:, 0:n], pt[:, 0:n], AF.Square, [pt], [sq])
            p2 = PS.next()
            pe(lambda e: e.matmul(p2[:, 0:n], lhsT=ones_b, rhs=sq[:, 0:n], start=True, stop=True), [sq, CSTB], [p2])
            sd = W512.next()
            act(sd[:, 0:n], p2[:, 0:n], AF.Sqrt, [p2], [sd], bias=EPSC[:, 0:1], scale=1.0 / 128)
            dve(lambda e: e.reciprocal(out=sd[:, 0:n], in_=sd[:, 0:n]), [sd], [sd])
            if raw_keep is not None:
                dve(lambda e: e.scalar_tensor_tensor(out=raw[:, 0:n], in0=raw[:, 0:n], scalar=VEC[:, gcol:gcol + 1], in1=sd[:, 0:n],
                                                     op0=ALU.mult, op1=ALU.mult), [raw, sd, VEC], [raw])
                dve(lambda e: e.tensor_copy(out=dst_bf[:, t0:t0 + n], in_=raw[:, 0:n]), [raw], [dst_bf])
                raw_keep(ti, t0, n, raw)
            else:
                dve(lambda e: e.scalar_tensor_tensor(out=dst_bf[:, t0:t0 + n], in0=raw[:, 0:n], scalar=VEC[:, gcol:gcol + 1], in1=sd[:, 0:n],
                                                     op0=ALU.mult, op1=ALU.mult), [raw, sd, VEC], [dst_bf])
        return consume

    def kv_out(l, h, o_p, o_s, KO):
        def keep(ti, t0, n, raw):
            if ti < 3:
                return
            pt = PS.next()
            nq = 4 if ti == 3 else 1
            w = 128 if ti == 3 else 64
            for q in range(nq):
                pe(lambda e, q=q: e.transpose(out=pt[0:w, q * 128:(q + 1) * 128], in_=raw[:, q * 128:q * 128 + w], identity=ident),
                   [raw, CST], [pt])
            ko = KO.next()
            dve(lambda e: e.tensor_copy(out=ko[0:w, 0:nq * 128], in_=pt[0:w, 0:nq * 128]), [pt], [ko])
            if ti == 3:
                store(o_p[l].rearrange("(a p) c -> p a c", p=128)[:, :, h * 128:(h + 1) * 128],
                      ko[:, :].rearrange("p (a c) -> p a c", a=4), [ko], final=True)
            else:
                store(o_s[l].rearrange("s t c -> (s t) c")[:, h * 128:(h + 1) * 128], ko[0:64, 0:128], [ko], final=True)
        return keep

    def mixer_c(l):
        st = ExitStack()
        mkws(st, 4, 2048)
        QNT = sbp(st, "qnt", [128, T], BF16)
        KNT = sbp(st, "knt", [128, T], BF16)
        VFT = sbp(st, "vft", [128, T], BF16)
        VTM = sbp(st, "vtm", [128, 17, 128], BF16)
        BP = sbp(st, "bp", [128, 5, 128])
        BSM = sbp(st, "bsm", [128, 5, 32])
        CKT = sbp(st, "ckt", [128, 2, 512], BF16)
        CKS = sbp(st, "cks", [128, 2, 4, 128], BF16)
        CVS = sbp(st, "cvs", [128, 2, 4, 128], BF16)
        OCP = Pool([sbp(st, "oc%d" % i, [128, T], BF16) for i in range(2)])
        KO = Pool([sbp(st, "ko%d" % i, [128, 512]) for i in range(3)])
        SCP = Pool([sbp(st, "sc%d" % i, [128, 640]) for i in range(2)])
        PTP = Pool([sbp(st, "ptp%d" % i, [128, 640], BF16) for i in range(2)])
        for h in range(8):
            load(BP[:, :, :], bias_p_d[l, :, h, :, :], [BP])
            load(BSM[:, :, :], bias_s_d[l, :, h, :, :], [BSM])
            for s in range(2):
                load(CKS[:, s, :, :], cache_k[l, s].rearrange("(a p) c -> p a c", p=128)[:, :, h * 128:(h + 1) * 128], [CKS], eng="pool")
                load(CVS[:, s, :, :], cache_v[l, s].rearrange("(a p) c -> p a c", p=128)[:, :, h * 128:(h + 1) * 128], [CVS], eng="pool")
            proj_chunk(w_in[l, :, 6160 + h * 128:6160 + (h + 1) * 128], headnorm_consume(QNT, V_QN))
            proj_chunk(w_in[l, :, 7184 + h * 128:7184 + (h + 1) * 128], headnorm_consume(KNT, V_KN, kv_out(l, h, o_pk, o_sk, KO)))
            vkeep = kv_out(l, h, o_pv, o_sv, KO)

            def vconsume(ti, t0, n, pt):
                raw = W512.next()
                act(raw[:, 0:n], pt[:, 0:n], AF.Copy, [pt], [raw])
                dve(lambda e: e.tensor_copy(out=VFT[:, t0:t0 + n], in_=raw[:, 0:n]), [raw], [VFT])
                vkeep(ti, t0, n, raw)
                pb = PSB.next()
                nq = 4 if ti < 4 else 1
                w = 128 if ti < 4 else 64
                for q in range(nq):
                    pe(lambda e, q=q: e.transpose(out=pb[0:w, q * 128:(q + 1) * 128], in_=VFT[:, t0 + q * 128:t0 + q * 128 + w], identity=ident_b),
                       [VFT, CSTB], [pb])
                dve(lambda e: e.tensor_copy(out=VTM[0:w, 4 * ti:4 * ti + nq, :], in_=pb[0:w, 0:nq * 128].rearrange("p (a c) -> p a c", a=nq)),
                    [pb], [VTM])
            proj_chunk(w_in[l, :, 8208 + h * 128:8208 + (h + 1) * 128], vconsume)
            for s in range(2):
                pb = PSB.next()
                for a in range(4):
                    pe(lambda e, pb=pb, s=s, a=a: e.transpose(out=pb[:, a * 128:(a + 1) * 128], in_=CKS[:, s, a, :], identity=ident_b),
                       [CKS, CSTB], [pb])
                dve(lambda e, pb=pb, s=s: e.tensor_copy(out=CKT[:, s, :], in_=pb[:, 0:512]), [pb], [CKT])
            oc = OCP.next()
            ptts = {}

            def att_A(m):
                i0 = max(0, 4 - m)
                pa = PS.next()
                pbk = PS.next()
                for i in range(i0, 5):
                    kt = m - 4 + i
                    dstp = pa[:, i * 128:(i + 1) * 128] if i < 4 else pbk[:, 0:128]
                    pe(lambda e, dstp=dstp, kt=kt, m=m: e.matmul(dstp, lhsT=KNT[:, kt * 128:(kt + 1) * 128], rhs=QNT[:, m * 128:(m + 1) * 128],
                                                              start=True, stop=True), [KNT, QNT], [pa if i < 4 else pbk])
                sc = SCP.next()
                if i0 < 4:
                    dve(lambda e, sc=sc, pa=pa, i0=i0: e.scalar_tensor_tensor(
                        out=sc[:, i0 * 128:512], in0=pa[:, i0 * 128:512], scalar=SCL,
                        in1=BP[:, i0:4, :].rearrange("p a q -> p (a q)"), op0=ALU.mult, op1=ALU.add), [pa, BP], [sc])
                dve(lambda e, sc=sc, pbk=pbk: e.scalar_tensor_tensor(out=sc[:, 512:640], in0=pbk[:, 0:128], scalar=SCL, in1=BP[:, 4, :],
                                                                     op0=ALU.mult, op1=ALU.add), [pbk, BP], [sc])
                ptt = PTP.next()
                act(ptt[:, i0 * 128:640], sc[:, i0 * 128:640], AF.Exp, [sc], [ptt])
                ptts[m] = (ptt, i0)

            def att_B(m):
                ptt, i0 = ptts.pop(m)
                po = PS.next()
                mm_acc(po[:, 0:128], [(VTM[:, m - 4 + i, :], ptt[:, i * 128:(i + 1) * 128]) for i in range(i0, 5)], [VTM, ptt], [po])
                psm = PS.next()
                mm_acc(psm[:, 0:128], [(ones_b, ptt[:, i * 128:(i + 1) * 128]) for i in range(i0, 5)], [CSTB, ptt], [psm])
                rec = W512.next()
                dve(lambda e, rec=rec, psm=psm: e.reciprocal(out=rec[:, 0:128], in_=psm[:, 0:128]), [psm], [rec])
                dve(lambda e, oc=oc, po=po, rec=rec, m=m: e.tensor_tensor(out=oc[:, m * 128:(m + 1) * 128], in0=po[:, 0:128], in1=rec[:, 0:128],
                                                                          op=ALU.mult), [po, rec], [oc])
            att_A(0)
            for m in range(16):
                if m + 1 < 16:
                    att_A(m + 1)
                att_B(m)
            for s in range(2):
                tq = 2048 + 32 * s
                pa = PS.next()
                for a in range(4):
                    pe(lambda e, pa=pa, a=a, s=s, tq=tq: e.matmul(pa[:, a * 32:(a + 1) * 32], lhsT=CKT[:, s, a * 128:(a + 1) * 128],
                                                                  rhs=QNT[:, tq:tq + 32], start=True, stop=True), [CKT, QNT], [pa])
                pe(lambda e, pa=pa, s=s, tq=tq: e.matmul(pa[32 * s:32 * s + 32, 128:160], lhsT=KNT[:, tq:tq + 32], rhs=QNT[:, tq:tq + 32],
                                                         start=True, stop=True), [KNT, QNT], [pa])
                sc = SCP.next()
                dve(lambda e, sc=sc, pa=pa: e.scalar_tensor_tensor(out=sc[:, 0:128], in0=pa[:, 0:128], scalar=SCL,
                                                                   in1=BSM[:, 0:4, :].rearrange("p a q -> p (a q)"), op0=ALU.mult, op1=ALU.add),
                    [pa, BSM], [sc])
                dve(lambda e, sc=sc, pa=pa, s=s: e.scalar_tensor_tensor(out=sc[32 * s:32 * s + 32, 128:160], in0=pa[32 * s:32 * s + 32, 128:160],
                                                                        scalar=SCL, in1=BSM[32 * s:32 * s + 32, 4, :], op0=ALU.mult, op1=ALU.add),
                    [pa, BSM, sc], [sc])
                ptt = PTP.next()
                act(ptt[:, 0:128], sc[:, 0:128], AF.Exp, [sc], [ptt])
                act(ptt[32 * s:32 * s + 32, 128:160], sc[32 * s:32 * s + 32, 128:160], AF.Exp, [sc, ptt], [ptt])
                po = PS.next()
                prs = [(CVS[:, s, a, :], ptt[:, a * 32:(a + 1) * 32]) for a in range(4)]
                prs.append((VTM[32 * s:32 * s + 32, 16, :], ptt[32 * s:32 * s + 32, 128:160]))
                mm_acc(po[:, 0:32], prs, [CVS, VTM, ptt], [po])
                psm = PS.next()
                prs2 = [(ones_b, ptt[:, a * 32:(a + 1) * 32]) for a in range(4)]
                prs2.append((CSTB[32 * s:32 * s + 32, 128:256], ptt[32 * s:32 * s + 32, 128:160]))
                mm_acc(psm[:, 0:32], prs2, [CSTB, ptt], [psm])
                rec = W512.next()
                dve(lambda e, rec=rec, psm=psm: e.reciprocal(out=rec[:, 0:32], in_=psm[:, 0:32]), [psm], [rec])
                dve(lambda e, oc=oc, po=po, rec=rec, tq=tq: e.tensor_tensor(out=oc[:, tq:tq + 32], in0=po[:, 0:32], in1=rec[:, 0:32], op=ALU.mult),
                    [po, rec], [oc])
            store(OUTS[(16 + h) * 128:(17 + h) * 128, :], oc[:, :], [oc], [R("outs", 16 + h)])
        P.flush()
        st.close()

    DZS = dscr("dzs", [T, 1024])
    CHK = [(n * 64, 64, 0) for n in range(32)] + [(2048, 32, 1), (2080, 32, 2)]
    NCK = len(CHK)

    def mixer_b(l):
        st = ExitStack()
        B = PS.items
        BB = PSB.items
        RXB = sbp(st, "rxb", [128, CB])
        QNT = sbp(st, "qnt", [128, T], BF16)
        KNT = sbp(st, "knt", [128, T], BF16)
        VCB = sbp(st, "vcb", [128, T], BF16)
        OBT = sbp(st, "obt", [128, T], BF16)
        GBT = sbp(st, "gbt", [64, NCK, 24])
        BEG = sbp(st, "beg", [64, NCK, 8])
        EKD = sbp(st, "ekd", [64, NCK, 8])
        EGL = sbp(st, "egl", [128, NCK, 8])
        NEGA = sbp(st, "nega", [128, 8])
        SF = sbp(st, "sf", [128, 3, 128])
        SFr = [Res("sfr%d" % i) for i in range(3)]
        GSU = sbp(st, "gsu", [64, NCK * 16])
        GS = Tn(GSU.h[:, :].rearrange("p (c x) -> p c x", x=16), "gsu")
        GS.res = GSU.res
        GUW = Tn(GSU.h[:, 0:512].rearrange("p (w c) -> p w c", w=8), "gsu")
        GUW.res = GSU.res
        GBr = Res("gbr")

        st0 = ExitStack()
        mkws(st0, 3, 8192)
        act(NEGA[:, :], VEC[:, V_ALOG:V_ALOG + 8], AF.Exp, [VEC], [NEGA])
        dve(lambda e: e.tensor_scalar(out=NEGA[:, :], in0=NEGA[:, :], scalar1=-1.0, scalar2=None, op0=ALU.mult), [NEGA], [NEGA])
        sab, vab = wload(w_in[l, :, 6144:6160], 16, 16)
        GRP = [(0, 32, 64), (32, 2, 32)]
        for gi, (c0, ncx, C) in enumerate(GRP):
            bk = B[gi]
            for i in range(ncx):
                t0 = CHK[c0 + i][0]
                mm_acc(bk[0:C, i * 16:(i + 1) * 16], [(ACT_T[:, k, t0:t0 + C], vab[:, k, :]) for k in range(16)], [sab] + ACTR, [bk])
            pv3 = bk[0:C, 0:ncx * 16].rearrange("p (c x) -> p c x", x=16)
            tm = GS[0:C, c0:c0 + ncx, 0:8]
            dve(lambda e, tm=tm, pv3=pv3, C=C, ncx=ncx: e.tensor_tensor(
                out=tm, in0=pv3[:, :, 0:8], in1=VEC[0:C, V_DTB:V_DTB + 8].unsqueeze(1).broadcast_to([C, ncx, 8]), op=ALU.add),
                [bk, VEC], [GS])
            act(tm, tm, AF.Exp, [GS], [GS])
            act(tm, tm, AF.Ln, [GS], [GS], bias=EPSC[0:C, 1:2], scale=1.0)
            dve(lambda e, tm=tm, C=C, ncx=ncx, c0=c0: e.tensor_tensor(
                out=GBT[0:C, c0:c0 + ncx, 0:8], in0=tm, in1=NEGA[0:C, :].unsqueeze(1).broadcast_to([C, ncx, 8]), op=ALU.mult),
                [GS, NEGA], [GBr])
            act(GBT[0:C, c0:c0 + ncx, 8:16], pv3[:, :, 8:16], AF.Sigmoid, [bk], [GBr])
            dve(lambda e, C=C, ncx=ncx, c0=c0: e.tensor_scalar(out=GBT[0:C, c0:c0 + ncx, 16:24], in0=GBT[0:C, c0:c0 + ncx, 8:16],
                                                              scalar1=-1.0, scalar2=None, op0=ALU.mult), [GBr], [GBr])
        for gi, (c0, ncx, C) in enumerate(GRP):
            bk = B[2 + gi]
            bk2 = B[4 + gi]
            for i in range(ncx):
                ci = c0 + i
                pe(lambda e, bk=bk, C=C, ci=ci, i=i: e.matmul(bk[0:C, i * 16:i * 16 + 8], lhsT=CST[0:C, C_UPI:C_UPI + C], rhs=GBT[0:C, ci, 0:8],
                                                             start=True, stop=True), [CST, GBr], [bk])
                pe(lambda e, bk=bk, C=C, ci=ci, i=i: e.matmul(bk[0:C, i * 16 + 8:i * 16 + 16], lhsT=CST[0:C, C_ONE:C_ONE + C], rhs=GBT[0:C, ci, 0:8],
                                                             start=True, stop=True), [CST, GBr], [bk])
                pe(lambda e, bk2=bk2, C=C, ci=ci, i=i: e.matmul(bk2[:, i * 8:i * 8 + 8], lhsT=CST[0:C, C_ONE:C_ONE + 128], rhs=GBT[0:C, ci, 0:8],
                                                               start=True, stop=True), [CST, GBr], [bk2])
            gsv = GS[0:C, c0:c0 + ncx, :]
            act(gsv, bk[0:C, 0:ncx * 16].rearrange("p (c x) -> p c x", x=16), AF.Copy, [bk], [GS])
            act(EGL[:, c0:c0 + ncx, :], bk2[:, 0:ncx * 8].rearrange("p (c x) -> p c x", x=8), AF.Exp, [bk2], [GBr])
            dve(lambda e, gsv=gsv: e.tensor_tensor(out=gsv[:, :, 8:16], in0=gsv[:, :, 8:16], in1=gsv[:, :, 0:8], op=ALU.subtract), [GS], [GS])
            act(EKD[0:C, c0:c0 + ncx, :], gsv[:, :, 8:16], AF.Exp, [GS], [GBr])
            act(gsv[:, :, 0:8], gsv[:, :, 0:8], AF.Exp, [GS], [GS])
            dve(lambda e, gsv=gsv, C=C, ncx=ncx, c0=c0: e.tensor_tensor(out=BEG[0:C, c0:c0 + ncx, :], in0=gsv[:, :, 0:8],
                                                                        in1=GBT[0:C, c0:c0 + ncx, 8:16], op=ALU.mult), [GS, GBr], [GBr])
        wz = [wload(w_in[l, :, 5120 + hf * 512:5120 + (hf + 1) * 512], 16, 512) for hf in range(2)]
        for ci, (t0, C, sq) in enumerate(CHK):
            for hf in range(2):
                pt = PS.next()
                mm_acc(pt[0:C, :], [(ACT_T[:, k, t0:t0 + C], wz[hf][1][:, k, :]) for k in range(16)], [wz[hf][0]] + ACTR, [pt])
                zt = W512.next()
                act(zt[0:C, :], pt[0:C, :], AF.Silu, [pt], [zt])
                store(DZS[t0:t0 + C, hf * 512:(hf + 1) * 512], zt[0:C, :], [zt], [R("dzs", ci)])

        P.flush()
        st0.close()
        mkws(st, 4, 2048)
        CVP = Pool([sbp(st, "cvp%d" % i_, [128, 512]) for i_ in range(10)])
        EGBW = sbp(st, "egbw", [128, 8, 64], BF16)
        P0T = sbp(st, "p0t", [64, 8, 64])

        def dbl(name, shape, dt, nb=2, nr=4):
            t = [sbp(st, "%s%d" % (name, i), shape, dt) for i in range(nb)]
            r = [[Res("%s_%d_%d" % (name, i, q)) for q in range(nr)] for i in range(nb)]
            if nb == 1:
                t = t * 2
                r = r * 2
            return t, r
        EDW, EDr = dbl("edw", [64, 8, 128], F32, 1, 2)
        QGW, QGr = dbl("qgw", [128, 8, 64], BF16, 2, 1)
        QKTW, QKr = dbl("qktw", [64, 8, 64], BF16, 2, 2)
        KVW, KVr = dbl("kvw", [64, 8, 256], BF16, 1, 2)
        KDEW, KDr = dbl("kdew", [64, 8, 128], BF16, 1, 2)
        NWTW, NWr = dbl("nwtw", [128, 8, 64], BF16, 2, 1)
        NYW, NYr = dbl("nyw", [64, 8, 128], BF16, 1, 4)
        PBW, PBr = dbl("pbw", [64, 8, 64], BF16, 1, 4)
        WUW, WUr = dbl("wuw", [64, 8, 256], BF16, 2, 4)
        MTW, MTr = dbl("mtw", [128, 8, 128], BF16, 2, 4)
        BCW, BCr = dbl("bcw", [128, 8, 128], BF16, 2, 4)
        SBT = sbp(st, "sbt", [128, 2, 128], BF16)
        SBTr = [Res("sbtr0"), Res("sbtr1")]
        SM = Pool([sbp(st, "sm%d" % i, [64, 128]) for i in range(6)])
        SMB = Pool([sbp(st, "smb%d" % i, [64, 128], BF16) for i in range(6)])
        SC1 = Pool([sbp(st, "sc1%d" % i, [64, 2]) for i in range(4)])
        WAVES = [(w * 8, 8, 64) for w in range(4)] + [(32, 2, 32)]

        def do_head(h):
            def q_proj(which):
                c0 = which * 1024 + h * 128
                conv_state_io(RXB, l, st_conv_b, o_conv_b, c0)
                proj_chunk(w_in[l, :, 2048 + c0:2048 + c0 + 128], evac_to_convbuf(RXB))
                conv_state_out(RXB, l, o_conv_b, c0)

            def q_conv(which, dst):
                cvs = []
                for ti, (t0, n) in enumerate(TT):
                    cv = conv_tile(RXB, V_WCB + which * 8 + h, 24, ti, t0, n, pool=CVP)
                    cvs.append(cv)
                return cvs

            def q_tail(which, dst, cvs):
                for ti, (t0, n) in enumerate(TT):
                    cv = cvs[ti]
                    if which == 2:
                        act(dst[:, t0:t0 + n], cv[:, 0:n], AF.Silu, [cv], [dst])
                        continue
                    act(cv[:, 0:n], cv[:, 0:n], AF.Silu, [cv], [cv])
                    sq_ = B512.next()
                    act(sq_[:, 0:n], cv[:, 0:n], AF.Square, [cv], [sq_])
                    p2 = PS.next()
                    pe(lambda e, p2=p2, sq_=sq_, n=n: e.matmul(p2[:, 0:n], lhsT=ones_b, rhs=sq_[:, 0:n], start=True, stop=True), [sq_, CSTB], [p2])
                    sd = W512.next()
                    act(sd[:, 0:n], p2[:, 0:n], AF.Sqrt, [p2], [sd], bias=EPSC[:, 0:1], scale=1.0)
                    dve(lambda e, sd=sd, n=n: e.reciprocal(out=sd[:, 0:n], in_=sd[:, 0:n]), [sd], [sd])
                    dve(lambda e, sd=sd, cv=cv, t0=t0, n=n: e.scalar_tensor_tensor(
                        out=dst[:, t0:t0 + n], in0=cv[:, 0:n], scalar=(SCL if which == 0 else 1.0), in1=sd[:, 0:n],
                        op0=ALU.mult, op1=ALU.mult), [cv, sd], [dst])
            q_proj(0)
            cq = q_conv(0, QNT)
            q_proj(1)
            q_tail(0, QNT, cq)
            ck = q_conv(1, KNT)
            q_proj(2)
            q_tail(1, KNT, ck)
            cvv = q_conv(2, VCB)
            q_tail(2, VCB, cvv)
            for s in range(2):
                load(SF[:, 1 + s, :], st_S[l, s, h, :, :], [SFr[1 + s]], key=("ld", "sf%d" % s))
            dve(lambda e: e.memset(SF[:, 0, :], 0.0), [], [SFr[0]])
            dve(lambda e: e.memset(SBT[:, 0, :], 0.0), [], [SBTr[0]])

            def stage1(wi):
                c0, W, C = WAVES[wi]
                par = wi % 2
                tq0 = CHK[c0][0]
                steps = []
                bc = lambda ap, n_: ap.unsqueeze(1).broadcast_to([C, n_, C])
                NH = (W + 3) // 4
                hv = [(q * 4, min(4, W - q * 4)) for q in range(NH)]
                NP_ = (W + 1) // 2
                pr = [(p * 2, min(2, W - p * 2)) for p in range(NP_)]
                allr = lambda rl: list(rl)

                def v4(ap, n_):
                    return ap.rearrange("p (w a b) -> p w a b", w=n_, a=2)[:, :, :, 0:C]

                def sA():
                    dve(lambda e: e.tensor_tensor(out=GUW[0:C, 0:W, 0:C], in0=bc(CST[0:C, C_UPI:C_UPI + C], W),
                                                  in1=GBT[0:C, c0:c0 + W, h:h + 1].broadcast_to([C, W, C]), op=ALU.mult), [CST, GBr], [GUW])
                    for sl in range(W):
                        bk = B[sl // 4]
                        o = (sl % 4) * 128
                        pe(lambda e, bk=bk, o=o, sl=sl: e.matmul(bk[0:C, o:o + C], lhsT=CST[0:C, C_LOS:C_LOS + C], rhs=GUW[0:C, sl, 0:C],
                                                                  start=True, stop=True), [CST, GUW], [bk])
                        pe(lambda e, bk=bk, o=o, sl=sl: e.matmul(bk[0:C, o + 64:o + 64 + C], lhsT=GUW[0:C, sl, 0:C], rhs=CST[0:C, C_LOS:C_LOS + C],
                                                                  start=True, stop=True), [CST, GUW], [bk])
                        pe(lambda e, sl=sl: e.matmul(B[2][:, sl * 64:sl * 64 + C], lhsT=CST[0:C, C_ONE:C_ONE + 128], rhs=GUW[0:C, sl, 0:C],
                                                      start=True, stop=True), [CST, GUW], [B[2]])
                steps.append(sA)

                def sB():
                    for q, (s0, n_) in enumerate(hv):
                        edv = EDW[par][0:C, s0:s0 + n_, :].rearrange("p w (a b) -> p w a b", a=2)[:, :, :, 0:C]
                        act(edv, v4(B[q][0:C, 0:n_ * 128], n_), AF.Exp, [B[q]], [EDr[par][q]])
                    edv = EDW[par][0:C, 0:W, :].rearrange("p w (a b) -> p w a b", a=2)[:, :, :, 0:C]
                    dve(lambda e: e.tensor_tensor(
                        out=edv, in0=edv, in1=CST[0:C, C_UPI:C_UPI + 128].rearrange("p (a b) -> p a b", a=2)[:, :, 0:C]
                        .unsqueeze(1).broadcast_to([C, W, 2, C]), op=ALU.mult), allr(EDr[par]) + [CST], allr(EDr[par]))
                    egv = EGBW[:, 0:W, 0:C]
                    act(egv, B[2][:, 0:W * 64].rearrange("p (w c) -> p w c", w=W)[:, :, 0:C], AF.Exp, [B[2]], [EGBW])
                    dve(lambda e: e.tensor_tensor(out=QGW[par][:, 0:W, 0:C], in0=QNT[:, tq0:tq0 + W * C].rearrange("p (w c) -> p w c", w=W),
                                                  in1=egv, op=ALU.mult), [QNT, EGBW], [QGr[par][0]])
                    for sl in range(W):
                        t0 = tq0 + sl * C
                        bk = B[sl // 4]
                        o = (sl % 4) * 128
                        pe(lambda e, bk=bk, o=o, t0=t0: e.matmul(bk[0:C, o:o + C], lhsT=KNT[:, t0:t0 + C], rhs=QNT[:, t0:t0 + C], start=True, stop=True),
                           [KNT, QNT], [bk])
                        pe(lambda e, bk=bk, o=o, t0=t0: e.matmul(bk[0:C, o + 64:o + 64 + C], lhsT=KNT[:, t0:t0 + C], rhs=KNT[:, t0:t0 + C], start=True, stop=True),
                           [KNT], [bk])
                steps.append(sB)

                def sC():
                    for q, (s0, n_) in enumerate(hv):
                        kv = v4(B[q][0:C, 0:n_ * 128], n_)
                        edv = EDW[par][0:C, s0:s0 + n_, :].rearrange("p w (a b) -> p w a b", a=2)[:, :, :, 0:C]
                        dve(lambda e, kv=kv, edv=edv, s0=s0, n_=n_: e.tensor_tensor(out=QKTW[par][0:C, s0:s0 + n_, 0:C], in0=kv[:, :, 0, :], in1=edv[:, :, 0, :],
                                                                                  op=ALU.mult), [B[q], EDr[par][q]], [QKr[par][q]])
                        dve(lambda e, kv=kv, edv=edv, s0=s0, n_=n_: e.tensor_tensor(out=P0T[0:C, s0:s0 + n_, 0:C], in0=kv[:, :, 1, :], in1=edv[:, :, 1, :],
                                                                                  op=ALU.mult), [B[q], EDr[par][q]], [P0T])
                    dve(lambda e: e.tensor_tensor(out=PBW[par][0:C, 0:W, 0:C], in0=P0T[0:C, 0:W, 0:C],
                                                  in1=GBT[0:C, c0:c0 + W, 16 + h:17 + h].broadcast_to([C, W, C]), op=ALU.mult),
                        [P0T, GBr], allr(PBr[par]))
                    for sl in range(W):
                        pe(lambda e, sl=sl: e.transpose(out=BB[0][0:C, sl * 64:sl * 64 + C], in_=PBW[par][0:C, sl, 0:C], identity=ident_b[0:C, 0:C]),
                           [PBr[par][sl // 2], CSTB], [BB[0]])
                    for sl in range(min(4, W)):
                        t0 = tq0 + sl * C
                        o = sl * 256
                        pe(lambda e, o=o, t0=t0: e.transpose(out=BB[1][0:C, o:o + 128], in_=KNT[:, t0:t0 + C], identity=ident_b), [KNT, CSTB], [BB[1]])
                        pe(lambda e, o=o, t0=t0: e.transpose(out=BB[1][0:C, o + 128:o + 256], in_=VCB[:, t0:t0 + C], identity=ident_b), [VCB, CSTB], [BB[1]])
                steps.append(sC)

                def kvb(bk, s0, n_, q):
                    kvv = bk[0:C, 0:n_ * 256].rearrange("p (w a d) -> p w a d", w=n_, a=2)
                    dve(lambda e: e.tensor_tensor(out=KVW[par][0:C, s0:s0 + n_, 0:128], in0=kvv[:, :, 0, :],
                                                  in1=BEG[0:C, c0 + s0:c0 + s0 + n_, h:h + 1].broadcast_to([C, n_, 128]), op=ALU.mult), [bk, GBr], [KVr[par][q]])
                    dve(lambda e: e.tensor_tensor(out=KDEW[par][0:C, s0:s0 + n_, :], in0=kvv[:, :, 0, :],
                                                  in1=EKD[0:C, c0 + s0:c0 + s0 + n_, h:h + 1].broadcast_to([C, n_, 128]), op=ALU.mult), [bk, GBr], [KDr[par][q]])
                    dve(lambda e: e.tensor_tensor(out=KVW[par][0:C, s0:s0 + n_, 128:256], in0=kvv[:, :, 1, :],
                                                  in1=GBT[0:C, c0 + s0:c0 + s0 + n_, 8 + h:9 + h].broadcast_to([C, n_, 128]), op=ALU.mult), [bk, GBr], [KVr[par][q]])

                def sD():
                    nv = BB[0][0:C, 0:W * 64].rearrange("p (w c) -> p w c", w=W)[:, :, 0:C]
                    act(NYW[par][0:C, 0:W, 0:C], nv, AF.Copy, [BB[0]], allr(NYr[par]))
                    dve(lambda e: e.tensor_tensor(out=NYW[par][0:C, 0:W, 64:64 + C], in0=nv, in1=bc(ident_b[0:C, 0:C], W), op=ALU.add),
                        [BB[0], CSTB], allr(NYr[par]))
                    kvb(BB[1], 0, min(4, W), 0)
                    if W > 4:
                        for sl in range(4, W):
                            t0 = tq0 + sl * C
                            o = (sl - 4) * 256
                            pe(lambda e, o=o, t0=t0: e.transpose(out=BB[0][0:C, o:o + 128], in_=KNT[:, t0:t0 + C], identity=ident_b), [KNT, CSTB], [BB[0]])
                            pe(lambda e, o=o, t0=t0: e.transpose(out=BB[0][0:C, o + 128:o + 256], in_=VCB[:, t0:t0 + C], identity=ident_b), [VCB, CSTB], [BB[0]])
                        kvb(BB[0], 4, W - 4, 1)
                steps.append(sD)

                def level(lev):
                    def f():
                        for p_, (s0, n2) in enumerate(pr):
                            bk = B[p_ % 3]
                            for j2 in range(n2):
                                sl = s0 + j2
                                o = j2 * 192
                                pe(lambda e, bk=bk, o=o, sl=sl: e.matmul(bk[0:C, o:o + 128], lhsT=PBW[par][0:C, sl, 0:C], rhs=NYW[par][0:C, sl, :],
                                                                          start=True, stop=True), [PBr[par][p_], NYr[par][p_]], [bk])
                                pe(lambda e, bk=bk, o=o, sl=sl: e.matmul(bk[0:C, o + 128:o + 128 + C], lhsT=NYW[par][0:C, sl, 0:C], rhs=PBW[par][0:C, sl, 0:C],
                                                                          start=True, stop=True), [PBr[par][p_], NYr[par][p_]], [bk])
                            pvw = bk[0:C, 0:n2 * 192].rearrange("p (w x) -> p w x", w=n2)
                            if lev > 0:
                                dve(lambda e, pvw=pvw, s0=s0, n2=n2: e.tensor_tensor(out=NYW[par][0:C, s0:s0 + n2, 64:64 + C], in0=pvw[:, :, 64:64 + C],
                                                                                    in1=NYW[par][0:C, s0:s0 + n2, 64:64 + C], op=ALU.add),
                                    [bk, NYr[par][p_]], [NYr[par][p_]])
                            act(NYW[par][0:C, s0:s0 + n2, 0:C], pvw[:, :, 0:C], AF.Copy, [bk], [NYr[par][p_]])
                            act(PBW[par][0:C, s0:s0 + n2, 0:C], pvw[:, :, 128:128 + C], AF.Copy, [bk], [PBr[par][p_]])
                    return f
                for lev in range(6):
                    steps.append(level(lev))

                def sE():
                    for p_, (s0, n2) in enumerate(pr):
                        bk = B[p_ % 3]
                        for j2 in range(n2):
                            sl = s0 + j2
                            pe(lambda e, bk=bk, j2=j2, sl=sl: e.matmul(bk[0:C, j2 * 256:(j2 + 1) * 256], lhsT=NYW[par][0:C, sl, 64:64 + C], rhs=KVW[par][0:C, sl, :],
                                                                        start=True, stop=True), [NYr[par][p_], KVr[par][sl // 4]], [bk])
                        act(WUW[par][0:C, s0:s0 + n2, :], bk[0:C, 0:n2 * 256].rearrange("p (w x) -> p w x", w=n2), AF.Copy, [bk], [WUr[par][p_]])
                    bkw = B[NP_ % 3]
                    for sl in range(W):
                        pe(lambda e, sl=sl: e.matmul(bkw[:, sl * 64:sl * 64 + C], lhsT=KVW[par][0:C, sl, 0:128], rhs=NYW[par][0:C, sl, 64:64 + C],
                                                      start=True, stop=True), [KVr[par][sl // 4], NYr[par][sl // 2]], [bkw])
                    act(NWTW[par][:, 0:W, 0:C], bkw[:, 0:W * 64].rearrange("p (w c) -> p w c", w=W)[:, :, 0:C], AF.Copy, [bkw], [NWr[par][0]], scale=-1.0)
                steps.append(sE)

                def sF():
                    for p_, (s0, n2) in enumerate(pr):
                        bk = B[(p_ + NP_ + 1) % 3]
                        for j2 in range(n2):
                            sl = s0 + j2
                            pe(lambda e, bk=bk, j2=j2, sl=sl: e.matmul(bk[:, j2 * 256:j2 * 256 + 128], lhsT=WUW[par][0:C, sl, 0:128], rhs=KDEW[par][0:C, sl, :],
                                                                        start=True, stop=True), [WUr[par][p_], KDr[par][sl // 4]], [bk])
                            pe(lambda e, bk=bk, j2=j2, sl=sl: e.matmul(bk[:, j2 * 256 + 128:j2 * 256 + 256], lhsT=KDEW[par][0:C, sl, :], rhs=WUW[par][0:C, sl, 128:256],
                                                                        start=True, stop=True), [WUr[par][p_], KDr[par][sl // 4]], [bk])
                        bv = bk[:, 0:n2 * 256].rearrange("p (w a d) -> p w a d", w=n2, a=2)
                        o1 = act(MTW[par][:, s0:s0 + n2, :], bv[:, :, 0, :], AF.Copy, [bk], [MTr[par][p_]], scale=-1.0)
                        P.op("dve", lambda e, bv=bv, s0=s0, n2=n2: e.tensor_copy(out=BCW[par][:, s0:s0 + n2, :], in_=bv[:, :, 1, :]),
                             r=rs([bk]), w=rs([BCr[par][p_]]), extra=[o1])
                steps.append(sF)
                return steps

            kcount = [0]

            def stage2(wi, sl):
                c0, W, C = WAVES[wi]
                par = wi % 2
                ci = c0 + sl
                t0, C, sq = CHK[ci]
                if sq == 0:
                    ko = kcount[0] % 2
                    kcount[0] += 1
                else:
                    ko = 0
                    act(SBT[:, 0, :], SF[:, sq, :], AF.Copy, [SFr[sq]], [SBTr[0]])
                kn = 1 - ko
                by = B[4 + ci % 2]
                bx = B[3]
                pe(lambda e: e.matmul(by[0:C, 0:128], lhsT=NWTW[par][:, sl, 0:C], rhs=SBT[:, ko, :], start=True, stop=False),
                   [NWr[par][0], SBTr[ko]], [by])
                pe(lambda e: e.matmul(by[0:C, 0:128], lhsT=ident_b[0:C, 0:C], rhs=WUW[par][0:C, sl, 128:256], start=False, stop=True),
                   [WUr[par][sl // 2], CSTB], [by])
                pe(lambda e: e.matmul(bx[:, 0:128], lhsT=MTW[par][:, sl, :], rhs=SBT[:, ko, :], start=True, stop=False),
                   [MTr[par][sl // 2], SBTr[ko]], [bx])
                pe(lambda e: e.matmul(bx[:, 0:128], lhsT=ident_b, rhs=BCW[par][:, sl, :], start=False, stop=True),
                   [BCr[par][sl // 2], CSTB], [bx])
                dve(lambda e: e.scalar_tensor_tensor(out=SBT[:, kn, :], in0=SF[:, sq, :], scalar=EGL[:, ci, h:h + 1], in1=bx[:, 0:128],
                                                     op0=ALU.mult, op1=ALU.add), [bx, GBr, SFr[sq]], [SBTr[kn]])
                dve(lambda e: e.scalar_tensor_tensor(out=SF[:, sq, :], in0=SF[:, sq, :], scalar=EGL[:, ci, h:h + 1], in1=bx[:, 0:128],
                                                     op0=ALU.mult, op1=ALU.add), [bx, GBr, SFr[sq]], [SFr[sq]])
                vn = SMB.next()
                act(vn[0:C, :], by[0:C, 0:128], AF.Copy, [by], [vn])
                pe(lambda e: e.matmul(by[0:C, 128:256], lhsT=QGW[par][:, sl, 0:C], rhs=SBT[:, ko, :], start=True, stop=False),
                   [QGr[par][0], SBTr[ko]], [by])
                pe(lambda e: e.matmul(by[0:C, 128:256], lhsT=QKTW[par][0:C, sl, 0:C], rhs=vn[0:C, :], start=False, stop=True),
                   [QKr[par][sl // 4], vn], [by])
                junk = SMB.next()
                s1 = SC1.next()
                act(junk[0:C, :], by[0:C, 128:256], AF.Square, [by], [junk, s1], accum=s1[0:C, 0:1])
                act(s1[0:C, 1:2], s1[0:C, 0:1], AF.Sqrt, [s1], [s1], bias=EPSC[0:C, 0:1], scale=1.0 / 128)
                dve(lambda e: e.reciprocal(out=s1[0:C, 1:2], in_=s1[0:C, 1:2]), [s1], [s1])
                on = SM.next()
                dve(lambda e: e.scalar_tensor_tensor(out=on[0:C, :], in0=by[0:C, 128:256], scalar=s1[0:C, 1:2],
                                                     in1=VEC[0:C, V_DNN:V_DNN + 128], op0=ALU.mult, op1=ALU.mult), [by, s1, VEC], [on])
                dz = SM.next()
                load(dz[0:C, :], DZS[t0:t0 + C, h * 128:(h + 1) * 128], [dz], r=[R("dzs", ci)])
                dve(lambda e: e.tensor_tensor(out=on[0:C, :], in0=on[0:C, :], in1=dz[0:C, :], op=ALU.mult), [on, dz], [on])
                pe(lambda e: e.transpose(out=by[:, 256:256 + C], in_=on[0:C, :], identity=ident[0:C, 0:C]), [on, CST], [by])
                act(OBT[:, t0:t0 + C], by[:, 256:256 + C], AF.Copy, [by], [OBT])

            nw = len(WAVES)
            for s in stage1(0):
                s()
            for wi in range(nw):
                nxt = stage1(wi + 1) if wi + 1 < nw else []
                W = WAVES[wi][1]
                ns = len(nxt)
                per = (ns + W - 1) // W if W else ns
                k = 0
                for i in range(W):
                    stage2(wi, i)
                    for _ in range(per):
                        if k < ns:
                            nxt[k]()
                            k += 1
                while k < ns:
                    nxt[k]()
                    k += 1
            for sq in range(3):
                store(o_S[l, sq, h, :, :], SF[:, sq, :], [SFr[sq]], final=True, key=("st", "sf%d" % sq))
            store(OUTS[(8 + h) * 128:(9 + h) * 128, :], OBT[:, :], [OBT], [R("outs", 8 + h)])
        for h_ in range(8):
            do_head(h_)
        P.flush()
        st.close()


    dve(lambda e: e.memset(VECN[:, :], 1.0), [], [VECN])
    phase0()
    cur = 0
    for l in range(nlayers):
        load(VEC[:, :], vecs_d[l], [VEC])
        if l == 0:
            norm_phase(cur, l, V_NMIX)
        mixer_a(l)
        if "nob" not in dbg:
            mixer_b(l)
        if "noc" not in dbg:
            mixer_c(l)
        merge_phase(l, cur, 1 - cur)
        wout_norm_phase(l, cur, 1 - cur)
        cur = 1 - cur
        ffn_phase(l, cur, 1 - cur)
        ffn_down_norm_phase(l, cur, 1 - cur)
        cur = 1 - cur
        ple_norm_phase(l, cur, 1 - cur, l + 1 < nlayers)
        cur = 1 - cur
    final_phase(cur)
    return finish(nc, P, es, outstores)


def finish(nc, P, es, outstores):
    P.flush()
    es.close()
    return nc


def _vecs(inp):
    v = np.zeros((L, 128, NV), np.float32)
    def pc(a, n):
        return np.ascontiguousarray(a.reshape(n, 128).T)
    for l in range(L):
        v[l, :, V_NMIX:V_NMIX + 16] = pc(inp["norm_mix"][l], 16)
        v[l, :, V_NFFN:V_NFFN + 16] = pc(inp["norm_ffn"][l], 16)
        v[l, :, V_NPLE:V_NPLE + 16] = pc(inp["norm_ple"][l], 16)
        v[l, :, V_BGATE:V_BGATE + 48] = pc(inp["b_gate"][l], 48)
        for j in range(4):
            v[l, :, V_WCA + j * 8:V_WCA + j * 8 + 8] = pc(inp["w_conv_a"][l, j], 8)
            v[l, :, V_WCB + j * 24:V_WCB + j * 24 + 24] = pc(inp["w_conv_b"][l, j], 24)
        v[l, :, V_BCA:V_BCA + 8] = pc(inp["b_conv_a"][l], 8)
        v[l, :, V_BLR:V_BLR + 8] = pc(inp["b_lru_r"][l], 8)
        v[l, :, V_BLI:V_BLI + 8] = pc(inp["b_lru_i"][l], 8)
        v[l, :, V_LAM:V_LAM + 8] = pc(inp["lru_lambda"][l], 8)
        v[l, :, V_QN] = inp["attn_q_norm"][l]
        v[l, :, V_KN] = inp["attn_k_norm"][l]
        v[l, :, V_ALOG:V_ALOG + 8] = inp["dn_a_log"][l][None, :]
        v[l, :, V_DTB:V_DTB + 8] = inp["dn_dt_bias"][l][None, :]
        v[l, :, V_DNN:V_DNN + 128] = inp["dn_norm"][l][None, :]
    return v


def _consts():
    c = np.zeros((128, NCC), np.float32)
    c[:, C_ID:C_ID + 128] = np.eye(128, dtype=np.float32)
    c[:, C_ONE:C_ONE + 128] = 1.0
    i = np.arange(64)
    for half in (0, 64):
        c[half:half + 64, C_UPI:C_UPI + 64] = (i[:, None] <= i[None, :])
        c[half:half + 64, C_LOS:C_LOS + 64] = (i[:, None] > i[None, :])
        c[half:half + 64, C_LOI:C_LOI + 64] = (i[:, None] >= i[None, :])
        c[half:half + 64, C_UPS:C_UPS + 64] = (i[:, None] < i[None, :])
    return c


def _bias_tables(rel_bias):
    NEG = np.float32(-30000.0)
    ext = np.concatenate([rel_bias, np.full((L, 8, 1), NEG, np.float32)], axis=2)
    kk = np.arange(128)[:, None, None]
    ti = np.arange(5)[None, :, None]
    qq = np.arange(128)[None, None, :]
    kpos = (ti - 4) * 128 + kk
    rel = qq - kpos
    idx = np.clip(rel, -128, 128) + 128
    qc = qq // 64
    kc = (kpos + 512) // 64 - 8
    vis = (kc <= qc) & (kc >= qc - 8)
    idx = np.where(vis, idx, 257)
    bp = ext[:, :, idx]
    bp = np.ascontiguousarray(np.transpose(bp, (0, 2, 1, 3, 4)))
    row = np.arange(128)[:, None, None]
    til = np.arange(5)[None, :, None]
    j = np.where(til < 4, til * 128 + row, 512 + row % 32)
    valid = (til < 4) | (row < 64)
    q = np.arange(32)[None, None, :]
    rel = 512 + q - j
    idx = np.clip(rel, -128, 128) + 128
    idx = np.where(valid & (q >= 0), idx, 257)
    bs = ext[:, :, idx]
    bs = np.ascontiguousarray(np.transpose(bs, (0, 2, 1, 3, 4)))
    return bp, bs


def _lru_bd(inp):
    bd = np.zeros((L, 2, 8, 128, 128), np.float32)
    for l in range(L):
        for g, nm in enumerate(("w_lru_r", "w_lru_i")):
            w = inp[nm][l]
            for c in range(8):
                bd[l, g, c, 0:64, 0:64] = w[2 * c]
                bd[l, g, c, 64:128, 64:128] = w[2 * c + 1]
    return bd


def make_in_maps(inp, cores=range(8)):
    f = lambda a: np.ascontiguousarray(a, dtype=np.float32)
    shared = {
        "w_in": f(inp["w_in"]), "w_gate": f(inp["w_gate"]),
        "w_bo": f(inp["w_branch_out"]).reshape(L, 3072, D), "w_out": f(inp["w_out"]),
        "w_fg": f(inp["w_ffn_gate"]), "w_fu": f(inp["w_ffn_up"]), "w_fd": f(inp["w_ffn_down"]),
        "w_pg": f(inp["w_ple_gate"]), "w_pp": f(inp["w_ple_proj"]),
        "lru_bd": _lru_bd(inp), "vecs": _vecs(inp), "consts": _consts(),
    }
    bp, bs = _bias_tables(f(inp["attn_rel_bias"]))
    shared["attn_bias_p"] = bp
    shared["attn_bias_s"] = bs
    maps = []
    for c in cores:
        s = slice(2 * c, 2 * c + 2)
        m = dict(shared)
        m["x_tok"] = np.concatenate([inp["x_prompt"][c], inp["x_sample"][s].reshape(64, D)], 0).astype(np.float32)
        m["pe_tok"] = np.concatenate([inp["p_prompt"][:, c], inp["p_sample"][:, s].reshape(L, 64, PLE)], 1).astype(np.float32)
        m["cache_k"] = f(inp["cache_attn_k"][:, s]).reshape(L, 2, 512, 1024)
        m["cache_v"] = f(inp["cache_attn_v"][:, s]).reshape(L, 2, 512, 1024)
        m["st_conv_a"] = f(inp["state_conv_a"][:, s])
        m["st_lru_h"] = f(inp["state_lru_h"][:, s])
        m["st_conv_b"] = f(inp["state_conv_b"][:, s])
        m["st_S"] = f(inp["state_delta_S"][:, s])
        maps.append(m)
    return maps


def kernel(**inputs):
    inp = {k: np.asarray(v) for k, v in inputs.items()}
    nc = build()
    maps = make_in_maps(inp, cores=range(8))
    res = run_bass_kernel_spmd(nc, maps, core_ids=list(range(8)))
    rs_ = [{k: np.asarray(v) for k, v in r.items()} for r in res.results]
    f = np.float32
    y_p = np.stack([r["y_p"] for r in rs_]).astype(f)
    y_s = np.concatenate([r["y_s"].reshape(2, TS, D) for r in rs_], 0).astype(f)

    def pst(name, shp_tail):
        p = np.stack([r[name][:, 0] for r in rs_], 1).astype(f)
        s = np.concatenate([r[name][:, 1:3] for r in rs_], 1).astype(f)
        return p.reshape((L, 8) + shp_tail), s.reshape((L, 16) + shp_tail)
    p_ca, s_ca = pst("o_conv_a", (3, 1024))
    p_h, s_h = pst("o_lru_h", (1024,))
    p_cb, s_cb = pst("o_conv_b", (3, 3072))
    p_S, s_S = pst("o_S", (8, 128, 128))
    p_k = np.stack([r["o_pk"] for r in rs_], 1).reshape(L, 8, 512, 8, 128).astype(f)
    p_v = np.stack([r["o_pv"] for r in rs_], 1).reshape(L, 8, 512, 8, 128).astype(f)
    s_k = np.concatenate([r["o_sk"] for r in rs_], 1).reshape(L, 16, TS, 8, 128).astype(f)
    s_v = np.concatenate([r["o_sv"] for r in rs_], 1).reshape(L, 16, TS, 8, 128).astype(f)
    return (y_p, y_s, p_ca, p_h, p_cb, p_S, p_k, p_v, s_ca, s_h, s_cb, s_S, s_k, s_v)
```

```python
import numpy as np
from contextlib import ExitStack
import concourse.bass as bass
import concourse.mybir as mybir
from concourse.bass_utils import run_bass_kernel_spmd

F32 = mybir.dt.float32
BF16 = mybir.dt.bfloat16
AF = mybir.ActivationFunctionType
ALU = mybir.AluOpType
AX = mybir.AxisListType

D = 2048
NCH = 16
TP = 2048
TS = 32
T = TP + 2 * TS
NIN = 9232
DFF = 5632
NFF = 44
PLE = 256
L = 2
TT = [(0, 512), (512, 512), (1024, 512), (1536, 512), (2048, 64)]
GELU_K = 2.0 * 0.7978845608028654
SEQS = [(0, 2048, 0), (2048, 32, 2051), (2080, 32, 2086)]
CB = 2121

V_NMIX, V_NFFN, V_NPLE = 0, 16, 32
V_BGATE = 48
V_WCA = 96
V_BCA = 128
V_BLR = 136
V_BLI = 144
V_LAM = 152
V_WCB = 160
V_QN = 256
V_KN = 257
V_ALOG = 258
V_DTB = 266
V_DNN = 274
NV = 402
C_ID, C_ONE, C_UPI, C_LOS, C_LOI, C_UPS = 0, 128, 256, 320, 384, 448
NCC = 512


ALL_RES = []


class Res:
    __slots__ = ("name", "w", "rd")

    def __init__(self, name):
        self.name = name
        self.w = None
        self.rd = []
        ALL_RES.append(self)


class Op:
    __slots__ = ("eng", "fn", "deps", "dma", "key", "sig", "val", "sem")


class Prog:
    ENGS = ["sync", "act", "pool", "dve", "pe"]

    def __init__(self, nc, es):
        self.nc = nc
        self.es = es
        self.ops = {e: [] for e in self.ENGS}
        self.keycnt = {}
        self.engcnt = {e: 0 for e in self.ENGS}
        self.engsem = {e: es.enter_context(nc.semaphore("s_" + e)) for e in self.ENGS}
        self.keysem = {}
        self.n = 0

    def op(self, eng, fn, r=(), w=(), dma=False, key=None, extra=()):
        o = Op()
        o.eng = eng
        o.fn = fn
        o.dma = dma
        o.key = key
        o.sig = dma
        o.val = 0
        o.sem = None
        deps = {}
        for x in r:
            if x.w is not None:
                deps[id(x.w)] = (x.w, True)
        for x in w:
            if x.w is not None and id(x.w) not in deps:
                deps[id(x.w)] = (x.w, False)
            for q in x.rd:
                if id(q) not in deps:
                    deps[id(q)] = (q, False)
        for q in extra:
            deps[id(q)] = (q, True)
        fin = []
        for d, raw in deps.values():
            if d is o:
                continue
            if d.dma:
                fin.append((d, self.keycnt[d.key] * 16))
                continue
            if (not dma) and d.eng == eng:
                if eng == "pe" or not raw:
                    continue
            d.sig = True
            fin.append((d, None))
        o.deps = fin
        if dma:
            assert key is not None
            self.keycnt[key] = self.keycnt.get(key, 0) + 1
        for x in r:
            x.rd.append(o)
        for x in w:
            x.w = o
            x.rd = []
        self.ops[eng].append(o)
        self.n += 1
        return o

    def flush(self):
        nc = self.nc
        for k in self.keycnt:
            if k not in self.keysem:
                self.keysem[k] = self.es.enter_context(nc.semaphore("k_%d" % len(self.keysem)))
        for e in self.ENGS:
            if self.ops[e]:
                last = [o for o in self.ops[e] if not o.dma]
                if last:
                    last[-1].sig = True
            for o in self.ops[e]:
                if o.dma:
                    o.sem = self.keysem[o.key]
                elif o.sig:
                    self.engcnt[e] += 1
                    o.val = self.engcnt[e]
                    o.sem = self.engsem[e]
        ops = self.ops
        finals = [(self.engsem[e], self.engcnt[e]) for e in self.ENGS if self.engcnt[e] > 0]
        finals += [(self.keysem[k], c * 16) for k, c in self.keycnt.items()]

        def make(e):
            def body(eng):
                waited = {}
                for o in ops[e]:
                    for d, v in o.deps:
                        s = d.sem
                        if v is None:
                            v = d.val
                        if waited.get(id(s), 0) < v:
                            eng.wait_ge(s, v)
                            waited[id(s)] = v
                    ins = o.fn(eng)
                    if o.sig:
                        ins.then_inc(o.sem, 16 if o.dma else 1)
                for s, v in finals:
                    if v > 0:
                        eng.wait_ge(s, v)
            return body

        with nc.Block() as block:
            block.sync(make("sync"))
            block.scalar(make("act"))
            block.gpsimd(make("pool"))
            block.vector(make("dve"))
            block.tensor(make("pe"))
        self.ops = {e: [] for e in self.ENGS}
        for x in ALL_RES:
            x.w = None
            x.rd = []


class Tn:
    __slots__ = ("h", "res", "name")

    def __init__(self, h, name):
        self.h = h
        self.res = Res(name)
        self.name = name

    def __getitem__(self, k):
        return self.h[k]


class Pool:
    def __init__(self, items):
        self.items = items
        self.i = 0

    def next(self):
        t = self.items[self.i % len(self.items)]
        self.i += 1
        return t


def build(dbg=(), nlayers=L, stop=None):
    nc = bass.Bass("TRN2", target_bir_lowering=False)
    es = ExitStack()
    del ALL_RES[:]
    P = Prog(nc, es)
    resmap = {}
    outstores = []

    def R(*key):
        if key not in resmap:
            resmap[key] = Res(str(key))
        return resmap[key]

    def din(name, shape, dt=F32):
        return nc.dram_tensor(name, list(shape), dt, kind="ExternalInput").ap()

    def dout(name, shape, dt=F32):
        return nc.dram_tensor(name, list(shape), dt, kind="ExternalOutput").ap()

    def dscr(name, shape, dt=F32):
        kind = "ExternalOutput" if name in dbg else "Internal"
        return nc.dram_tensor(name, list(shape), dt, kind=kind).ap()

    def sb(name, shape, dt=F32):
        return Tn(es.enter_context(nc.sbuf_tensor(name, list(shape), dt)), name)

    def sbpool(name, shape, dt, n):
        return Pool([sb("%s%d" % (name, i), shape, dt) for i in range(n)])

    def psum(name, shape, dt=F32):
        return Tn(es.enter_context(nc.psum_tensor(name, list(shape), dt)), name)

    def load(dst_ap, src_ap, w, r=(), eng="sync", key=None):
        wr = [x.res if isinstance(x, Tn) else x for x in w]
        rr = [x.res if isinstance(x, Tn) else x for x in r]
        k = key if key is not None else ("ld", wr[0].name)
        return P.op(eng, lambda e: e.dma_start(out=dst_ap, in_=src_ap), r=rr, w=wr, dma=True, key=k)

    def store(dst_ap, src_ap, r, w=(), eng="sync", key=None, final=False):
        wr = [x.res if isinstance(x, Tn) else x for x in w]
        rr = [x.res if isinstance(x, Tn) else x for x in r]
        k = key if key is not None else ("st", rr[0].name)
        o = P.op(eng, lambda e: e.dma_start(out=dst_ap, in_=src_ap), r=rr, w=wr, dma=True, key=k)
        if final:
            outstores.append(o)
        return o

    def rs(xs):
        return [x.res if isinstance(x, Tn) else x for x in xs]

    def act(out, in_, func, r, w, bias=None, scale=None, accum=None):
        kw = {}
        if bias is not None:
            kw["bias"] = bias
        if scale is not None:
            kw["scale"] = scale
        if accum is not None:
            kw["accum_out"] = accum
        return P.op("act", lambda e: e.activation(out=out, in_=in_, func=func, **kw), r=rs(r), w=rs(w))

    def dve(fn, r, w):
        return P.op("dve", fn, r=rs(r), w=rs(w))

    def pe(fn, r, w):
        return P.op("pe", fn, r=rs(r), w=rs(w))

    x_tok = din("x_tok", [T, D])
    pe_tok = din("pe_tok", [L, T, PLE])
    cache_k = din("cache_k", [L, 2, 512, 1024])
    cache_v = din("cache_v", [L, 2, 512, 1024])
    st_conv_a = din("st_conv_a", [L, 2, 3, 1024])
    st_lru_h = din("st_lru_h", [L, 2, 1024])
    st_conv_b = din("st_conv_b", [L, 2, 3, 3072])
    st_S = din("st_S", [L, 2, 8, 128, 128])
    w_in = din("w_in", [L, D, NIN])
    w_gate = din("w_gate", [L, D, 3 * D])
    w_bo = din("w_bo", [L, 3 * 1024, D])
    w_out = din("w_out", [L, D, D])
    w_fg = din("w_fg", [L, D, DFF])
    w_fu = din("w_fu", [L, D, DFF])
    w_fd = din("w_fd", [L, DFF, D])
    w_pg = din("w_pg", [L, D, D])
    w_pp = din("w_pp", [L, PLE, D])
    lru_bd = din("lru_bd", [L, 2, 8, 128, 128])
    vecs_d = din("vecs", [L, 128, NV])
    bias_p_d = din("attn_bias_p", [L, 128, 8, 5, 128])
    bias_s_d = din("attn_bias_s", [L, 128, 8, 5, 32])
    consts_d = din("consts", [128, NCC])

    y_p = dout("y_p", [TP, D])
    y_s = dout("y_s", [2 * TS, D])
    o_conv_a = dout("o_conv_a", [L, 3, 3, 1024])
    o_lru_h = dout("o_lru_h", [L, 3, 1024])
    o_conv_b = dout("o_conv_b", [L, 3, 3, 3072])
    o_S = dout("o_S", [L, 3, 8, 128, 128])
    o_pk = dout("o_pk", [L, 512, 1024])
    o_pv = dout("o_pv", [L, 512, 1024])
    o_sk = dout("o_sk", [L, 2, 32, 1024])
    o_sv = dout("o_sv", [L, 2, 32, 1024])

    XT = [dscr("xt0", [D, T]), dscr("xt1", [D, T])]
    OUTS = dscr("outs_t", [3072, T], BF16)
    HID = dscr("hid_t", [DFF, T], BF16)

    uniq = [0]

    def sbp(st, name, shape, dt=F32):
        uniq[0] += 1
        return Tn(st.enter_context(nc.sbuf_tensor("%s_u%d" % (name, uniq[0]), list(shape), dt)), name)

    ACT_T = sb("act_t", [128, NCH, T], BF16)
    ACTR = [Res("actr%d" % i) for i in range(len(TT))]
    CST = sb("cst", [128, NCC])
    CSTB = sb("cstb", [128, 256], BF16)
    VEC = sb("vec", [128, NV])
    WSH = [None]

    def mkws(st, n, nelem):
        WSH[0] = Pool([sbp(st, "ws%d" % i, [128, nelem], BF16) for i in range(n)])
    PS = Pool([psum("ps%d" % i, [128, 512]) for i in range(6)])
    PSB = Pool([psum("psb%d" % i, [128, 1024], BF16) for i in range(2)])
    W512 = sbpool("w512", [128, 512], F32, 7)
    B512 = sbpool("b512", [128, 512], BF16, 4)

    ident = CST[:, C_ID:C_ID + 128]
    ones_b = CSTB[:, 128:256]
    ident_b = CSTB[:, 0:128]

    load(CST[:, :], consts_d[:, :], [CST])
    dve(lambda e: e.tensor_copy(out=CSTB[:, :], in_=CST[:, 0:256]), [CST], [CSTB])

    def XR(b, j, ti):
        return R("xt", b, j, ti)

    def mm_acc(ps_ap, pairs, r, w):
        def fn(e):
            ins = None
            n = len(pairs)
            for i, (a, b) in enumerate(pairs):
                ins = e.matmul(ps_ap, lhsT=a, rhs=b, start=(i == 0), stop=(i == n - 1))
            return ins
        return pe(fn, r, w)

    def wload(src2d, nk, ncols):
        s = WSH[0].next()
        v = s[:, 0:nk * ncols].rearrange("p (k n) -> p k n", k=nk)
        load(v, src2d.rearrange("(k p) n -> p k n", p=128), [s], eng="pool")
        return s, v

    def slow_load(dst, src, w, r=()):
        wr = rs(w)
        return P.op("sync", lambda e: e.dma_start(out=dst, in_=src, allow_slow_non_contiguous=True),
                    r=rs(r), w=wr, dma=True, key=("ld", wr[0].name))

    def slow_store(dst, src, r, final=True):
        rr = rs(r)
        o = P.op("sync", lambda e: e.dma_start(out=dst, in_=src, allow_slow_non_contiguous=True),
                 r=rr, w=[], dma=True, key=("st", rr[0].name))
        if final:
            outstores.append(o)
        return o

    def phase0():
        st = ExitStack()
        TOKT = Pool([sbp(st, "tokt%d" % i, [128, D]) for i in range(2)])
        FTT = Pool([sbp(st, "ftt%d" % i, [128, NCH, 128]) for i in range(2)])
        for tt in range(17):
            t0 = tt * 128
            n = min(128, T - t0)
            tk = TOKT.next()
            load(tk[0:n, :], x_tok[t0:t0 + n, :], [tk])
            ft = FTT.next()
            for g in range(4):
                pt = PS.next()
                for q in range(4):
                    c = g * 4 + q
                    pe(lambda e, pt=pt, q=q, tk=tk, c=c, n=n: e.transpose(
                        out=pt[:, q * 128:q * 128 + n], in_=tk[0:n, c * 128:(c + 1) * 128], identity=ident[0:n, 0:n]),
                        [tk, CST], [pt])
                dve(lambda e, pt=pt, ft=ft, g=g, n=n: e.tensor_copy(
                    out=ft[:, g * 4:(g + 1) * 4, 0:n], in_=pt[:, :].rearrange("p (q t) -> p q t", q=4)[:, :, 0:n]),
                    [pt], [ft])
            store(XT[0][:, t0:t0 + n].rearrange("(c p) t -> p c t", p=128), ft[:, :, 0:n], [ft],
                  [XR(0, j, min(tt // 4, 4)) for j in range(NCH)])
        P.flush()
        st.close()

    def norm_phase(b, l, vcol):
        st = ExitStack()
        XTL = sbp(st, "xtl", [128, NCH, 512])
        for ti, (t0, n) in enumerate(TT):
            load(XTL[:, :, 0:n], XT[b][:, t0:t0 + n].rearrange("(c p) t -> p c t", p=128), [XTL],
                 r=[XR(b, j, ti) for j in range(NCH)])
            pt = PS.next()
            for c in range(NCH):
                sq = B512.next()
                act(sq[:, 0:n], XTL[:, c, 0:n], AF.Square, [XTL], [sq])
                pe(lambda e, pt=pt, sq=sq, c=c, n=n: e.matmul(pt[:, 0:n], lhsT=ones_b, rhs=sq[:, 0:n],
                                                              start=(c == 0), stop=(c == NCH - 1)), [sq, CSTB], [pt])
            sd = W512.next()
            act(sd[:, 0:n], pt[:, 0:n], AF.Sqrt, [pt], [sd], bias=EPSC[:, 0:1], scale=1.0 / D)
            rstd = W512.next()
            dve(lambda e, rstd=rstd, sd=sd, n=n: e.reciprocal(out=rstd[:, 0:n], in_=sd[:, 0:n]), [sd], [rstd])
            for c in range(NCH):
                dve(lambda e, c=c, t0=t0, n=n, rstd=rstd: e.scalar_tensor_tensor(
                    out=ACT_T[:, c, t0:t0 + n], in0=XTL[:, c, 0:n], scalar=VEC[:, vcol + c:vcol + c + 1],
                    in1=rstd[:, 0:n], op0=ALU.mult, op1=ALU.mult), [XTL, rstd, VEC], [ACTR[ti]])
        P.flush()
        st.close()

    EPSC = sb("epsc", [128, 2])
    dve(lambda e: e.memset(EPSC[:, 0:1], 1e-6), [], [EPSC])
    dve(lambda e: e.memset(EPSC[:, 1:2], 1.0), [EPSC], [EPSC])

    def proj_chunk(wsrc2d, consume, nk=NCH):
        s, v = wload(wsrc2d, nk, 128)
        for ti, (t0, n) in enumerate(TT):
            pt = PS.next()
            mm_acc(pt[:, 0:n], [(v[:, k, :], ACT_T[:, k, t0:t0 + n]) for k in range(nk)], [s, ACTR[ti]], [pt])
            consume(ti, t0, n, pt)

    def evac_to_convbuf(RXB):
        def consume(ti, t0, n, pt):
            if ti < 4:
                act(RXB[:, 3 + t0:3 + t0 + n], pt[:, 0:n], AF.Copy, [pt], [RXB])
            else:
                act(RXB[:, 2051:2121].rearrange("p (s c) -> p s c", c=35)[:, :, 3:35],
                    pt[:, 0:64].rearrange("p (s c) -> p s c", c=32), AF.Copy, [pt], [RXB])
        return consume

    def conv_tile(RXB, wcol0, wstride, ti, t0, n, bias_ap=None, pool=None):
        o = (pool or W512).next()
        if ti < 4:
            src = lambda j: RXB[:, t0 + j:t0 + j + n]
            dst = o[:, 0:n]
        else:
            src = lambda j: RXB[:, 2051:2121].rearrange("p (s c) -> p s c", c=35)[:, :, j:j + 32]
            dst = o[:, 0:64].rearrange("p (s c) -> p s c", c=32)
        w = lambda j: VEC[:, wcol0 + j * wstride:wcol0 + j * wstride + 1]
        if bias_ap is not None:
            dve(lambda e: e.tensor_scalar(out=dst, in0=src(0), scalar1=w(0), scalar2=bias_ap, op0=ALU.mult, op1=ALU.add),
                [RXB, VEC], [o])
        else:
            dve(lambda e: e.tensor_scalar(out=dst, in0=src(0), scalar1=w(0), scalar2=None, op0=ALU.mult), [RXB, VEC], [o])
        for j in (1, 2, 3):
            dve(lambda e, j=j: e.scalar_tensor_tensor(out=dst, in0=src(j), scalar=w(j), in1=dst, op0=ALU.mult, op1=ALU.add),
                [RXB, VEC, o], [o])
        return o

    def conv_state_io(RXB, l, st_in, o_out, c0):
        dve(lambda e: e.memset(RXB[:, 0:3], 0.0), [], [RXB])
        for s in range(2):
            cb = SEQS[s + 1][2]
            slow_load(RXB[:, cb:cb + 3], st_in[l, s, :, c0:c0 + 128].rearrange("k p -> p k"), [RXB])

    def conv_state_out(RXB, l, o_out, c0):
        for s, (ts, ln, cb) in enumerate(SEQS):
            slow_store(o_out[l, s, :, c0:c0 + 128].rearrange("k p -> p k"), RXB[:, cb + ln:cb + ln + 3], [RXB])

    def mixer_a(l):
        st = ExitStack()
        mkws(st, 3, 2048)
        RXB2 = [sbp(st, "rxb%d" % i_, [128, CB]) for i_ in range(2)]
        HST = sbp(st, "hst", [128, 2, 8])
        CC = sbp(st, "cc", [128, 8])
        BDP = Pool([sbp(st, "bd%d" % i, [128, 2, 128], BF16) for i in range(3)])
        OAP = Pool([sbp(st, "oa%d" % i, [128, T], BF16) for i in range(2)])
        GLB2 = [sbp(st, "glb%d" % i_, [128, T]) for i_ in range(2)]
        HL = sbp(st, "hl", [128, 8, 3])
        XA5 = [sbp(st, "xa5%d" % i_, [128, 512]) for i_ in range(5)]
        XAB5 = [sbp(st, "xab5%d" % i_, [128, 512], BF16) for i_ in range(5)]
        RR5 = [sbp(st, "rr5%d" % i_, [128, 512]) for i_ in range(5)]
        GI5 = [sbp(st, "gi5%d" % i_, [128, 512]) for i_ in range(5)]
        AA5 = [sbp(st, "aa5%d" % i_, [128, 512]) for i_ in range(5)]
        BB5 = [sbp(st, "bb5%d" % i_, [128, 512]) for i_ in range(5)]
        HHP = Pool([sbp(st, "hhp%d" % i_, [128, 512]) for i_ in range(3)])
        for s in range(2):
            slow_load(HST[:, s, :], st_lru_h[l, s, :].rearrange("(c p) -> p c", p=128), [HST])
        act(CC[:, :], VEC[:, V_LAM:V_LAM + 8], AF.Exp, [VEC], [CC], scale=-1.0)
        act(CC[:, :], CC[:, :], AF.Ln, [CC], [CC], bias=EPSC[:, 1:2], scale=1.0)
        dve(lambda e: e.tensor_scalar(out=CC[:, :], in0=CC[:, :], scalar1=-8.0, scalar2=None, op0=ALU.mult), [CC], [CC])
        bds = {}

        def a_proj(j):
            RXB = RXB2[j % 2]
            GLB = GLB2[j % 2]
            bd = BDP.next()
            bds[j] = bd
            load(bd[:, :, :], lru_bd[l, :, j, :, :].rearrange("g p n -> p g n"), [bd], eng="pool")
            conv_state_io(RXB, l, st_conv_a, o_conv_a, j * 128)
            proj_chunk(w_in[l, :, j * 128:(j + 1) * 128], evac_to_convbuf(RXB))
            conv_state_out(RXB, l, o_conv_a, j * 128)

            def gelu_consume(ti, t0, n, pt):
                rg = W512.next()
                act(rg[:, 0:n], pt[:, 0:n], AF.Copy, [pt], [rg])
                t1 = W512.next()
                dve(lambda e: e.tensor_tensor(out=t1[:, 0:n], in0=rg[:, 0:n], in1=rg[:, 0:n], op=ALU.mult), [rg], [t1])
                dve(lambda e: e.tensor_scalar(out=t1[:, 0:n], in0=t1[:, 0:n], scalar1=0.044715, scalar2=1.0,
                                              op0=ALU.mult, op1=ALU.add), [t1], [t1])
                dve(lambda e: e.tensor_tensor(out=t1[:, 0:n], in0=t1[:, 0:n], in1=rg[:, 0:n], op=ALU.mult), [t1, rg], [t1])
                act(t1[:, 0:n], t1[:, 0:n], AF.Sigmoid, [t1], [t1], scale=GELU_K)
                dve(lambda e: e.tensor_tensor(out=GLB[:, t0:t0 + n], in0=t1[:, 0:n], in1=rg[:, 0:n], op=ALU.mult),
                    [t1, rg], [GLB])
            proj_chunk(w_in[l, :, 1024 + j * 128:1024 + (j + 1) * 128], gelu_consume)

        def a_tiles(j):
            RXB = RXB2[j % 2]
            GLB = GLB2[j % 2]
            bd = bds[j]
            oa = OAP.next()
            hprev = None
            xap = Pool(XA5)
            NT = len(TT)
            for ti, (t0, n) in enumerate(TT):
                xa = conv_tile(RXB, V_WCA + j, 8, ti, t0, n, bias_ap=VEC[:, V_BCA + j:V_BCA + j + 1], pool=xap)
                act(XAB5[ti][:, 0:n], xa[:, 0:n], AF.Copy, [xa], [XAB5[ti]])
            for ti, (t0, n) in enumerate(TT):
                pr = PS.next()
                pe(lambda e, pr=pr, ti=ti, n=n, bd=bd: e.matmul(pr[:, 0:n], lhsT=bd[:, 0, :], rhs=XAB5[ti][:, 0:n], start=True, stop=True), [bd, XAB5[ti]], [pr])
                pi = PS.next()
                pe(lambda e, pi=pi, ti=ti, n=n, bd=bd: e.matmul(pi[:, 0:n], lhsT=bd[:, 1, :], rhs=XAB5[ti][:, 0:n], start=True, stop=True), [bd, XAB5[ti]], [pi])
                act(RR5[ti][:, 0:n], pr[:, 0:n], AF.Sigmoid, [pr, VEC], [RR5[ti]], bias=VEC[:, V_BLR + j:V_BLR + j + 1])
                act(GI5[ti][:, 0:n], pi[:, 0:n], AF.Sigmoid, [pi, VEC], [GI5[ti]], bias=VEC[:, V_BLI + j:V_BLI + j + 1])
            for ti, (t0, n) in enumerate(TT):
                act(AA5[ti][:, 0:n], RR5[ti][:, 0:n], AF.Exp, [RR5[ti], CC], [AA5[ti]], scale=CC[:, j:j + 1])
            for ti, (t0, n) in enumerate(TT):
                dve(lambda e, ti=ti, n=n: e.tensor_tensor(out=BB5[ti][:, 0:n], in0=AA5[ti][:, 0:n], in1=AA5[ti][:, 0:n], op=ALU.mult), [AA5[ti]], [BB5[ti]])
                dve(lambda e, ti=ti, n=n: e.tensor_scalar(out=BB5[ti][:, 0:n], in0=BB5[ti][:, 0:n], scalar1=-1.0, scalar2=1.0, op0=ALU.mult, op1=ALU.add),
                    [BB5[ti]], [BB5[ti]])
            for ti, (t0, n) in enumerate(TT):
                act(BB5[ti][:, 0:n], BB5[ti][:, 0:n], AF.Sqrt, [BB5[ti]], [BB5[ti]])
            for ti, (t0, n) in enumerate(TT):
                dve(lambda e, ti=ti, n=n: e.tensor_tensor(out=BB5[ti][:, 0:n], in0=BB5[ti][:, 0:n], in1=GI5[ti][:, 0:n], op=ALU.mult), [BB5[ti], GI5[ti]], [BB5[ti]])
                dve(lambda e, ti=ti, n=n: e.tensor_tensor(out=BB5[ti][:, 0:n], in0=BB5[ti][:, 0:n], in1=XA5[ti][:, 0:n], op=ALU.mult), [BB5[ti], XA5[ti]], [BB5[ti]])
            pend = {ti: (AA5[ti], BB5[ti]) for ti in range(NT)}

            def a_scan(ti, t0, n, hprev, j=j, oa=oa):
                aa, bb = pend.pop(ti)
                hh = HHP.next()
                if ti < 4:
                    init = 0.0 if ti == 0 else hprev[:, 511:512]
                    rd = [aa, bb] + ([hprev] if ti > 0 else [])
                    dve(lambda e: e.tensor_tensor_scan(out=hh[:, 0:n], data0=aa[:, 0:n], data1=bb[:, 0:n], initial=init,
                                                       op0=ALU.mult, op1=ALU.add), rd, [hh])
                    if ti == 3:
                        dve(lambda e: e.tensor_copy(out=HL[:, j, 0:1], in_=hh[:, 511:512]), [hh], [HL])
                else:
                    for s in range(2):
                        dve(lambda e, s=s: e.tensor_tensor_scan(
                            out=hh[:, s * 32:(s + 1) * 32], data0=aa[:, s * 32:(s + 1) * 32], data1=bb[:, s * 32:(s + 1) * 32],
                            initial=HST[:, s, j:j + 1], op0=ALU.mult, op1=ALU.add), [aa, bb, HST], [hh])
                        dve(lambda e, s=s: e.tensor_copy(out=HL[:, j, 1 + s:2 + s], in_=hh[:, s * 32 + 31:s * 32 + 32]), [hh], [HL])
                dve(lambda e: e.tensor_tensor(out=oa[:, t0:t0 + n], in0=hh[:, 0:n], in1=GLB[:, t0:t0 + n], op=ALU.mult), [hh, GLB], [oa])
                return hh
            for ti, (t0, n) in enumerate(TT):
                hprev = a_scan(ti, t0, n, hprev)
            store(OUTS[j * 128:(j + 1) * 128, :], oa[:, :], [oa], [R("outs", j)])
        a_proj(0)
        for j in range(8):
            if j + 1 < 8:
                a_proj(j + 1)
            a_tiles(j)
        for s in range(3):
            slow_store(o_lru_h[l, s, :].rearrange("(c p) -> p c", p=128), HL[:, :, s], [HL])
        P.flush()
        st.close()

    MRGD = dscr("mrg_t", [D, T], BF16)

    def load_act(src):
        for ti, (t0, n) in enumerate(TT):
            load(ACT_T[:, :, t0:t0 + n], src[:, t0:t0 + n].rearrange("(c p) t -> p c t", p=128), [ACTR[ti]],
                 r=[R("mrg", j, ti) for j in range(NCH)], key=("ld", "act_t"))

    def resid_consume(src_b, dst_b, j, extra_fn=None):
        def consume(ti, t0, n, pt):
            xt = W512.next()
            load(xt[:, 0:n], XT[src_b][j * 128:(j + 1) * 128, t0:t0 + n], [xt], r=[XR(src_b, j, ti)])
            o = W512.next()
            dve(lambda e: e.tensor_tensor(out=o[:, 0:n], in0=pt[:, 0:n], in1=xt[:, 0:n], op=ALU.add), [pt, xt], [o])
            store(XT[dst_b][j * 128:(j + 1) * 128, t0:t0 + n], o[:, 0:n], [o], [XR(dst_b, j, ti)])
        return consume

    def merge_phase(l, cur, nxt):
        st = ExitStack()
        mkws(st, 4, 6144)
        OTP = Pool([sbp(st, "ot%d" % i, [128, 24, 512], BF16) for i in range(2)])
        MJP = Pool([sbp(st, "mj%d" % i, [128, T], BF16) for i in range(2)])
        wg3 = w_gate[l].rearrange("(k p) (n c) -> p k n c", p=128, n=3)
        wb3 = w_bo[l].rearrange("(n k p) c -> p n k c", n=3, p=128)
        for j in range(NCH):
            sg = WSH[0].next()
            vg = sg[:, 0:6144].rearrange("p (k n c) -> p k n c", k=16, n=3)
            for nb in range(3):
                load(vg[:, :, nb, :], wg3[:, :, nb, j * 128:(j + 1) * 128], [sg], eng="pool")
            sbo = WSH[0].next()
            vb = sbo[:, 0:3072].rearrange("p (n k c) -> p n k c", n=3, k=8)
            load(vb, wb3[:, :, :, j * 128:(j + 1) * 128], [sbo], eng="pool")
            mj = MJP.next()
            for ti, (t0, n) in enumerate(TT):
                ot = OTP.next()
                load(ot[:, :, 0:n], OUTS[:, t0:t0 + n].rearrange("(c p) t -> p c t", p=128), [ot],
                     r=[R("outs", c) for c in range(24)])
                mrg = W512.next()
                for nb in range(3):
                    pg = PS.next()
                    mm_acc(pg[:, 0:n], [(vg[:, k, nb, :], ACT_T[:, k, t0:t0 + n]) for k in range(16)], [sg, ACTR[ti]], [pg])
                    gs = W512.next()
                    act(gs[:, 0:n], pg[:, 0:n], AF.Sigmoid, [pg, VEC], [gs],
                        bias=VEC[:, V_BGATE + nb * 16 + j:V_BGATE + nb * 16 + j + 1])
                    pb = PS.next()
                    mm_acc(pb[:, 0:n], [(vb[:, nb, k, :], ot[:, nb * 8 + k, 0:n]) for k in range(8)], [sbo, ot], [pb])
                    if nb == 0:
                        dve(lambda e, mrg=mrg, gs=gs, pb=pb, n=n: e.tensor_tensor(out=mrg[:, 0:n], in0=pb[:, 0:n], in1=gs[:, 0:n], op=ALU.mult),
                            [pb, gs], [mrg])
                    else:
                        dve(lambda e, gs=gs, pb=pb, n=n: e.tensor_tensor(out=gs[:, 0:n], in0=pb[:, 0:n], in1=gs[:, 0:n], op=ALU.mult),
                            [pb, gs], [gs])
                        dve(lambda e, mrg=mrg, gs=gs, n=n: e.tensor_tensor(out=mrg[:, 0:n], in0=mrg[:, 0:n], in1=gs[:, 0:n], op=ALU.add),
                            [mrg, gs], [mrg])
                act(mj[:, t0:t0 + n], mrg[:, 0:n], AF.Copy, [mrg], [mj])
            store(MRGD[j * 128:(j + 1) * 128, :], mj[:, :], [mj], [R("mrg", j, ti) for ti in range(5)])
        P.flush()
        st.close()

    def ffn_phase(l, cur, nxt):
        st = ExitStack()
        mkws(st, 6, 2048)
        HJP = Pool([sbp(st, "hj%d" % i, [128, T], BF16) for i in range(2)])
        for j in range(NFF):
            s1, v1 = wload(w_fg[l, :, j * 128:(j + 1) * 128], 16, 128)
            s2, v2 = wload(w_fu[l, :, j * 128:(j + 1) * 128], 16, 128)
            hj = HJP.next()
            for ti, (t0, n) in enumerate(TT):
                pg = PS.next()
                mm_acc(pg[:, 0:n], [(v1[:, k, :], ACT_T[:, k, t0:t0 + n]) for k in range(16)], [s1, ACTR[ti]], [pg])
                pu = PS.next()
                mm_acc(pu[:, 0:n], [(v2[:, k, :], ACT_T[:, k, t0:t0 + n]) for k in range(16)], [s2, ACTR[ti]], [pu])
                sgl = W512.next()
                act(sgl[:, 0:n], pg[:, 0:n], AF.Silu, [pg], [sgl])
                dve(lambda e, hj=hj, sgl=sgl, pu=pu, t0=t0, n=n: e.tensor_tensor(out=hj[:, t0:t0 + n], in0=pu[:, 0:n], in1=sgl[:, 0:n], op=ALU.mult),
                    [pu, sgl], [hj])
            store(HID[j * 128:(j + 1) * 128, :], hj[:, :], [hj], [R("hid", j)])
        P.flush()
        st.close()

    def ple_phase(l, cur, nxt):
        st = ExitStack()
        mkws(st, 4, 2048)
        PET = sbp(st, "pet", [128, 2, T], BF16)
        PTK = Pool([sbp(st, "ptk%d" % i, [128, PLE]) for i in range(2)])
        for tt in range(17):
            t0 = tt * 128
            n = min(128, T - t0)
            tk = PTK.next()
            load(tk[0:n, :], pe_tok[l, t0:t0 + n, :], [tk])
            pt = PS.next()
            for q in range(2):
                pe(lambda e, pt=pt, q=q, tk=tk, n=n: e.transpose(out=pt[:, q * 128:q * 128 + n], in_=tk[0:n, q * 128:(q + 1) * 128],
                                                                 identity=ident[0:n, 0:n]), [tk, CST], [pt])
            dve(lambda e, pt=pt, t0=t0, n=n: e.tensor_copy(out=PET[:, :, t0:t0 + n],
                                                           in_=pt[:, 0:256].rearrange("p (q t) -> p q t", q=2)[:, :, 0:n]), [pt], [PET])
        for j in range(NCH):
            s1, v1 = wload(w_pg[l, :, j * 128:(j + 1) * 128], 16, 128)
            s2, v2 = wload(w_pp[l, :, j * 128:(j + 1) * 128], 2, 128)
            for ti, (t0, n) in enumerate(TT):
                pg = PS.next()
                mm_acc(pg[:, 0:n], [(v1[:, k, :], ACT_T[:, k, t0:t0 + n]) for k in range(16)], [s1, ACTR[ti]], [pg])
                pp = PS.next()
                mm_acc(pp[:, 0:n], [(v2[:, k, :], PET[:, k, t0:t0 + n]) for k in range(2)], [s2, PET], [pp])
                sg = W512.next()
                act(sg[:, 0:n], pg[:, 0:n], AF.Sigmoid, [pg], [sg])
                xt = W512.next()
                load(xt[:, 0:n], XT[cur][j * 128:(j + 1) * 128, t0:t0 + n], [xt], r=[XR(cur, j, ti)])
                dve(lambda e, sg=sg, pp=pp, n=n: e.tensor_tensor(out=sg[:, 0:n], in0=pp[:, 0:n], in1=sg[:, 0:n], op=ALU.mult), [pp, sg], [sg])
                o = W512.next()
                dve(lambda e, o=o, sg=sg, xt=xt, n=n: e.tensor_tensor(out=o[:, 0:n], in0=sg[:, 0:n], in1=xt[:, 0:n], op=ALU.add), [sg, xt], [o])
                store(XT[nxt][j * 128:(j + 1) * 128, t0:t0 + n], o[:, 0:n], [o], [XR(nxt, j, ti)])
        P.flush()
        st.close()

    VECN = sb("vecn", [128, 16])

    def norm_tile(xt, xtr, ti, t0, n, gain):
        pt = PS.next()
        for c in range(NCH):
            sq = B512.next()
            act(sq[:, 0:n], xt[:, c, 0:n], AF.Square, [xtr[c]], [sq])
            pe(lambda e, pt=pt, sq=sq, c=c: e.matmul(pt[:, 0:n], lhsT=ones_b, rhs=sq[:, 0:n], start=(c == 0), stop=(c == NCH - 1)),
               [sq, CSTB], [pt])
        sd = W512.next()
        act(sd[:, 0:n], pt[:, 0:n], AF.Sqrt, [pt], [sd], bias=EPSC[:, 0:1], scale=1.0 / D)
        dve(lambda e: e.reciprocal(out=sd[:, 0:n], in_=sd[:, 0:n]), [sd], [sd])
        for c in range(NCH):
            dve(lambda e, c=c: e.scalar_tensor_tensor(out=ACT_T[:, c, t0:t0 + n], in0=xt[:, c, 0:n], scalar=gain(c), in1=sd[:, 0:n],
                                                      op0=ALU.mult, op1=ALU.mult), [xtr[c], sd, VEC, VECN], [ACTR[ti]])

    def fused_tiles(st, cur, nxt, gain, tile_fn, nb=2):
        XTP = [sbp(st, "xtp%d" % i_, [128, NCH, 512]) for i_ in range(nb)]
        XTr = [[Res("xtp%d_%d" % (i_, c)) for c in range(NCH)] for i_ in range(nb)]
        for ti, (t0, n) in enumerate(TT):
            xt = XTP[ti % nb]
            xtr = XTr[ti % nb]
            for j in range(NCH):
                src = tile_fn(ti, t0, n, j)
                xin = W512.next()
                load(xin[:, 0:n], XT[cur][j * 128:(j + 1) * 128, t0:t0 + n], [xin], r=[XR(cur, j, ti)])
                dve(lambda e, src=src, xin=xin, xt=xt, j=j, n=n: e.tensor_tensor(out=xt[:, j, 0:n], in0=src[0], in1=xin[:, 0:n], op=ALU.add),
                    list(src[1]) + [xin], [xtr[j]])
                store(XT[nxt][j * 128:(j + 1) * 128, t0:t0 + n], xt[:, j, 0:n], [xtr[j]], [XR(nxt, j, ti)], key=("st", "xtp%d" % (ti % nb)))
            if gain is not None:
                norm_tile(xt, xtr, ti, t0, n, gain)

    def wout_norm_phase(l, cur, nxt):
        st = ExitStack()
        mkws(st, 4, 2048)
        load_act(MRGD)

        def tile_fn(ti, t0, n, j):
            s, v = wload(w_out[l, :, j * 128:(j + 1) * 128], NCH, 128)
            pt = PS.next()
            mm_acc(pt[:, 0:n], [(v[:, k, :], ACT_T[:, k, t0:t0 + n]) for k in range(NCH)], [s, ACTR[ti]], [pt])
            return (pt[:, 0:n], [pt])
        fused_tiles(st, cur, nxt, lambda c: VEC[:, V_NFFN + c:V_NFFN + c + 1], tile_fn)
        P.flush()
        st.close()

    def ffn_down_norm_phase(l, cur, nxt):
        st = ExitStack()
        mkws(st, 3, 5632)
        HT = sbp(st, "ht", [128, NFF, 512], BF16)

        def tile_fn(ti, t0, n, j):
            if j == 0:
                load(HT[:, :, 0:n], HID[:, t0:t0 + n].rearrange("(c p) t -> p c t", p=128), [HT], r=[R("hid", jj) for jj in range(NFF)])
            s, v = wload(w_fd[l, :, j * 128:(j + 1) * 128], NFF, 128)
            pt = PS.next()
            mm_acc(pt[:, 0:n], [(v[:, k, :], HT[:, k, 0:n]) for k in range(NFF)], [s, HT], [pt])
            return (pt[:, 0:n], [pt])
        fused_tiles(st, cur, nxt, lambda c: VEC[:, V_NPLE + c:V_NPLE + c + 1], tile_fn, nb=1)
        P.flush()
        st.close()

    def ple_norm_phase(l, cur, nxt, next_norm):
        st = ExitStack()
        mkws(st, 4, 2048)
        WPP = sbp(st, "wpp", [128, 2, D], BF16)
        load(WPP[:, :, :], w_pp[l].rearrange("(k p) n -> p k n", p=128), [WPP], eng="pool")
        PET = sbp(st, "pet", [128, 2, T], BF16)
        PTK = Pool([sbp(st, "ptk%d" % i, [128, PLE]) for i in range(2)])
        if next_norm:
            load(VECN[:, :], vecs_d[l + 1, :, V_NMIX:V_NMIX + 16], [VECN])
        for tt in range(17):
            t0 = tt * 128
            n = min(128, T - t0)
            tk = PTK.next()
            load(tk[0:n, :], pe_tok[l, t0:t0 + n, :], [tk])
            pt = PS.next()
            for q in range(2):
                pe(lambda e, pt=pt, q=q, tk=tk, n=n: e.transpose(out=pt[:, q * 128:q * 128 + n], in_=tk[0:n, q * 128:(q + 1) * 128],
                                                                 identity=ident[0:n, 0:n]), [tk, CST], [pt])
            dve(lambda e, pt=pt, t0=t0, n=n: e.tensor_copy(out=PET[:, :, t0:t0 + n],
                                                           in_=pt[:, 0:256].rearrange("p (q t) -> p q t", q=2)[:, :, 0:n]), [pt], [PET])

        def tile_fn(ti, t0, n, j):
            s1, v1 = wload(w_pg[l, :, j * 128:(j + 1) * 128], 16, 128)
            pg = PS.next()
            mm_acc(pg[:, 0:n], [(v1[:, k, :], ACT_T[:, k, t0:t0 + n]) for k in range(16)], [s1, ACTR[ti]], [pg])
            pp = PS.next()
            mm_acc(pp[:, 0:n], [(WPP[:, k, j * 128:(j + 1) * 128], PET[:, k, t0:t0 + n]) for k in range(2)], [WPP, PET], [pp])
            sg = W512.next()
            act(sg[:, 0:n], pg[:, 0:n], AF.Sigmoid, [pg], [sg])
            dve(lambda e: e.tensor_tensor(out=sg[:, 0:n], in0=pp[:, 0:n], in1=sg[:, 0:n], op=ALU.mult), [pp, sg], [sg])
            return (sg[:, 0:n], [sg])
        fused_tiles(st, cur, nxt, (lambda c: VECN[:, c:c + 1]) if next_norm else None, tile_fn)
        P.flush()
        st.close()

    def final_phase(b):
        st = ExitStack()
        FIN = Pool([sbp(st, "fin%d" % i, [128, NCH, 128]) for i in range(2)])
        TOUT = Pool([sbp(st, "tout%d" % i, [128, D]) for i in range(2)])
        for tt in range(17):
            t0 = tt * 128
            n = min(128, T - t0)
            fi = FIN.next()
            load(fi[:, :, 0:n], XT[b][:, t0:t0 + n].rearrange("(c p) t -> p c t", p=128), [fi],
                 r=[XR(b, j, min(tt // 4, 4)) for j in range(NCH)])
            to = TOUT.next()
            for g in range(4):
                pt = PS.next()
                for q in range(4):
                    c = g * 4 + q
                    pe(lambda e, pt=pt, q=q, fi=fi, c=c, n=n: e.transpose(out=pt[0:n, q * 128:(q + 1) * 128], in_=fi[:, c, 0:n],
                                                                          identity=ident), [fi, CST], [pt])
                dve(lambda e, pt=pt, to=to, g=g, n=n: e.tensor_copy(out=to[0:n, g * 512:(g + 1) * 512], in_=pt[0:n, :]), [pt], [to])
            if tt < 16:
                store(y_p[t0:t0 + n, :], to[0:n, :], [to], final=True)
            else:
                store(y_s[:, :], to[0:n, :], [to], final=True)
        P.flush()
        st.close()

    SCL = 128.0 ** -0.5
    RAW5 = [None] * 5
    SQ5 = [None] * 5
    SD5 = Pool([None])

    def headnorm_consume(dst_bf, gcol, raw_keep=None):
        pend = []

        def tail(ti, t0, n, raw, sq):
            p2 = PS.next()
            pe(lambda e: e.matmul(p2[:, 0:n], lhsT=ones_b, rhs=sq[:, 0:n], start=True, stop=True), [sq, CSTB], [p2])
            sd = SD5.next()
            act(sd[:, 0:n], p2[:, 0:n], AF.Sqrt, [p2], [sd], bias=EPSC[:, 0:1], scale=1.0 / 128)
            dve(lambda e: e.reciprocal(out=sd[:, 0:n], in_=sd[:, 0:n]), [sd], [sd])
            if raw_keep is not None:
                dve(lambda e: e.scalar_tensor_tensor(out=raw[:, 0:n], in0=raw[:, 0:n], scalar=VEC[:, gcol:gcol + 1], in1=sd[:, 0:n],
                                                     op0=ALU.mult, op1=ALU.mult), [raw, sd, VEC], [raw])
                dve(lambda e: e.tensor_copy(out=dst_bf[:, t0:t0 + n], in_=raw[:, 0:n]), [raw], [dst_bf])
                raw_keep(ti, t0, n, raw)
            else:
                dve(lambda e: e.scalar_tensor_tensor(out=dst_bf[:, t0:t0 + n], in0=raw[:, 0:n], scalar=VEC[:, gcol:gcol + 1], in1=sd[:, 0:n],
                                                     op0=ALU.mult, op1=ALU.mult), [raw, sd, VEC], [dst_bf])

        def consume(ti, t0, n, pt):
            raw = RAW5[ti]
            act(raw[:, 0:n], pt[:, 0:n], AF.Copy, [pt], [raw])
            sq = SQ5[ti]
            act(sq[:, 0:n], pt[:, 0:n], AF.Square, [pt], [sq])
            pend.append((ti, t0, n, raw, sq))
            if ti == len(TT) - 1:
                for args in pend:
                    tail(*args)
        return consume

    def kv_out(l, h, o_p, o_s, KO):
        def keep(ti, t0, n, raw):
            if ti < 3:
                return
            pt = PS.next()
            nq = 4 if ti == 3 else 1
            w = 128 if ti == 3 else 64
            for q in range(nq):
                pe(lambda e, q=q: e.transpose(out=pt[0:w, q * 128:(q + 1) * 128], in_=raw[:, q * 128:q * 128 + w], identity=ident),
                   [raw, CST], [pt])
            ko = KO.next()
            dve(lambda e: e.tensor_copy(out=ko[0:w, 0:nq * 128], in_=pt[0:w, 0:nq * 128]), [pt], [ko])
            if ti == 3:
                store(o_p[l].rearrange("(a p) c -> p a c", p=128)[:, :, h * 128:(h + 1) * 128],
                      ko[:, :].rearrange("p (a c) -> p a c", a=4), [ko], final=True)
            else:
                store(o_s[l].rearrange("s t c -> (s t) c")[:, h * 128:(h + 1) * 128], ko[0:64, 0:128], [ko], final=True)
        return keep

    def mixer_c(l):
        st = ExitStack()
        mkws(st, 4, 2048)
        for i_ in range(5):
            RAW5[i_] = sbp(st, "raw5%d" % i_, [128, 512])
            SQ5[i_] = sbp(st, "sq5%d" % i_, [128, 512], BF16)
        SD5.items = [sbp(st, "sd5%d" % i_, [128, 512]) for i_ in range(3)]
        SD5.i = 0
        QNT = sbp(st, "qnt", [128, T], BF16)
        KNT = sbp(st, "knt", [128, T], BF16)
        VFT = sbp(st, "vft", [128, T], BF16)
        VTM = sbp(st, "vtm", [128, 17, 128], BF16)
        BP = sbp(st, "bp", [128, 5, 128])
        BSM = sbp(st, "bsm", [128, 5, 32])
        CKT = sbp(st, "ckt", [128, 2, 512], BF16)
        CKS = sbp(st, "cks", [128, 2, 4, 128], BF16)
        CVS = sbp(st, "cvs", [128, 2, 4, 128], BF16)
        OCP = Pool([sbp(st, "oc%d" % i, [128, T], BF16) for i in range(2)])
        KO = Pool([sbp(st, "ko%d" % i, [128, 512]) for i in range(3)])
        SCP = Pool([sbp(st, "sc%d" % i, [128, 640]) for i in range(2)])
        PTP = Pool([sbp(st, "ptp%d" % i, [128, 640], BF16) for i in range(2)])
        for h in range(8):
            load(BP[:, :, :], bias_p_d[l, :, h, :, :], [BP])
            load(BSM[:, :, :], bias_s_d[l, :, h, :, :], [BSM])
            for s in range(2):
                load(CKS[:, s, :, :], cache_k[l, s].rearrange("(a p) c -> p a c", p=128)[:, :, h * 128:(h + 1) * 128], [CKS], eng="pool")
                load(CVS[:, s, :, :], cache_v[l, s].rearrange("(a p) c -> p a c", p=128)[:, :, h * 128:(h + 1) * 128], [CVS], eng="pool")
            proj_chunk(w_in[l, :, 6160 + h * 128:6160 + (h + 1) * 128], headnorm_consume(QNT, V_QN))
            proj_chunk(w_in[l, :, 7184 + h * 128:7184 + (h + 1) * 128], headnorm_consume(KNT, V_KN, kv_out(l, h, o_pk, o_sk, KO)))
            vkeep = kv_out(l, h, o_pv, o_sv, KO)

            def vconsume(ti, t0, n, pt):
                raw = W512.next()
                act(raw[:, 0:n], pt[:, 0:n], AF.Copy, [pt], [raw])
                dve(lambda e: e.tensor_copy(out=VFT[:, t0:t0 + n], in_=raw[:, 0:n]), [raw], [VFT])
                vkeep(ti, t0, n, raw)
                pb = PSB.next()
                nq = 4 if ti < 4 else 1
                w = 128 if ti < 4 else 64
                for q in range(nq):
                    pe(lambda e, q=q: e.transpose(out=pb[0:w, q * 128:(q + 1) * 128], in_=VFT[:, t0 + q * 128:t0 + q * 128 + w], identity=ident_b),
                       [VFT, CSTB], [pb])
                dve(lambda e: e.tensor_copy(out=VTM[0:w, 4 * ti:4 * ti + nq, :], in_=pb[0:w, 0:nq * 128].rearrange("p (a c) -> p a c", a=nq)),
                    [pb], [VTM])
            proj_chunk(w_in[l, :, 8208 + h * 128:8208 + (h + 1) * 128], vconsume)
            for s in range(2):
                pb = PSB.next()
                for a in range(4):
                    pe(lambda e, pb=pb, s=s, a=a: e.transpose(out=pb[:, a * 128:(a + 1) * 128], in_=CKS[:, s, a, :], identity=ident_b),
                       [CKS, CSTB], [pb])
                dve(lambda e, pb=pb, s=s: e.tensor_copy(out=CKT[:, s, :], in_=pb[:, 0:512]), [pb], [CKT])
            oc = OCP.next()
            ptts = {}

            def att_A(m):
                i0 = max(0, 4 - m)
                pa = PS.next()
                pbk = PS.next()
                for i in range(i0, 5):
                    kt = m - 4 + i
                    dstp = pa[:, i * 128:(i + 1) * 128] if i < 4 else pbk[:, 0:128]
                    pe(lambda e, dstp=dstp, kt=kt, m=m: e.matmul(dstp, lhsT=KNT[:, kt * 128:(kt + 1) * 128], rhs=QNT[:, m * 128:(m + 1) * 128],
                                                              start=True, stop=True), [KNT, QNT], [pa if i < 4 else pbk])
                sc = SCP.next()
                if i0 < 4:
                    dve(lambda e, sc=sc, pa=pa, i0=i0: e.scalar_tensor_tensor(
                        out=sc[:, i0 * 128:512], in0=pa[:, i0 * 128:512], scalar=SCL,
                        in1=BP[:, i0:4, :].rearrange("p a q -> p (a q)"), op0=ALU.mult, op1=ALU.add), [pa, BP], [sc])
                dve(lambda e, sc=sc, pbk=pbk: e.scalar_tensor_tensor(out=sc[:, 512:640], in0=pbk[:, 0:128], scalar=SCL, in1=BP[:, 4, :],
                                                                     op0=ALU.mult, op1=ALU.add), [pbk, BP], [sc])
                ptt = PTP.next()
                act(ptt[:, i0 * 128:640], sc[:, i0 * 128:640], AF.Exp, [sc], [ptt])
                ptts[m] = (ptt, i0)

            def att_B(m):
                ptt, i0 = ptts.pop(m)
                po = PS.next()
                mm_acc(po[:, 0:128], [(VTM[:, m - 4 + i, :], ptt[:, i * 128:(i + 1) * 128]) for i in range(i0, 5)], [VTM, ptt], [po])
                psm = PS.next()
                mm_acc(psm[:, 0:128], [(ones_b, ptt[:, i * 128:(i + 1) * 128]) for i in range(i0, 5)], [CSTB, ptt], [psm])
                rec = W512.next()
                dve(lambda e, rec=rec, psm=psm: e.reciprocal(out=rec[:, 0:128], in_=psm[:, 0:128]), [psm], [rec])
                dve(lambda e, oc=oc, po=po, rec=rec, m=m: e.tensor_tensor(out=oc[:, m * 128:(m + 1) * 128], in0=po[:, 0:128], in1=rec[:, 0:128],
                                                                          op=ALU.mult), [po, rec], [oc])
            att_A(0)
            for m in range(16):
                if m + 1 < 16:
                    att_A(m + 1)
                att_B(m)
            for s in range(2):
                tq = 2048 + 32 * s
                pa = PS.next()
                for a in range(4):
                    pe(lambda e, pa=pa, a=a, s=s, tq=tq: e.matmul(pa[:, a * 32:(a + 1) * 32], lhsT=CKT[:, s, a * 128:(a + 1) * 128],
                                                                  rhs=QNT[:, tq:tq + 32], start=True, stop=True), [CKT, QNT], [pa])
                pe(lambda e, pa=pa, s=s, tq=tq: e.matmul(pa[32 * s:32 * s + 32, 128:160], lhsT=KNT[:, tq:tq + 32], rhs=QNT[:, tq:tq + 32],
                                                         start=True, stop=True), [KNT, QNT], [pa])
                sc = SCP.next()
                dve(lambda e, sc=sc, pa=pa: e.scalar_tensor_tensor(out=sc[:, 0:128], in0=pa[:, 0:128], scalar=SCL,
                                                                   in1=BSM[:, 0:4, :].rearrange("p a q -> p (a q)"), op0=ALU.mult, op1=ALU.add),
                    [pa, BSM], [sc])
                dve(lambda e, sc=sc, pa=pa, s=s: e.scalar_tensor_tensor(out=sc[32 * s:32 * s + 32, 128:160], in0=pa[32 * s:32 * s + 32, 128:160],
                                                                        scalar=SCL, in1=BSM[32 * s:32 * s + 32, 4, :], op0=ALU.mult, op1=ALU.add),
                    [pa, BSM, sc], [sc])
                ptt = PTP.next()
                act(ptt[:, 0:128], sc[:, 0:128], AF.Exp, [sc], [ptt])
                act(ptt[32 * s:32 * s + 32, 128:160], sc[32 * s:32 * s + 32, 128:160], AF.Exp, [sc, ptt], [ptt])
                po = PS.next()
                prs = [(CVS[:, s, a, :], ptt[:, a * 32:(a + 1) * 32]) for a in range(4)]
                prs.append((VTM[32 * s:32 * s + 32, 16, :], ptt[32 * s:32 * s + 32, 128:160]))
                mm_acc(po[:, 0:32], prs, [CVS, VTM, ptt], [po])
                psm = PS.next()
                prs2 = [(ones_b, ptt[:, a * 32:(a + 1) * 32]) for a in range(4)]
                prs2.append((CSTB[32 * s:32 * s + 32, 128:256], ptt[32 * s:32 * s + 32, 128:160]))
                mm_acc(psm[:, 0:32], prs2, [CSTB, ptt], [psm])
                rec = W512.next()
                dve(lambda e, rec=rec, psm=psm: e.reciprocal(out=rec[:, 0:32], in_=psm[:, 0:32]), [psm], [rec])
                dve(lambda e, oc=oc, po=po, rec=rec, tq=tq: e.tensor_tensor(out=oc[:, tq:tq + 32], in0=po[:, 0:32], in1=rec[:, 0:32], op=ALU.mult),
                    [po, rec], [oc])
            store(OUTS[(16 + h) * 128:(17 + h) * 128, :], oc[:, :], [oc], [R("outs", 16 + h)])
        P.flush()
        st.close()

    DZS = dscr("dzs", [T, 1024])
    CHK = [(n * 64, 64, 0) for n in range(32)] + [(2048, 32, 1), (2080, 32, 2)]
    NCK = len(CHK)

    def mixer_b(l):
        st = ExitStack()
        B = PS.items
        BB = PSB.items
        RXB = sbp(st, "rxb", [128, CB])
        QNT = sbp(st, "qnt", [128, T], BF16)
        KNT = sbp(st, "knt", [128, T], BF16)
        VCB = sbp(st, "vcb", [128, T], BF16)
        OBT = sbp(st, "obt", [128, T], BF16)
        GBT = sbp(st, "gbt", [64, NCK, 24])
        BEG = sbp(st, "beg", [64, NCK, 8])
        EKD = sbp(st, "ekd", [64, NCK, 8])
        EGL = sbp(st, "egl", [128, NCK, 8])
        NEGA = sbp(st, "nega", [128, 8])
        SF = sbp(st, "sf", [128, 3, 128])
        SFr = [Res("sfr%d" % i) for i in range(3)]
        GSU = sbp(st, "gsu", [64, NCK * 16])
        GS = Tn(GSU.h[:, :].rearrange("p (c x) -> p c x", x=16), "gsu")
        GS.res = GSU.res
        GUW = Tn(GSU.h[:, 0:512].rearrange("p (w c) -> p w c", w=8), "gsu")
        GUW.res = GSU.res
        GBr = Res("gbr")

        st0 = ExitStack()
        mkws(st0, 3, 8192)
        act(NEGA[:, :], VEC[:, V_ALOG:V_ALOG + 8], AF.Exp, [VEC], [NEGA])
        dve(lambda e: e.tensor_scalar(out=NEGA[:, :], in0=NEGA[:, :], scalar1=-1.0, scalar2=None, op0=ALU.mult), [NEGA], [NEGA])
        sab, vab = wload(w_in[l, :, 6144:6160], 16, 16)
        GRP = [(0, 32, 64), (32, 2, 32)]
        for gi, (c0, ncx, C) in enumerate(GRP):
            bk = B[gi]
            for i in range(ncx):
                t0 = CHK[c0 + i][0]
                mm_acc(bk[0:C, i * 16:(i + 1) * 16], [(ACT_T[:, k, t0:t0 + C], vab[:, k, :]) for k in range(16)], [sab] + ACTR, [bk])
            pv3 = bk[0:C, 0:ncx * 16].rearrange("p (c x) -> p c x", x=16)
            tm = GS[0:C, c0:c0 + ncx, 0:8]
            dve(lambda e, tm=tm, pv3=pv3, C=C, ncx=ncx: e.tensor_tensor(
                out=tm, in0=pv3[:, :, 0:8], in1=VEC[0:C, V_DTB:V_DTB + 8].unsqueeze(1).broadcast_to([C, ncx, 8]), op=ALU.add),
                [bk, VEC], [GS])
            act(tm, tm, AF.Exp, [GS], [GS])
            act(tm, tm, AF.Ln, [GS], [GS], bias=EPSC[0:C, 1:2], scale=1.0)
            dve(lambda e, tm=tm, C=C, ncx=ncx, c0=c0: e.tensor_tensor(
                out=GBT[0:C, c0:c0 + ncx, 0:8], in0=tm, in1=NEGA[0:C, :].unsqueeze(1).broadcast_to([C, ncx, 8]), op=ALU.mult),
                [GS, NEGA], [GBr])
            act(GBT[0:C, c0:c0 + ncx, 8:16], pv3[:, :, 8:16], AF.Sigmoid, [bk], [GBr])
            dve(lambda e, C=C, ncx=ncx, c0=c0: e.tensor_scalar(out=GBT[0:C, c0:c0 + ncx, 16:24], in0=GBT[0:C, c0:c0 + ncx, 8:16],
                                                              scalar1=-1.0, scalar2=None, op0=ALU.mult), [GBr], [GBr])
        for gi, (c0, ncx, C) in enumerate(GRP):
            bk = B[2 + gi]
            bk2 = B[4 + gi]
            for i in range(ncx):
                ci = c0 + i
                pe(lambda e, bk=bk, C=C, ci=ci, i=i: e.matmul(bk[0:C, i * 16:i * 16 + 8], lhsT=CST[0:C, C_UPI:C_UPI + C], rhs=GBT[0:C, ci, 0:8],
                                                             start=True, stop=True), [CST, GBr], [bk])
                pe(lambda e, bk=bk, C=C, ci=ci, i=i: e.matmul(bk[0:C, i * 16 + 8:i * 16 + 16], lhsT=CST[0:C, C_ONE:C_ONE + C], rhs=GBT[0:C, ci, 0:8],
                                                             start=True, stop=True), [CST, GBr], [bk])
                pe(lambda e, bk2=bk2, C=C, ci=ci, i=i: e.matmul(bk2[:, i * 8:i * 8 + 8], lhsT=CST[0:C, C_ONE:C_ONE + 128], rhs=GBT[0:C, ci, 0:8],
                                                               start=True, stop=True), [CST, GBr], [bk2])
            gsv = GS[0:C, c0:c0 + ncx, :]
            act(gsv, bk[0:C, 0:ncx * 16].rearrange("p (c x) -> p c x", x=16), AF.Copy, [bk], [GS])
            act(EGL[:, c0:c0 + ncx, :], bk2[:, 0:ncx * 8].rearrange("p (c x) -> p c x", x=8), AF.Exp, [bk2], [GBr])
            dve(lambda e, gsv=gsv: e.tensor_tensor(out=gsv[:, :, 8:16], in0=gsv[:, :, 8:16], in1=gsv[:, :, 0:8], op=ALU.subtract), [GS], [GS])
            act(EKD[0:C, c0:c0 + ncx, :], gsv[:, :, 8:16], AF.Exp, [GS], [GBr])
            act(gsv[:, :, 0:8], gsv[:, :, 0:8], AF.Exp, [GS], [GS])
            dve(lambda e, gsv=gsv, C=C, ncx=ncx, c0=c0: e.tensor_tensor(out=BEG[0:C, c0:c0 + ncx, :], in0=gsv[:, :, 0:8],
                                                                        in1=GBT[0:C, c0:c0 + ncx, 8:16], op=ALU.mult), [GS, GBr], [GBr])
        wz = [wload(w_in[l, :, 5120 + hf * 512:5120 + (hf + 1) * 512], 16, 512) for hf in range(2)]
        for tt in range(17):
            t0 = tt * 128
            C = min(128, T - t0)
            for hf in range(2):
                pt = PS.next()
                mm_acc(pt[0:C, :], [(ACT_T[:, k, t0:t0 + C], wz[hf][1][:, k, :]) for k in range(16)], [wz[hf][0]] + ACTR, [pt])
                zt = W512.next()
                act(zt[0:C, :], pt[0:C, :], AF.Silu, [pt], [zt])
                store(DZS[t0:t0 + C, hf * 512:(hf + 1) * 512], zt[0:C, :], [zt], [R("dzs128", tt)])

        P.flush()
        st0.close()
        mkws(st, 4, 2048)
        CVP = Pool([sbp(st, "cvp%d" % i_, [128, 512]) for i_ in range(10)])
        EGBW = sbp(st, "egbw", [128, 8, 64], BF16)
        P0T = sbp(st, "p0t", [64, 8, 64])

        def dbl(name, shape, dt, nb=2, nr=4):
            t = [sbp(st, "%s%d" % (name, i), shape, dt) for i in range(nb)]
            r = [[Res("%s_%d_%d" % (name, i, q)) for q in range(nr)] for i in range(nb)]
            if nb == 1:
                t = t * 2
                r = r * 2
            return t, r
        EDW, EDr = dbl("edw", [64, 8, 128], F32, 1, 2)
        QGW, QGr = dbl("qgw", [128, 8, 64], BF16, 2, 1)
        QKTW, QKr = dbl("qktw", [64, 8, 64], BF16, 2, 2)
        KVW, KVr = dbl("kvw", [64, 8, 256], BF16, 1, 2)
        KDEW, KDr = dbl("kdew", [64, 8, 128], BF16, 1, 2)
        NWTW, NWr = dbl("nwtw", [128, 8, 64], BF16, 2, 1)
        NYW, NYr = dbl("nyw", [64, 8, 128], BF16, 1, 4)
        PBW, PBr = dbl("pbw", [64, 8, 64], BF16, 1, 4)
        WUW, WUr = dbl("wuw", [64, 8, 256], BF16, 2, 4)
        MTW, MTr = dbl("mtw", [128, 8, 128], BF16, 2, 4)
        BCW, BCr = dbl("bcw", [128, 8, 128], BF16, 2, 4)
        SBT = sbp(st, "sbt", [128, 2, 128], BF16)
        SBTr = [Res("sbtr0"), Res("sbtr1")]
        SM = Pool([sbp(st, "sm%d" % i, [64, 128]) for i in range(6)])
        SMB = Pool([sbp(st, "smb%d" % i, [64, 128], BF16) for i in range(6)])
        SC1 = Pool([sbp(st, "sc1%d" % i, [64, 2]) for i in range(4)])
        WAVES = [(w * 8, 8, 64) for w in range(4)] + [(32, 2, 32)]

        def do_head(h):
            def q_proj(which):
                c0 = which * 1024 + h * 128
                conv_state_io(RXB, l, st_conv_b, o_conv_b, c0)
                proj_chunk(w_in[l, :, 2048 + c0:2048 + c0 + 128], evac_to_convbuf(RXB))
                conv_state_out(RXB, l, o_conv_b, c0)

            def q_conv(which, dst):
                cvs = []
                for ti, (t0, n) in enumerate(TT):
                    cv = conv_tile(RXB, V_WCB + which * 8 + h, 24, ti, t0, n, pool=CVP)
                    cvs.append(cv)
                return cvs

            def q_tail(which, dst, cvs):
                for ti, (t0, n) in enumerate(TT):
                    cv = cvs[ti]
                    if which == 2:
                        act(dst[:, t0:t0 + n], cv[:, 0:n], AF.Silu, [cv], [dst])
                        continue
                    act(cv[:, 0:n], cv[:, 0:n], AF.Silu, [cv], [cv])
                    sq_ = B512.next()
                    act(sq_[:, 0:n], cv[:, 0:n], AF.Square, [cv], [sq_])
                    p2 = PS.next()
                    pe(lambda e, p2=p2, sq_=sq_, n=n: e.matmul(p2[:, 0:n], lhsT=ones_b, rhs=sq_[:, 0:n], start=True, stop=True), [sq_, CSTB], [p2])
                    sd = W512.next()
                    act(sd[:, 0:n], p2[:, 0:n], AF.Sqrt, [p2], [sd], bias=EPSC[:, 0:1], scale=1.0)
                    dve(lambda e, sd=sd, n=n: e.reciprocal(out=sd[:, 0:n], in_=sd[:, 0:n]), [sd], [sd])
                    dve(lambda e, sd=sd, cv=cv, t0=t0, n=n: e.scalar_tensor_tensor(
                        out=dst[:, t0:t0 + n], in0=cv[:, 0:n], scalar=(SCL if which == 0 else 1.0), in1=sd[:, 0:n],
                        op0=ALU.mult, op1=ALU.mult), [cv, sd], [dst])
            q_proj(0)
            cq = q_conv(0, QNT)
            q_proj(1)
            q_tail(0, QNT, cq)
            ck = q_conv(1, KNT)
            q_proj(2)
            q_tail(1, KNT, ck)
            cvv = q_conv(2, VCB)
            q_tail(2, VCB, cvv)
            for s in range(2):
                load(SF[:, 1 + s, :], st_S[l, s, h, :, :], [SFr[1 + s]], key=("ld", "sf%d" % s))
            dve(lambda e: e.memset(SF[:, 0, :], 0.0), [], [SFr[0]])
            dve(lambda e: e.memset(SBT[:, 0, :], 0.0), [], [SBTr[0]])

            def stage1(wi):
                c0, W, C = WAVES[wi]
                par = wi % 2
                tq0 = CHK[c0][0]
                steps = []
                bc = lambda ap, n_: ap.unsqueeze(1).broadcast_to([C, n_, C])
                NH = (W + 3) // 4
                hv = [(q * 4, min(4, W - q * 4)) for q in range(NH)]
                NP_ = (W + 1) // 2
                pr = [(p * 2, min(2, W - p * 2)) for p in range(NP_)]
                allr = lambda rl: list(rl)

                def v4(ap, n_):
                    return ap.rearrange("p (w a b) -> p w a b", w=n_, a=2)[:, :, :, 0:C]

                def sA():
                    dve(lambda e: e.tensor_tensor(out=GUW[0:C, 0:W, 0:C], in0=bc(CST[0:C, C_UPI:C_UPI + C], W),
                                                  in1=GBT[0:C, c0:c0 + W, h:h + 1].broadcast_to([C, W, C]), op=ALU.mult), [CST, GBr], [GUW])
                    for sl in range(W):
                        bk = B[sl // 4]
                        o = (sl % 4) * 128
                        pe(lambda e, bk=bk, o=o, sl=sl: e.matmul(bk[0:C, o:o + C], lhsT=CST[0:C, C_LOS:C_LOS + C], rhs=GUW[0:C, sl, 0:C],
                                                                  start=True, stop=True), [CST, GUW], [bk])
                        pe(lambda e, bk=bk, o=o, sl=sl: e.matmul(bk[0:C, o + 64:o + 64 + C], lhsT=GUW[0:C, sl, 0:C], rhs=CST[0:C, C_LOS:C_LOS + C],
                                                                  start=True, stop=True), [CST, GUW], [bk])
                        pe(lambda e, sl=sl: e.matmul(B[2][:, sl * 64:sl * 64 + C], lhsT=CST[0:C, C_ONE:C_ONE + 128], rhs=GUW[0:C, sl, 0:C],
                                                      start=True, stop=True), [CST, GUW], [B[2]])
                steps.append(sA)

                def sB():
                    for q, (s0, n_) in enumerate(hv):
                        edv = EDW[par][0:C, s0:s0 + n_, :].rearrange("p w (a b) -> p w a b", a=2)[:, :, :, 0:C]
                        act(edv, v4(B[q][0:C, 0:n_ * 128], n_), AF.Exp, [B[q]], [EDr[par][q]])
                    edv = EDW[par][0:C, 0:W, :].rearrange("p w (a b) -> p w a b", a=2)[:, :, :, 0:C]
                    dve(lambda e: e.tensor_tensor(
                        out=edv, in0=edv, in1=CST[0:C, C_UPI:C_UPI + 128].rearrange("p (a b) -> p a b", a=2)[:, :, 0:C]
                        .unsqueeze(1).broadcast_to([C, W, 2, C]), op=ALU.mult), allr(EDr[par]) + [CST], allr(EDr[par]))
                    egv = EGBW[:, 0:W, 0:C]
                    act(egv, B[2][:, 0:W * 64].rearrange("p (w c) -> p w c", w=W)[:, :, 0:C], AF.Exp, [B[2]], [EGBW])
                    dve(lambda e: e.tensor_tensor(out=QGW[par][:, 0:W, 0:C], in0=QNT[:, tq0:tq0 + W * C].rearrange("p (w c) -> p w c", w=W),
                                                  in1=egv, op=ALU.mult), [QNT, EGBW], [QGr[par][0]])
                    for sl in range(W):
                        t0 = tq0 + sl * C
                        bk = B[sl // 4]
                        o = (sl % 4) * 128
                        pe(lambda e, bk=bk, o=o, t0=t0: e.matmul(bk[0:C, o:o + C], lhsT=KNT[:, t0:t0 + C], rhs=QNT[:, t0:t0 + C], start=True, stop=True),
                           [KNT, QNT], [bk])
                        pe(lambda e, bk=bk, o=o, t0=t0: e.matmul(bk[0:C, o + 64:o + 64 + C], lhsT=KNT[:, t0:t0 + C], rhs=KNT[:, t0:t0 + C], start=True, stop=True),
                           [KNT], [bk])
                steps.append(sB)

                def sC():
                    for q, (s0, n_) in enumerate(hv):
                        kv = v4(B[q][0:C, 0:n_ * 128], n_)
                        edv = EDW[par][0:C, s0:s0 + n_, :].rearrange("p w (a b) -> p w a b", a=2)[:, :, :, 0:C]
                        dve(lambda e, kv=kv, edv=edv, s0=s0, n_=n_: e.tensor_tensor(out=QKTW[par][0:C, s0:s0 + n_, 0:C], in0=kv[:, :, 0, :], in1=edv[:, :, 0, :],
                                                                                  op=ALU.mult), [B[q], EDr[par][q]], [QKr[par][q]])
                        dve(lambda e, kv=kv, edv=edv, s0=s0, n_=n_: e.tensor_tensor(out=P0T[0:C, s0:s0 + n_, 0:C], in0=kv[:, :, 1, :], in1=edv[:, :, 1, :],
                                                                                  op=ALU.mult), [B[q], EDr[par][q]], [P0T])
                    dve(lambda e: e.tensor_tensor(out=PBW[par][0:C, 0:W, 0:C], in0=P0T[0:C, 0:W, 0:C],
                                                  in1=GBT[0:C, c0:c0 + W, 16 + h:17 + h].broadcast_to([C, W, C]), op=ALU.mult),
                        [P0T, GBr], allr(PBr[par]))
                    for sl in range(W):
                        pe(lambda e, sl=sl: e.transpose(out=BB[0][0:C, sl * 64:sl * 64 + C], in_=PBW[par][0:C, sl, 0:C], identity=ident_b[0:C, 0:C]),
                           [PBr[par][sl // 2], CSTB], [BB[0]])
                    for sl in range(min(4, W)):
                        t0 = tq0 + sl * C
                        o = sl * 256
                        pe(lambda e, o=o, t0=t0: e.transpose(out=BB[1][0:C, o:o + 128], in_=KNT[:, t0:t0 + C], identity=ident_b), [KNT, CSTB], [BB[1]])
                        pe(lambda e, o=o, t0=t0: e.transpose(out=BB[1][0:C, o + 128:o + 256], in_=VCB[:, t0:t0 + C], identity=ident_b), [VCB, CSTB], [BB[1]])
                steps.append(sC)

                def kvb(bk, s0, n_, q):
                    kvv = bk[0:C, 0:n_ * 256].rearrange("p (w a d) -> p w a d", w=n_, a=2)
                    dve(lambda e: e.tensor_tensor(out=KVW[par][0:C, s0:s0 + n_, 0:128], in0=kvv[:, :, 0, :],
                                                  in1=BEG[0:C, c0 + s0:c0 + s0 + n_, h:h + 1].broadcast_to([C, n_, 128]), op=ALU.mult), [bk, GBr], [KVr[par][q]])
                    dve(lambda e: e.tensor_tensor(out=KDEW[par][0:C, s0:s0 + n_, :], in0=kvv[:, :, 0, :],
                                                  in1=EKD[0:C, c0 + s0:c0 + s0 + n_, h:h + 1].broadcast_to([C, n_, 128]), op=ALU.mult), [bk, GBr], [KDr[par][q]])
                    dve(lambda e: e.tensor_tensor(out=KVW[par][0:C, s0:s0 + n_, 128:256], in0=kvv[:, :, 1, :],
                                                  in1=GBT[0:C, c0 + s0:c0 + s0 + n_, 8 + h:9 + h].broadcast_to([C, n_, 128]), op=ALU.mult), [bk, GBr], [KVr[par][q]])

                def sD():
                    nv = BB[0][0:C, 0:W * 64].rearrange("p (w c) -> p w c", w=W)[:, :, 0:C]
                    act(NYW[par][0:C, 0:W, 0:C], nv, AF.Copy, [BB[0]], allr(NYr[par]))
                    dve(lambda e: e.tensor_tensor(out=NYW[par][0:C, 0:W, 64:64 + C], in0=nv, in1=bc(ident_b[0:C, 0:C], W), op=ALU.add),
                        [BB[0], CSTB], allr(NYr[par]))
                    kvb(BB[1], 0, min(4, W), 0)
                    if W > 4:
                        for sl in range(4, W):
                            t0 = tq0 + sl * C
                            o = (sl - 4) * 256
                            pe(lambda e, o=o, t0=t0: e.transpose(out=BB[0][0:C, o:o + 128], in_=KNT[:, t0:t0 + C], identity=ident_b), [KNT, CSTB], [BB[0]])
                            pe(lambda e, o=o, t0=t0: e.transpose(out=BB[0][0:C, o + 128:o + 256], in_=VCB[:, t0:t0 + C], identity=ident_b), [VCB, CSTB], [BB[0]])
                        kvb(BB[0], 4, W - 4, 1)
                steps.append(sD)

                def level(lev):
                    def f():
                        for p_, (s0, n2) in enumerate(pr):
                            bk = B[p_ % 3]
                            for j2 in range(n2):
                                sl = s0 + j2
                                o = j2 * 192
                                pe(lambda e, bk=bk, o=o, sl=sl: e.matmul(bk[0:C, o:o + 128], lhsT=PBW[par][0:C, sl, 0:C], rhs=NYW[par][0:C, sl, :],
                                                                          start=True, stop=True), [PBr[par][p_], NYr[par][p_]], [bk])
                                pe(lambda e, bk=bk, o=o, sl=sl: e.matmul(bk[0:C, o + 128:o + 128 + C], lhsT=NYW[par][0:C, sl, 0:C], rhs=PBW[par][0:C, sl, 0:C],
                                                                          start=True, stop=True), [PBr[par][p_], NYr[par][p_]], [bk])
                            pvw = bk[0:C, 0:n2 * 192].rearrange("p (w x) -> p w x", w=n2)
                            if lev > 0:
                                dve(lambda e, pvw=pvw, s0=s0, n2=n2: e.tensor_tensor(out=NYW[par][0:C, s0:s0 + n2, 64:64 + C], in0=pvw[:, :, 64:64 + C],
                                                                                    in1=NYW[par][0:C, s0:s0 + n2, 64:64 + C], op=ALU.add),
                                    [bk, NYr[par][p_]], [NYr[par][p_]])
                            act(NYW[par][0:C, s0:s0 + n2, 0:C], pvw[:, :, 0:C], AF.Copy, [bk], [NYr[par][p_]])
                            act(PBW[par][0:C, s0:s0 + n2, 0:C], pvw[:, :, 128:128 + C], AF.Copy, [bk], [PBr[par][p_]])
                    return f
                for lev in range(6):
                    steps.append(level(lev))

                def sE():
                    for p_, (s0, n2) in enumerate(pr):
                        bk = B[p_ % 3]
                        for j2 in range(n2):
                            sl = s0 + j2
                            pe(lambda e, bk=bk, j2=j2, sl=sl: e.matmul(bk[0:C, j2 * 256:(j2 + 1) * 256], lhsT=NYW[par][0:C, sl, 64:64 + C], rhs=KVW[par][0:C, sl, :],
                                                                        start=True, stop=True), [NYr[par][p_], KVr[par][sl // 4]], [bk])
                        act(WUW[par][0:C, s0:s0 + n2, :], bk[0:C, 0:n2 * 256].rearrange("p (w x) -> p w x", w=n2), AF.Copy, [bk], [WUr[par][p_]])
                    bkw = B[NP_ % 3]
                    for sl in range(W):
                        pe(lambda e, sl=sl: e.matmul(bkw[:, sl * 64:sl * 64 + C], lhsT=KVW[par][0:C, sl, 0:128], rhs=NYW[par][0:C, sl, 64:64 + C],
                                                      start=True, stop=True), [KVr[par][sl // 4], NYr[par][sl // 2]], [bkw])
                    act(NWTW[par][:, 0:W, 0:C], bkw[:, 0:W * 64].rearrange("p (w c) -> p w c", w=W)[:, :, 0:C], AF.Copy, [bkw], [NWr[par][0]], scale=-1.0)
                steps.append(sE)

                def sF():
                    for p_, (s0, n2) in enumerate(pr):
                        bk = B[(p_ + NP_ + 1) % 3]
                        for j2 in range(n2):
                            sl = s0 + j2
                            pe(lambda e, bk=bk, j2=j2, sl=sl: e.matmul(bk[:, j2 * 256:j2 * 256 + 128], lhsT=WUW[par][0:C, sl, 0:128], rhs=KDEW[par][0:C, sl, :],
                                                                        start=True, stop=True), [WUr[par][p_], KDr[par][sl // 4]], [bk])
                            pe(lambda e, bk=bk, j2=j2, sl=sl: e.matmul(bk[:, j2 * 256 + 128:j2 * 256 + 256], lhsT=KDEW[par][0:C, sl, :], rhs=WUW[par][0:C, sl, 128:256],
                                                                        start=True, stop=True), [WUr[par][p_], KDr[par][sl // 4]], [bk])
                        bv = bk[:, 0:n2 * 256].rearrange("p (w a d) -> p w a d", w=n2, a=2)
                        o1 = act(MTW[par][:, s0:s0 + n2, :], bv[:, :, 0, :], AF.Copy, [bk], [MTr[par][p_]], scale=-1.0)
                        P.op("dve", lambda e, bv=bv, s0=s0, n2=n2: e.tensor_copy(out=BCW[par][:, s0:s0 + n2, :], in_=bv[:, :, 1, :]),
                             r=rs([bk]), w=rs([BCr[par][p_]]), extra=[o1])
                steps.append(sF)
                return steps

            kcount = [0]

            def stage2(wi, sl):
                c0, W, C = WAVES[wi]
                par = wi % 2
                ci = c0 + sl
                t0, C, sq = CHK[ci]
                if sq == 0:
                    ko = kcount[0] % 2
                    kcount[0] += 1
                else:
                    ko = 0
                    act(SBT[:, 0, :], SF[:, sq, :], AF.Copy, [SFr[sq]], [SBTr[0]])
                kn = 1 - ko
                by = B[4 + ci % 2]
                bx = B[3]
                pe(lambda e: e.matmul(by[0:C, 0:128], lhsT=NWTW[par][:, sl, 0:C], rhs=SBT[:, ko, :], start=True, stop=False),
                   [NWr[par][0], SBTr[ko]], [by])
                pe(lambda e: e.matmul(by[0:C, 0:128], lhsT=ident_b[0:C, 0:C], rhs=WUW[par][0:C, sl, 128:256], start=False, stop=True),
                   [WUr[par][sl // 2], CSTB], [by])
                pe(lambda e: e.matmul(bx[:, 0:128], lhsT=MTW[par][:, sl, :], rhs=SBT[:, ko, :], start=True, stop=False),
                   [MTr[par][sl // 2], SBTr[ko]], [bx])
                pe(lambda e: e.matmul(bx[:, 0:128], lhsT=ident_b, rhs=BCW[par][:, sl, :], start=False, stop=True),
                   [BCr[par][sl // 2], CSTB], [bx])
                dve(lambda e: e.scalar_tensor_tensor(out=SBT[:, kn, :], in0=SF[:, sq, :], scalar=EGL[:, ci, h:h + 1], in1=bx[:, 0:128],
                                                     op0=ALU.mult, op1=ALU.add), [bx, GBr, SFr[sq]], [SBTr[kn]])
                dve(lambda e: e.scalar_tensor_tensor(out=SF[:, sq, :], in0=SF[:, sq, :], scalar=EGL[:, ci, h:h + 1], in1=bx[:, 0:128],
                                                     op0=ALU.mult, op1=ALU.add), [bx, GBr, SFr[sq]], [SFr[sq]])
                vn = SMB.next()
                act(vn[0:C, :], by[0:C, 0:128], AF.Copy, [by], [vn])
                pe(lambda e: e.matmul(by[0:C, 128:256], lhsT=QGW[par][:, sl, 0:C], rhs=SBT[:, ko, :], start=True, stop=False),
                   [QGr[par][0], SBTr[ko]], [by])
                pe(lambda e: e.matmul(by[0:C, 128:256], lhsT=QKTW[par][0:C, sl, 0:C], rhs=vn[0:C, :], start=False, stop=True),
                   [QKr[par][sl // 4], vn], [by])
                junk = SMB.next()
                s1 = SC1.next()
                act(junk[0:C, :], by[0:C, 128:256], AF.Square, [by], [junk, s1], accum=s1[0:C, 0:1])
                act(s1[0:C, 1:2], s1[0:C, 0:1], AF.Sqrt, [s1], [s1], bias=EPSC[0:C, 0:1], scale=1.0 / 128)
                dve(lambda e: e.reciprocal(out=s1[0:C, 1:2], in_=s1[0:C, 1:2]), [s1], [s1])
                on = SM.next()
                dve(lambda e: e.scalar_tensor_tensor(out=on[0:C, :], in0=by[0:C, 128:256], scalar=s1[0:C, 1:2],
                                                     in1=VEC[0:C, V_DNN:V_DNN + 128], op0=ALU.mult, op1=ALU.mult), [by, s1, VEC], [on])
                dz = SM.next()
                load(dz[0:C, :], DZS[t0:t0 + C, h * 128:(h + 1) * 128], [dz], r=[R("dzs128", t0 // 128)])
                dve(lambda e: e.tensor_tensor(out=on[0:C, :], in0=on[0:C, :], in1=dz[0:C, :], op=ALU.mult), [on, dz], [on])
                pe(lambda e: e.transpose(out=by[:, 256:256 + C], in_=on[0:C, :], identity=ident[0:C, 0:C]), [on, CST], [by])
                act(OBT[:, t0:t0 + C], by[:, 256:256 + C], AF.Copy, [by], [OBT])

            nw = len(WAVES)
            for s in stage1(0):
                s()
            for wi in range(nw):
                nxt = stage1(wi + 1) if wi + 1 < nw else []
                W = WAVES[wi][1]
                ns = len(nxt)
                per = (ns + W - 1) // W if W else ns
                k = 0
                for i in range(W):
                    stage2(wi, i)
                    for _ in range(per):
                        if k < ns:
                            nxt[k]()
                            k += 1
                while k < ns:
                    nxt[k]()
                    k += 1
            for sq in range(3):
                store(o_S[l, sq, h, :, :], SF[:, sq, :], [SFr[sq]], final=True, key=("st", "sf%d" % sq))
            store(OUTS[(8 + h) * 128:(9 + h) * 128, :], OBT[:, :], [OBT], [R("outs", 8 + h)])
        for h_ in range(8):
            do_head(h_)
        P.flush()
        st.close()


    dve(lambda e: e.memset(VECN[:, :], 1.0), [], [VECN])
    phase0()
    cur = 0
    for l in range(nlayers):
        load(VEC[:, :], vecs_d[l], [VEC])
        if l == 0:
            norm_phase(cur, l, V_NMIX)
        mixer_a(l)
        if "nob" not in dbg:
            mixer_b(l)
        if "noc" not in dbg:
            mixer_c(l)
        merge_phase(l, cur, 1 - cur)
        wout_norm_phase(l, cur, 1 - cur)
        cur = 1 - cur
        ffn_phase(l, cur, 1 - cur)
        ffn_down_norm_phase(l, cur, 1 - cur)
        cur = 1 - cur
        ple_norm_phase(l, cur, 1 - cur, l + 1 < nlayers)
        cur = 1 - cur
    final_phase(cur)
    return finish(nc, P, es, outstores)


def finish(nc, P, es, outstores):
    P.flush()
    es.close()
    return nc


def _vecs(inp):
    v = np.zeros((L, 128, NV), np.float32)
    def pc(a, n):
        return np.ascontiguousarray(a.reshape(n, 128).T)
    for l in range(L):
        v[l, :, V_NMIX:V_NMIX + 16] = pc(inp["norm_mix"][l], 16)
        v[l, :, V_NFFN:V_NFFN + 16] = pc(inp["norm_ffn"][l], 16)
        v[l, :, V_NPLE:V_NPLE + 16] = pc(inp["norm_ple"][l], 16)
        v[l, :, V_BGATE:V_BGATE + 48] = pc(inp["b_gate"][l], 48)
        for j in range(4):
            v[l, :, V_WCA + j * 8:V_WCA + j * 8 + 8] = pc(inp["w_conv_a"][l, j], 8)
            v[l, :, V_WCB + j * 24:V_WCB + j * 24 + 24] = pc(inp["w_conv_b"][l, j], 24)
        v[l, :, V_BCA:V_BCA + 8] = pc(inp["b_conv_a"][l], 8)
        v[l, :, V_BLR:V_BLR + 8] = pc(inp["b_lru_r"][l], 8)
        v[l, :, V_BLI:V_BLI + 8] = pc(inp["b_lru_i"][l], 8)
        v[l, :, V_LAM:V_LAM + 8] = pc(inp["lru_lambda"][l], 8)
        v[l, :, V_QN] = inp["attn_q_norm"][l]
        v[l, :, V_KN] = inp["attn_k_norm"][l]
        v[l, :, V_ALOG:V_ALOG + 8] = inp["dn_a_log"][l][None, :]
        v[l, :, V_DTB:V_DTB + 8] = inp["dn_dt_bias"][l][None, :]
        v[l, :, V_DNN:V_DNN + 128] = inp["dn_norm"][l][None, :]
    return v


def _consts():
    c = np.zeros((128, NCC), np.float32)
    c[:, C_ID:C_ID + 128] = np.eye(128, dtype=np.float32)
    c[:, C_ONE:C_ONE + 128] = 1.0
    i = np.arange(64)
    for half in (0, 64):
        c[half:half + 64, C_UPI:C_UPI + 64] = (i[:, None] <= i[None, :])
        c[half:half + 64, C_LOS:C_LOS + 64] = (i[:, None] > i[None, :])
        c[half:half + 64, C_LOI:C_LOI + 64] = (i[:, None] >= i[None, :])
        c[half:half + 64, C_UPS:C_UPS + 64] = (i[:, None] < i[None, :])
    return c


def _bias_tables(rel_bias):
    NEG = np.float32(-30000.0)
    ext = np.concatenate([rel_bias, np.full((L, 8, 1), NEG, np.float32)], axis=2)
    kk = np.arange(128)[:, None, None]
    ti = np.arange(5)[None, :, None]
    qq = np.arange(128)[None, None, :]
    kpos = (ti - 4) * 128 + kk
    rel = qq - kpos
    idx = np.clip(rel, -128, 128) + 128
    qc = qq // 64
    kc = (kpos + 512) // 64 - 8
    vis = (kc <= qc) & (kc >= qc - 8)
    idx = np.where(vis, idx, 257)
    bp = ext[:, :, idx]
    bp = np.ascontiguousarray(np.transpose(bp, (0, 2, 1, 3, 4)))
    row = np.arange(128)[:, None, None]
    til = np.arange(5)[None, :, None]
    j = np.where(til < 4, til * 128 + row, 512 + row % 32)
    valid = (til < 4) | (row < 64)
    q = np.arange(32)[None, None, :]
    rel = 512 + q - j
    idx = np.clip(rel, -128, 128) + 128
    idx = np.where(valid & (q >= 0), idx, 257)
    bs = ext[:, :, idx]
    bs = np.ascontiguousarray(np.transpose(bs, (0, 2, 1, 3, 4)))
    return bp, bs


def _lru_bd(inp):
    bd = np.zeros((L, 2, 8, 128, 128), np.float32)
    for l in range(L):
        for g, nm in enumerate(("w_lru_r", "w_lru_i")):
            w = inp[nm][l]
            for c in range(8):
                bd[l, g, c, 0:64, 0:64] = w[2 * c]
                bd[l, g, c, 64:128, 64:128] = w[2 * c + 1]
    return bd


def make_in_maps(inp, cores=range(8)):
    f = lambda a: np.ascontiguousarray(a, dtype=np.float32)
    shared = {
        "w_in": f(inp["w_in"]), "w_gate": f(inp["w_gate"]),
        "w_bo": f(inp["w_branch_out"]).reshape(L, 3072, D), "w_out": f(inp["w_out"]),
        "w_fg": f(inp["w_ffn_gate"]), "w_fu": f(inp["w_ffn_up"]), "w_fd": f(inp["w_ffn_down"]),
        "w_pg": f(inp["w_ple_gate"]), "w_pp": f(inp["w_ple_proj"]),
        "lru_bd": _lru_bd(inp), "vecs": _vecs(inp), "consts": _consts(),
    }
    bp, bs = _bias_tables(f(inp["attn_rel_bias"]))
    shared["attn_bias_p"] = bp
    shared["attn_bias_s"] = bs
    maps = []
    for c in cores:
        s = slice(2 * c, 2 * c + 2)
        m = dict(shared)
        m["x_tok"] = np.concatenate([inp["x_prompt"][c], inp["x_sample"][s].reshape(64, D)], 0).astype(np.float32)
        m["pe_tok"] = np.concatenate([inp["p_prompt"][:, c], inp["p_sample"][:, s].reshape(L, 64, PLE)], 1).astype(np.float32)
        m["cache_k"] = f(inp["cache_attn_k"][:, s]).reshape(L, 2, 512, 1024)
        m["cache_v"] = f(inp["cache_attn_v"][:, s]).reshape(L, 2, 512, 1024)
        m["st_conv_a"] = f(inp["state_conv_a"][:, s])
        m["st_lru_h"] = f(inp["state_lru_h"][:, s])
        m["st_conv_b"] = f(inp["state_conv_b"][:, s])
        m["st_S"] = f(inp["state_delta_S"][:, s])
        maps.append(m)
    return maps


def kernel(**inputs):
    inp = {k: np.asarray(v) for k, v in inputs.items()}
    nc = build()
    maps = make_in_maps(inp, cores=range(8))
    res = run_bass_kernel_spmd(nc, maps, core_ids=list(range(8)))
    rs_ = [{k: np.asarray(v) for k, v in r.items()} for r in res.results]
    f = np.float32
    y_p = np.stack([r["y_p"] for r in rs_]).astype(f)
    y_s = np.concatenate([r["y_s"].reshape(2, TS, D) for r in rs_], 0).astype(f)

    def pst(name, shp_tail):
        p = np.stack([r[name][:, 0] for r in rs_], 1).astype(f)
        s = np.concatenate([r[name][:, 1:3] for r in rs_], 1).astype(f)
        return p.reshape((L, 8) + shp_tail), s.reshape((L, 16) + shp_tail)
    p_ca, s_ca = pst("o_conv_a", (3, 1024))
    p_h, s_h = pst("o_lru_h", (1024,))
    p_cb, s_cb = pst("o_conv_b", (3, 3072))
    p_S, s_S = pst("o_S", (8, 128, 128))
    p_k = np.stack([r["o_pk"] for r in rs_], 1).reshape(L, 8, 512, 8, 128).astype(f)
    p_v = np.stack([r["o_pv"] for r in rs_], 1).reshape(L, 8, 512, 8, 128).astype(f)
    s_k = np.concatenate([r["o_sk"] for r in rs_], 1).reshape(L, 16, TS, 8, 128).astype(f)
    s_v = np.concatenate([r["o_sv"] for r in rs_], 1).reshape(L, 16, TS, 8, 128).astype(f)
    return (y_p, y_s, p_ca, p_h, p_cb, p_S, p_k, p_v, s_ca, s_h, s_cb, s_S, s_k, s_v)
```
